# Optimizing a Trainium2 kernel written in Bass

```python
import math
import jax, jax.numpy as jnp
from jax import lax
import numpy as np

D_MODEL = 1024
BATCH = 4
SEQ = 4096
DEPTH = 2

CTX_LEN = 256
GRID_W = 64

S5_WIDTH = 512
S5_GROUP = 16
S5_GROUPS = S5_WIDTH // S5_GROUP
S5_STATE = 64
S5_DT_MIN = 1e-3
S5_DT_MAX = 1e-1
SC_WIDTH = 512
HY_WIDTH = 512
HY_BANDS = 16
HY_EMB = 1 + 2 * HY_BANDS
HY_FILTER_HIDDEN = 64
HY_FAST_DECAY = 0.3
HY_SLOW_DECAY = 1.5
HY_DECAY_TARGET = 1e-2
HY_DECAY_SHIFT = 0.05
HY_MAX_DECAY = math.log(HY_DECAY_TARGET) / HY_FAST_DECAY
HY_MIN_DECAY = math.log(HY_DECAY_TARGET) / HY_SLOW_DECAY

N_BRANCH = 3
D_FF = 4 * D_MODEL
N_MOD = 6
EPS = 1e-6

OFF_S5 = 0
OFF_SC = OFF_S5 + S5_WIDTH
OFF_HY = OFF_SC + 3 * SC_WIDTH
OFF_GATE = OFF_HY + 3 * HY_WIDTH
D_IN = OFF_GATE + N_BRANCH * D_MODEL

kernel_name = "hybrid_s5_shortconv_hyena_prefix_dit"

F32 = jnp.float32


def rmsnorm(x, g):
    xf = x.astype(F32)
    y = xf * lax.rsqrt(jnp.mean(xf * xf, axis=-1, keepdims=True) + EPS)
    return (y * g.astype(F32)).astype(x.dtype)


def modulate(h, shift, scale):
    return h * (1.0 + scale) + shift


def sincos_2d(rows, cols, dim):
    quarter = dim // 4
    omega = 1.0 / (10000.0 ** (jnp.arange(quarter, dtype=F32) / quarter))
    er = jnp.arange(rows, dtype=F32)[:, None] * omega[None]
    ec = jnp.arange(cols, dtype=F32)[:, None] * omega[None]
    er = jnp.concatenate([jnp.sin(er), jnp.cos(er)], axis=-1)
    ec = jnp.concatenate([jnp.sin(ec), jnp.cos(ec)], axis=-1)
    emb = jnp.concatenate([
        jnp.broadcast_to(er[:, None, :], (rows, cols, dim // 2)),
        jnp.broadcast_to(ec[None, :, :], (rows, cols, dim // 2))], axis=-1)
    return emb.reshape(rows * cols, dim)


def conv3(x, w, b=None):
    xp = jnp.pad(x, ((0, 0), (1, 1), (0, 0)))
    y = w[0] * xp[:, :-2] + w[1] * xp[:, 1:-1] + w[2] * xp[:, 2:]
    if b is not None:
        y = y + b
    return y


def s5_discretise(a_re, a_im, log_dt, b_re, b_im):
    a_re = a_re.astype(F32); a_im = a_im.astype(F32)
    dt = jnp.exp(log_dt.astype(F32))[:, None]
    mag = jnp.exp(a_re * dt)
    ang = a_im * dt
    abar_re = mag * jnp.cos(ang)
    abar_im = mag * jnp.sin(ang)
    den = a_re * a_re + a_im * a_im
    nr = abar_re - 1.0
    ni = abar_im
    f_re = (nr * a_re + ni * a_im) / den
    f_im = (ni * a_re - nr * a_im) / den
    b_re = b_re.astype(F32); b_im = b_im.astype(F32)
    bb_re = f_re[..., None] * b_re - f_im[..., None] * b_im
    bb_im = f_re[..., None] * b_im + f_im[..., None] * b_re
    return abar_re, abar_im, bb_re, bb_im


def _cscan_combine(e1, e2):
    a1r, a1i, b1r, b1i = e1
    a2r, a2i, b2r, b2i = e2
    return (a2r * a1r - a2i * a1i,
            a2r * a1i + a2i * a1r,
            a2r * b1r - a2i * b1i + b2r,
            a2r * b1i + a2i * b1r + b2i)


def s5_mixer(u, lp, init, readout):
    bsz, L, _ = u.shape
    uf = u.astype(F32)
    ug = uf.reshape(bsz, L, S5_GROUPS, S5_GROUP)
    states = []
    finals = []
    for k, rev in enumerate((False, True)):
        abr, abi, bbr, bbi = s5_discretise(lp["s5_a_re"][k], lp["s5_a_im"][k], lp["s5_log_dt"][k],
                                           lp["s5_b_re"][k], lp["s5_b_im"][k])
        bur = jnp.einsum("blgp,gnp->blgn", ug, bbr)
        bui = jnp.einsum("blgp,gnp->blgn", ug, bbi)
        if init is not None:
            s0r, s0i = init[k]
            edge = -1 if rev else 0
            bur = bur.at[:, edge].add(abr * s0r - abi * s0i)
            bui = bui.at[:, edge].add(abr * s0i + abi * s0r)
        ar = jnp.broadcast_to(abr, bur.shape)
        ai = jnp.broadcast_to(abi, bui.shape)
        _, _, sr, si = lax.associative_scan(_cscan_combine, (ar, ai, bur, bui), reverse=rev, axis=1)
        last = 0 if rev else -1
        finals.append((sr[:, last], si[:, last]))
        states.append((k, sr, si))
    if not readout:
        return None, finals
    y = uf * lp["s5_d"].astype(F32)
    for k, sr, si in states:
        cr = lp["s5_c_re"][k].astype(F32)
        ci = lp["s5_c_im"][k].astype(F32)
        yk = jnp.einsum("blgn,gpn->blgp", sr, cr) - jnp.einsum("blgn,gpn->blgp", si, ci)
        y = y + yk.reshape(bsz, L, S5_WIDTH)
    h = jax.nn.gelu(y).astype(u.dtype)
    a, g = jnp.split(h @ lp["s5_glu_w"], 2, axis=-1)
    return a * jax.nn.sigmoid(g), finals


def short_conv_branch(p, lp):
    sc_x, sc_b, sc_c = jnp.split(p[..., OFF_SC:OFF_HY], 3, axis=-1)
    return (sc_b * conv3(sc_c * sc_x, lp["sc_conv_w"])) @ lp["sc_out_w"]


def hyena_filter(L, lp):
    t = jnp.linspace(0.0, 1.0, L, dtype=F32)[:, None]
    ang = 2.0 * math.pi * jnp.arange(L, dtype=F32)[:, None] / L
    bands = jnp.linspace(1e-4, HY_BANDS - 1, HY_BANDS, dtype=F32)[None, :]
    z = jnp.concatenate([t, jnp.cos(bands * ang), -jnp.sin(bands * ang)], axis=-1)
    freq = lp["hy_f_freq"].astype(F32)
    h = jnp.sin(freq * (z @ lp["hy_f_w1"].astype(F32) + lp["hy_f_b1"].astype(F32)))
    h = jnp.sin(freq * (h @ lp["hy_f_w2"].astype(F32) + lp["hy_f_b2"].astype(F32)))
    h = jnp.sin(freq * (h @ lp["hy_f_w3"].astype(F32) + lp["hy_f_b3"].astype(F32)))
    h = h @ lp["hy_f_w4"].astype(F32)
    deltas = jnp.abs(jnp.linspace(HY_MIN_DECAY, HY_MAX_DECAY, HY_WIDTH, dtype=F32))
    window = jnp.exp(-t * deltas[None, :]) + HY_DECAY_SHIFT
    h_f = h[:, :HY_WIDTH] * window
    h_b = h[:, HY_WIDTH:] * window
    k = jnp.concatenate([h_f, jnp.zeros((1, HY_WIDTH), F32), h_b[:0:-1]], axis=0)
    return k / jnp.sum(jnp.abs(k), axis=0, keepdims=True)


def fft_long_conv(u, k, d):
    L = u.shape[1]
    uf = u.astype(F32)
    U = jnp.fft.rfft(uf, n=2 * L, axis=1)
    K = jnp.fft.rfft(k, n=2 * L, axis=0)
    y = jnp.fft.irfft(U * K[None], n=2 * L, axis=1)[:, :L]
    return y + uf * d.astype(F32)


def hyena_branch(p, lp):
    L = p.shape[1]
    q = conv3(p[..., OFF_HY:OFF_GATE], lp["hy_conv_w"], lp["hy_conv_b"])
    v, x1, x0 = jnp.split(q, 3, axis=-1)
    z = fft_long_conv(x1 * v, hyena_filter(L, lp), lp["hy_d"]).astype(p.dtype)
    return (x0 * z) @ lp["hy_out_w"]


def merge_branches(p, y_s5, lp):
    g = jax.nn.sigmoid(p[..., OFF_GATE:]).reshape(p.shape[:-1] + (N_BRANCH, D_MODEL))
    m = (g[..., 0, :] * y_s5
         + g[..., 1, :] * short_conv_branch(p, lp)
         + g[..., 2, :] * hyena_branch(p, lp))
    return m @ lp["out_w"]


def channel_mixer(h, lp):
    r = jax.nn.relu(h @ lp["mlp_w1"])
    return (r * r) @ lp["mlp_w2"]


def setup_inputs(seed: int = 0) -> dict:
    key = jax.random.key(seed)
    ks = iter(jax.random.split(key, 48))

    def nrm(shape, scale):
        return jax.random.normal(next(ks), shape, F32) * scale

    G, N, P, H = S5_GROUPS, S5_STATE, S5_GROUP, HY_FILTER_HIDDEN
    inp = {}
    inp["x"] = nrm((BATCH, SEQ, D_MODEL), 1.0)
    inp["c"] = nrm((BATCH, D_MODEL), 1.0)
    inp["ctx"] = nrm((BATCH, CTX_LEN, D_MODEL), 1.0)
    inp["c_ctx"] = nrm((D_MODEL,), 1.0)
    inp["ada_w"] = nrm((DEPTH, D_MODEL, N_MOD * D_MODEL), 0.5 * D_MODEL ** -0.5)
    inp["ada_b"] = nrm((DEPTH, N_MOD * D_MODEL), 0.02)
    inp["norm1_g"] = 1.0 + nrm((DEPTH, D_MODEL), 0.05)
    inp["norm2_g"] = 1.0 + nrm((DEPTH, D_MODEL), 0.05)
    inp["w_in"] = nrm((DEPTH, D_MODEL, D_IN), D_MODEL ** -0.5)
    inp["s5_a_re"] = -0.5 + nrm((DEPTH, 2, G, N), 0.01)
    inp["s5_a_im"] = math.pi * jnp.arange(N, dtype=F32) + nrm((DEPTH, 2, G, N), 0.01)
    inp["s5_log_dt"] = jax.random.uniform(next(ks), (DEPTH, 2, G), F32,
                                          math.log(S5_DT_MIN), math.log(S5_DT_MAX))
    inp["s5_b_re"] = nrm((DEPTH, 2, G, N, P), (2 * P) ** -0.5)
    inp["s5_b_im"] = nrm((DEPTH, 2, G, N, P), (2 * P) ** -0.5)
    inp["s5_c_re"] = nrm((DEPTH, 2, G, P, N), (2 * N) ** -0.5)
    inp["s5_c_im"] = nrm((DEPTH, 2, G, P, N), (2 * N) ** -0.5)
    inp["s5_d"] = nrm((DEPTH, S5_WIDTH), 1.0)
    inp["s5_glu_w"] = nrm((DEPTH, S5_WIDTH, 2 * D_MODEL), S5_WIDTH ** -0.5)
    inp["sc_conv_w"] = nrm((DEPTH, 3, SC_WIDTH), 3 ** -0.5)
    inp["sc_out_w"] = nrm((DEPTH, SC_WIDTH, D_MODEL), SC_WIDTH ** -0.5)
    inp["hy_conv_w"] = nrm((DEPTH, 3, 3 * HY_WIDTH), 3 ** -0.5)
    inp["hy_conv_b"] = nrm((DEPTH, 3 * HY_WIDTH), 0.02)
    inp["hy_f_w1"] = nrm((DEPTH, HY_EMB, H), HY_EMB ** -0.5)
    inp["hy_f_b1"] = nrm((DEPTH, H), 0.1)
    inp["hy_f_w2"] = nrm((DEPTH, H, H), H ** -0.5)
    inp["hy_f_b2"] = nrm((DEPTH, H), 0.1)
    inp["hy_f_w3"] = nrm((DEPTH, H, H), H ** -0.5)
    inp["hy_f_b3"] = nrm((DEPTH, H), 0.1)
    inp["hy_f_freq"] = 1.0 + nrm((DEPTH, H), 0.05)
    inp["hy_f_w4"] = nrm((DEPTH, H, 2 * HY_WIDTH), H ** -0.5)
    inp["hy_d"] = nrm((DEPTH, HY_WIDTH), 0.1)
    inp["hy_out_w"] = nrm((DEPTH, HY_WIDTH, D_MODEL), HY_WIDTH ** -0.5)
    inp["out_w"] = nrm((DEPTH, D_MODEL, D_MODEL), D_MODEL ** -0.5)
    inp["mlp_w1"] = nrm((DEPTH, D_MODEL, D_FF), D_MODEL ** -0.5)
    inp["mlp_w2"] = nrm((DEPTH, D_FF, D_MODEL), D_FF ** -0.5)
    inp["final_g"] = 1.0 + nrm((D_MODEL,), 0.05)
    return inp


def reference(x, c, ctx, c_ctx, ada_w, ada_b, norm1_g, norm2_g, w_in,
              s5_a_re, s5_a_im, s5_log_dt, s5_b_re, s5_b_im, s5_c_re, s5_c_im, s5_d, s5_glu_w,
              sc_conv_w, sc_out_w, hy_conv_w, hy_conv_b,
              hy_f_w1, hy_f_b1, hy_f_w2, hy_f_b2, hy_f_w3, hy_f_b3, hy_f_freq, hy_f_w4,
              hy_d, hy_out_w, out_w, mlp_w1, mlp_w2, final_g):
    L = x.shape[1]
    rows = L // GRID_W
    x = x + sincos_2d(rows, GRID_W, D_MODEL).astype(x.dtype)[None]
    xc = ctx
    sc_vec = jax.nn.silu(c)
    sc_ctx_vec = jax.nn.silu(c_ctx)
    for l in range(DEPTH):
        last = l == DEPTH - 1
        lp = {
            "s5_a_re": s5_a_re[l], "s5_a_im": s5_a_im[l], "s5_log_dt": s5_log_dt[l],
            "s5_b_re": s5_b_re[l], "s5_b_im": s5_b_im[l], "s5_c_re": s5_c_re[l],
            "s5_c_im": s5_c_im[l], "s5_d": s5_d[l], "s5_glu_w": s5_glu_w[l],
            "sc_conv_w": sc_conv_w[l], "sc_out_w": sc_out_w[l],
            "hy_conv_w": hy_conv_w[l], "hy_conv_b": hy_conv_b[l],
            "hy_f_w1": hy_f_w1[l], "hy_f_b1": hy_f_b1[l], "hy_f_w2": hy_f_w2[l],
            "hy_f_b2": hy_f_b2[l], "hy_f_w3": hy_f_w3[l], "hy_f_b3": hy_f_b3[l],
            "hy_f_freq": hy_f_freq[l], "hy_f_w4": hy_f_w4[l], "hy_d": hy_d[l],
            "hy_out_w": hy_out_w[l], "out_w": out_w[l],
            "mlp_w1": mlp_w1[l], "mlp_w2": mlp_w2[l],
        }
        mod = (sc_vec @ ada_w[l] + ada_b[l])[:, None, :]
        mod_c = sc_ctx_vec @ ada_w[l] + ada_b[l]
        sh1, scl1, g1, sh2, scl2, g2 = jnp.split(mod, N_MOD, axis=-1)
        csh1, cscl1, cg1, csh2, cscl2, cg2 = jnp.split(mod_c, N_MOD, axis=-1)
        w_in_l = w_in[l]

        hc = modulate(rmsnorm(xc, norm1_g[l]), csh1, cscl1)
        pc = hc @ (w_in_l[:, OFF_S5:OFF_S5 + S5_WIDTH] if last else w_in_l)
        yc_s5, ctx_final = s5_mixer(pc[..., OFF_S5:OFF_S5 + S5_WIDTH], lp, None, not last)

        h = modulate(rmsnorm(x, norm1_g[l]), sh1, scl1)
        p = h @ w_in_l
        y_s5, _ = s5_mixer(p[..., OFF_S5:OFF_S5 + S5_WIDTH], lp, ctx_final, True)
        x = x + g1 * merge_branches(p, y_s5, lp)
        x = x + g2 * channel_mixer(modulate(rmsnorm(x, norm2_g[l]), sh2, scl2), lp)

        if not last:
            xc = xc + cg1 * merge_branches(pc, yc_s5, lp)
            xc = xc + cg2 * channel_mixer(modulate(rmsnorm(xc, norm2_g[l]), csh2, cscl2), lp)
    return rmsnorm(x, final_g)
```

```python
import math
import numpy as np
import ml_dtypes
import concourse.bass as bass
import concourse.mybir as mybir
from concourse.bass_utils import run_bass_kernel_spmd
from contextlib import ExitStack

F32 = mybir.dt.float32
BF16 = mybir.dt.bfloat16
I32 = mybir.dt.int32
ALU = mybir.AluOpType
AF = mybir.ActivationFunctionType

D_MODEL = 1024; SEQ = 4096; CTX = 256; DEPTH = 2; TOK = SEQ + CTX; NT = TOK // 128
D_IN = 6656; OFF_SC = 512; OFF_HY = 2048; OFF_GATE = 3584; D_FF = 4096
EPS = 1e-6
TWO_PI = 2.0 * math.pi
ENGS = ["sync", "scalar", "vector", "gpsimd", "tensor"]
SB_BASE = 16512
SB_END = 229376


class Buf:
    __slots__ = ("name", "last_w", "readers")

    def __init__(self, name):
        self.name = name
        self.last_w = None
        self.readers = []


class Op:
    __slots__ = ("eng", "fn", "dma", "pos", "deps", "signal", "sig_idx", "sem_key", "dma_val", "scope")

    def __init__(self, eng, fn, dma):
        self.eng = eng; self.fn = fn; self.dma = dma
        self.deps = []; self.signal = False; self.sig_idx = 0; self.sem_key = None; self.dma_val = 0


class Sched:
    def __init__(self, nc, same_engine_sync=True, n_dma_sems=72):
        self.nc = nc
        self.ops = {e: [] for e in ENGS}
        self.same_engine_sync = same_engine_sync
        self.dma_cnt = {}
        self.dma_since_barrier = []
        self.key_map = {}
        self.n_dma_sems = n_dma_sems
        import os
        self.use_scopes = bool(os.environ.get("KSCOPES"))

    def op(self, eng, fn, reads=(), writes=(), dma=False, sem_key=None):
        o = Op(eng, fn, dma)
        o.scope = getattr(self, "cur_scope", None)
        o.pos = len(self.ops[eng])
        deps = []
        for b in reads:
            if b.last_w is not None:
                deps.append(b.last_w)
        for b in writes:
            if b.last_w is not None:
                deps.append(b.last_w)
            deps.extend(b.readers)
        seen = set()
        for d in deps:
            if id(d) in seen or d is o:
                continue
            seen.add(id(d))
            if not d.dma and d.eng == eng:
                if eng == "tensor" or not self.same_engine_sync:
                    continue
            o.deps.append(d)
        for b in reads:
            if not dma:
                b.readers = [r for r in b.readers if r.dma or r.eng != eng]
            b.readers.append(o)
        for b in writes:
            b.last_w = o
            b.readers = []
        if dma:
            key = sem_key or (writes[0].name if writes else "dma_misc")
            if key not in self.key_map:
                self.key_map[key] = "q%d" % len(self.key_map)
                assert len(self.key_map) <= self.n_dma_sems, ("too many DMA streams in one phase", len(self.key_map))
            key = self.key_map[key]
            o.sem_key = key
            self.dma_cnt[key] = self.dma_cnt.get(key, 0) + 16
            o.dma_val = self.dma_cnt[key]
            self.dma_since_barrier.append(o)
        self.ops[eng].append(o)
        return o

    def barrier(self):
        lasts = [self.ops[e][-1] for e in ENGS if self.ops[e]]
        dmas = list(self.dma_since_barrier)
        self.dma_since_barrier = []
        self.key_map = {}
        for e in ENGS:
            o = Op(e, None, False)
            o.pos = len(self.ops[e])
            o.deps = [d for d in lasts if d.eng != e and not d.dma and d.fn is not None] + dmas
            self.ops[e].append(o)

    def emit(self, es):
        nc = self.nc
        for e in ENGS:
            for o in self.ops[e]:
                for d in o.deps:
                    if not d.dma:
                        d.signal = True
        eng_sem = {}
        for e in ENGS:
            n = 0
            for o in self.ops[e]:
                if o.signal and not o.dma:
                    n += 1
                    o.sig_idx = n
            if n:
                eng_sem[e] = es.enter_context(nc.semaphore("s_" + e))
            self.stats = getattr(self, "stats", {})
            self.stats[e] = (len(self.ops[e]), n)
        dma_sem = {}
        for key in self.dma_cnt:
            dma_sem[key] = es.enter_context(nc.semaphore("d_" + key))
        self.stats["dma"] = dict(self.dma_cnt)
        block = es.enter_context(nc.Block())

        def run(e, eng):
            waited = {}
            for o in self.ops[e]:
                need = {}
                for d in o.deps:
                    if d.dma:
                        k, v, sem = ("d", d.sem_key), d.dma_val, dma_sem[d.sem_key]
                    else:
                        k, v, sem = ("e", d.eng), d.sig_idx, eng_sem[d.eng]
                    if k not in need or need[k][0] < v:
                        need[k] = (v, sem)
                for k, (v, sem) in need.items():
                    if waited.get(k, 0) >= v:
                        continue
                    waited[k] = v
                    eng.wait_ge(sem, v)
                if o.fn is None:
                    continue
                if self.use_scopes and o.scope:
                    with nc.named_scope(o.scope):
                        ins = o.fn(eng)
                else:
                    ins = o.fn(eng)
                if o.dma:
                    ins.then_inc(dma_sem[o.sem_key], 16)
                elif o.signal:
                    ins.then_inc(eng_sem[e], 1)

        @block.sync
        def _(eng):
            run("sync", eng)

        @block.scalar
        def _(eng):
            run("scalar", eng)

        @block.vector
        def _(eng):
            run("vector", eng)

        @block.gpsimd
        def _(eng):
            run("gpsimd", eng)

        @block.tensor
        def _(eng):
            run("tensor", eng)


class T:
    def __init__(self, ap, name, nbuf=1):
        self.ap = ap
        self.b = Buf(name)

    def __getitem__(self, k):
        return self.ap[k]


class Ctx:
    def __init__(self, nc):
        self.nc = nc
        self.S = Sched(nc)
        self.uid = 0
        self.persist = SB_BASE
        self.top = SB_BASE
        self.dq = 0

    def reset_arena(self):
        self.top = self.persist

    def sb(self, name, shape, dt=F32, persist=False):
        esz = 4 if dt in (F32, I32) else 2
        nbytes = int(np.prod(shape[1:])) * esz
        nbytes = (nbytes + 63) // 64 * 64
        self.uid += 1
        off = self.top
        assert off + nbytes <= SB_END, ("SBUF overflow", name, off, nbytes)
        t = self.nc.alloc_sbuf_tensor_at("%s_%d" % (name, self.uid), list(shape), dt, offset=off)
        self.top += nbytes
        if persist:
            assert self.persist == off
            self.persist = self.top
        return T(t.ap(), "%s_%d" % (name, self.uid))

    def op(self, eng, name, reads=(), writes=(), **kw):
        rb = [x.b if isinstance(x, T) else x for x in reads]
        wb = [x.b if isinstance(x, T) else x for x in writes]
        return self.S.op(eng, lambda e: getattr(e, name)(**kw), rb, wb)

    def dma(self, out, in_, reads=(), writes=(), key=None, eng=None):
        rb = [x.b if isinstance(x, T) else x for x in reads]
        wb = [x.b if isinstance(x, T) else x for x in writes]
        if eng is None:
            eng = "sync"
        if key is None:
            dram = ("xs", "mods", "yabd")
            if wb and not wb[0].name.startswith(dram):
                key = wb[0].name
            elif rb:
                key = "st_" + rb[0].name
        return self.S.op(eng, lambda e: e.dma_start(out=out, in_=in_), rb, wb, dma=True, sem_key=key)

    def mm(self, out_t, out_ap, lhsT, rhs, reads, start, stop):
        return self.op("tensor", "matmul", reads=reads, writes=[out_t], out=out_ap, lhsT=lhsT, rhs=rhs,
                       start=start, stop=stop)


def ctile_cols(i):
    return 1 + i * 128 if i < 2 else 259 + (i - 2) * 128


PFW = 4356
GROUPS = [(0, 256)] + [(256 + 512 * i, 512) for i in range(8)]


def pf_off(tok):
    return 1 + tok if tok < 256 else 259 + (tok - 256)


def build_program(dbg=(), stop_after=None, n_layers=DEPTH):
    nc = bass.Bass("TRN2", target_bir_lowering=False)
    D = {}

    def din(name, shape, dt=F32):
        D[name] = nc.dram_tensor(name, list(shape), dt, kind="ExternalInput").ap()

    def dscr(name, shape, dt=F32):
        D[name] = nc.dram_tensor(name, list(shape), dt, kind="Internal").ap()

    def dout(name, shape, dt=F32):
        D[name] = nc.dram_tensor(name, list(shape), dt, kind="ExternalOutput").ap()

    for name, shape, dt in input_specs():
        din(name, shape, dt)
    dout("out", [SEQ, D_MODEL])
    dscr("xs", [TOK, D_MODEL])
    dscr("mods", [DEPTH, 2, 6 * D_MODEL])
    dscr("p5", [512, TOK], BF16)
    dscr("x0s", [512, TOK], BF16)
    dscr("scm", [512, TOK], BF16)
    dscr("zxs", [512, TOK], BF16)
    dscr("yss", [512, TOK], BF16)
    dscr("yab", [32, 128, 2, 512], BF16)
    for name, shape, dt in dbg_specs(dbg):
        dout(name, shape, dt)

    es = ExitStack()
    with es:
        C = Ctx(nc)
        C.D = D
        C.dbg = set(dbg)
        C.dbg_stop = stop_after
        C.psum_all = nc.alloc_psum_tensor("psall", [128, 4096], F32).ap()
        C.psum = [T(C.psum_all[:, i * 512:(i + 1) * 512], "pb%d" % i) for i in range(8)]
        C.ident_bf = C.sb("identbf", [128, 128], BF16, persist=True)
        C.ident_f = C.sb("identf", [128, 128], F32, persist=True)
        C.ones_bf = C.sb("onesbf", [128, 128], BF16, persist=True)
        C.alt_bf = C.sb("altbf", [128, 128], BF16, persist=True)
        C.altcol = C.sb("altcol", [128, 1], F32, persist=True)
        C.sel2 = C.sb("sel2", [2, 2, 128], F32, persist=True)
        C.scv = C.sb("scv", [128, 8, 2], F32, persist=True)
        C.cols = C.sb("cols", [128, 48, 2], F32, persist=True)
        C.scale1 = C.sb("scale1", [128, 8, 2], F32, persist=True)
        C.scale2 = C.sb("scale2", [128, 8, 2], F32, persist=True)
        C.ncol = C.sb("ncol", [128, DEPTH, 2, 8], F32, persist=True)
        C.dma(C.ident_bf[:], D["ident_bf"][:, :], writes=[C.ident_bf])
        C.dma(C.ident_f[:], D["ident_f"][:, :], writes=[C.ident_f])
        C.dma(C.ones_bf[:], D["ones_bf"][:, :], writes=[C.ones_bf])
        C.dma(C.alt_bf[:], D["alt_bf"][:, :], writes=[C.alt_bf])
        C.dma(C.altcol[:], D["altcol"][:, :], writes=[C.altcol])
        C.dma(C.sel2[:], D["sel2"][:, :, :], writes=[C.sel2])
        C.dma(C.ncol[:], D["ncol"][:, :, :, :], writes=[C.ncol])

        phase0(C)
        done = False
        for l in range(n_layers):
            last = l == DEPTH - 1
            for ph in (phaseP, phaseA, phaseH, phaseS, phaseC, phaseD):
                C.S.barrier()
                C.reset_arena()
                C.S.cur_scope = "L%d_%s" % (l, ph.__name__)
                ph(C, l, last)
                if stop_after == (ph.__name__, l):
                    done = True
                    break
            if done:
                break
        C.S.barrier()
        C.S.emit(es)
        import os
        if os.environ.get("KSTATS"):
            print("STATS", C.S.stats)
    return nc


def dbg_specs(dbg):
    specs = {
        "d_h": ([128, 8, TOK], BF16),
        "d_mod": ([2, 6 * D_MODEL], F32),
        "d_cols": ([128, 48, 2], F32),
        "d_utm": ([128, NT, 512], BF16),
        "d_p5": ([512, TOK], BF16),
        "d_x0": ([512, TOK], BF16),
        "d_scm": ([512, TOK], BF16),
        "d_zx": ([512, TOK], BF16),
        "d_ys": ([512, TOK], BF16),
        "d_xs": ([TOK, D_MODEL], F32),
        "d_filt": ([128, 32, 2, 512], BF16),
        "d_misc": ([128, 4, 512], F32),
        "d_s5": ([128, 6, TOK], F32),
        "d_s5p": ([128, 8, 32], F32),
    }
    return [(k, specs[k][0], specs[k][1]) for k in dbg]


def input_specs():
    L = DEPTH
    return [
        ("x_in", [SEQ, D_MODEL], F32), ("pos", [SEQ, D_MODEL], F32), ("ctx_in", [CTX, D_MODEL], F32),
        ("cc", [128, 8, 2], F32), ("ada_w", [L, D_MODEL, 6 * D_MODEL], F32), ("adab2", [L, 2, 6 * D_MODEL], F32),
        ("ncol", [128, L, 2, 8], F32), ("finalg_bc", [128, D_MODEL], F32),
        ("w_in", [L, D_MODEL, D_IN], F32), ("glu_w", [L, 512, 2048], F32), ("sc_out_w", [L, 512, 1024], F32),
        ("hy_out_w", [L, 512, 1024], F32), ("out_w", [L, 1024, 1024], F32),
        ("mlp_w1", [L, 1024, D_FF], F32), ("mlp_w2", [L, D_FF, 1024], F32),
        ("scw", [128, L, 4, 3], F32), ("hcw", [128, L, 12, 3], F32), ("hcb", [128, L, 12], F32),
        ("s5a", [128, L, 3, 32], F32),
        ("s5b", [128, L, 2, 32, 16], F32),
        ("s5c", [128, L, 2, 32, 16], F32),
        ("s5d", [128, L, 4], F32),
        ("hyw1", [L, 33, 64], F32), ("hyw2", [L, 64, 64], F32), ("hyw3", [L, 64, 64], F32),
        ("hyw4", [L, 64, 1024], F32), ("hyfb", [64, L, 4], F32),
        ("hyd_bc", [128, L, 512], F32),
        ("ident_bf", [128, 128], BF16), ("ident_f", [128, 128], F32), ("ones_bf", [128, 128], BF16),
        ("alt_bf", [128, 128], BF16), ("altcol", [128, 1], F32), ("sel2", [2, 2, 128], F32),
        ("dftc", [32, 128, 32, 128], BF16), ("dfts", [32, 128, 32, 128], BF16),
        ("dftc256", [2, 128, 2, 128], BF16), ("dfts256", [2, 128, 2, 128], BF16),
        ("zT", [33, SEQ], F32), ("zT256", [33, CTX], F32),
        ("win", [SEQ, 512], F32), ("win256", [CTX, 512], F32),
        ("wf", [128, 32], F32), ("wf256", [128, 2], F32),
        ("kpos", [128, TOK], F32),
        ("sel8", [128, 8, 8, 128], BF16), ("selT8", [128, 8, 8, 128], BF16), ("cmask", [128, 2, 128], F32),
    ]


def phase0(C):
    D = C.D
    C.bxs = [Buf("xs%d" % i) for i in range(NT)]
    C.bmods = Buf("mods")
    cc = C.sb("cc", [128, 8, 2])
    C.dma(cc[:], D["cc"][:, :, :], writes=[cc])
    C.op("scalar", "activation", reads=[cc], writes=[C.scv], out=C.scv[:], in_=cc[:], func=AF.Silu)
    for i in range(2):
        C.dma(D["xs"][i * 128:(i + 1) * 128, :], D["ctx_in"][i * 128:(i + 1) * 128, :], writes=[C.bxs[i]], key="xsst%d" % (i % 2), eng="scalar")
    xt = [C.sb("p0x%d" % i, [128, 1024]) for i in range(2)]
    pt = [C.sb("p0p%d" % i, [128, 1024]) for i in range(2)]
    for i in range(2, NT):
        a, b = xt[i % 2], pt[i % 2]
        r0 = (i - 2) * 128
        C.dma(a[:], D["x_in"][r0:r0 + 128, :], writes=[a])
        C.dma(b[:], D["pos"][r0:r0 + 128, :], writes=[b])
        C.op("vector", "tensor_tensor", reads=[a, b], writes=[a], out=a[:], in0=a[:], in1=b[:], op=ALU.add)
        C.dma(D["xs"][i * 128:(i + 1) * 128, :], a[:], reads=[a], writes=[C.bxs[i]], eng="scalar")


def phaseP(C, l, last):
    D = C.D
    adab = C.sb("adab", [2, 6144])
    modrow = C.sb("modrow", [2, 6144])
    C.dma(adab[:], D["adab2"][l], writes=[adab])
    wst = [C.sb("adaw%d" % i, [128, 8, 512]) for i in range(2)]
    wsrc = D["ada_w"][l].rearrange("(k p) c -> p k c", p=128)
    for j in range(12):
        w = wst[j % 2]
        C.dma(w[:], wsrc[:, :, j * 512:(j + 1) * 512], writes=[w])
        ps = C.psum[j % 2]
        for k in range(8):
            C.mm(ps, ps[0:2, :], lhsT=C.scv[:, k, :], rhs=w[:, k, :], reads=[C.scv, w], start=(k == 0), stop=(k == 7))
        C.op("vector", "tensor_tensor", reads=[ps, adab], writes=[modrow], out=modrow[:, j * 512:(j + 1) * 512],
             in0=ps[0:2, :], in1=adab[:, j * 512:(j + 1) * 512], op=ALU.add)
    C.dma(D["mods"][l], modrow[:], reads=[modrow], writes=[C.bmods])
    if "d_mod" in C.dbg and l == 0:
        C.dma(D["d_mod"][:, :], modrow[:], reads=[modrow], key="dbg")
    ps = C.psum[2]
    for c in range(48):
        C.mm(ps, ps[:, 2 * c:2 * c + 2], lhsT=modrow[:, c * 128:(c + 1) * 128], rhs=C.ident_f[0:2, 0:2],
             reads=[modrow, C.ident_f], start=True, stop=True)
    C.op("vector", "tensor_copy", reads=[ps], writes=[C.cols], out=C.cols[:].rearrange("p c r -> p (c r)"), in_=ps[:, 0:96])
    for r in range(2):
        C.op("vector", "scalar_tensor_tensor", reads=[C.cols, C.ncol], writes=[C.scale1], out=C.scale1[:, :, r],
             in0=C.cols[:, 8:16, r], scalar=1.0, in1=C.ncol[:, l, 0, :], op0=ALU.add, op1=ALU.mult)
        C.op("vector", "scalar_tensor_tensor", reads=[C.cols, C.ncol], writes=[C.scale2], out=C.scale2[:, :, r],
             in0=C.cols[:, 32:40, r], scalar=1.0, in1=C.ncol[:, l, 1, :], op0=ALU.add, op1=ALU.mult)
    if "d_cols" in C.dbg and l == 0:
        C.dma(D["d_cols"][:, :, :], C.cols[:], reads=[C.cols], key="dbg")


def make_norm_scratch(C, n=2):
    st = []
    for i in range(n):
        st.append(dict(ss=C.sb("nss%d" % i, [128, 1]), rs=C.sb("nrs%d" % i, [128, 1]), xn=C.sb("nxn%d" % i, [128, 1024], BF16)))
    return st


def norm_part1(C, which, xt, st, ps):
    C.op("scalar", "activation", reads=[xt], writes=[st["xn"], st["ss"]], out=st["xn"][:], in_=xt[:],
         func=AF.Square, accum_out=st["ss"][:])
    C.op("vector", "tensor_scalar", reads=[st["ss"]], writes=[st["rs"]], out=st["rs"][:], in0=st["ss"][:],
         scalar1=1.0 / D_MODEL, scalar2=EPS, op0=ALU.mult, op1=ALU.add)
    C.op("scalar", "activation", reads=[st["rs"]], writes=[st["rs"]], out=st["rs"][:], in_=st["rs"][:], func=AF.Sqrt)
    C.op("vector", "reciprocal", reads=[st["rs"]], writes=[st["rs"]], out=st["rs"][:], in_=st["rs"][:])
    C.op("vector", "tensor_scalar", reads=[xt, st["rs"]], writes=[st["xn"]], out=st["xn"][:], in0=xt[:],
         scalar1=st["rs"][:, 0:1], scalar2=None, op0=ALU.mult)
    pv = ps.ap.bitcast(BF16).rearrange("p (k t) -> p k t", k=8)
    for k in range(8):
        C.op("tensor", "transpose", reads=[st["xn"], C.ident_bf], writes=[ps], out=pv[:, k, :],
             in_=st["xn"][:, k * 128:(k + 1) * 128], identity=C.ident_bf[:])


def norm_part2(C, which, r, hT, hcol, ps):
    scale = C.scale1 if which == 0 else C.scale2
    shj = 0 if which == 0 else 3
    pv = ps.ap.bitcast(BF16).rearrange("p (k t) -> p k t", k=8)
    for k in range(8):
        if k % 2 == 0:
            C.op("vector", "tensor_scalar", reads=[ps, scale, C.cols], writes=[hT], out=hT[:, k, hcol:hcol + 128],
                 in0=pv[:, k, :], scalar1=scale[:, k, r:r + 1], scalar2=C.cols[:, shj * 8 + k, r:r + 1],
                 op0=ALU.mult, op1=ALU.add)
        else:
            C.op("scalar", "activation", reads=[ps, scale, C.cols], writes=[hT], out=hT[:, k, hcol:hcol + 128],
                 in_=pv[:, k, :], func=AF.Identity, bias=C.cols[:, shj * 8 + k, r:r + 1], scale=scale[:, k, r:r + 1])


def norm_tiles(C, which, items, sts, pss):
    n = len(items)
    for idx in range(n + 1):
        if idx < n:
            r, xt, hT, hcol = items[idx]
            norm_part1(C, which, xt, sts[idx % 2], pss[idx % 2])
        if idx >= 1:
            r, xt, hT, hcol = items[idx - 1]
            norm_part2(C, which, r, hT, hcol, pss[(idx - 1) % 2])


def norm_tile(C, which, r, xt, hT, hcol, st, ps):
    norm_part1(C, which, xt, st, ps)
    norm_part2(C, which, r, hT, hcol, ps)


def conv3(C, eng, out_t, out_ap, in_t, in_ap_fn, wcol, bias=None, n=PFW - 2):
    C.op("scalar", "activation", reads=[in_t], writes=[out_t], out=out_ap, in_=in_ap_fn(1), func=AF.Identity,
         bias=(bias if bias is not None else 0.0), scale=wcol[:, 1:2])
    for s in (0, 2):
        C.op(eng, "scalar_tensor_tensor", reads=[in_t, out_t], writes=[out_t], out=out_ap, in0=in_ap_fn(s),
             scalar=wcol[:, s:s + 1], in1=out_ap, op0=ALU.mult, op1=ALU.add)


def phaseA(C, l, last):
    D = C.D
    utm = C.utm = C.sb("utm", [128, NT, 512], BF16)
    hT = C.sb("hT", [128, 8, TOK], BF16)
    nsc = make_norm_scratch(C)
    xt = [C.sb("ax%d" % i, [128, 1024]) for i in range(2)]
    hbufs = [Buf("hTg%d" % g) for g in range(len(GROUPS))]
    items = []
    for i in range(NT):
        hTg = T(hT.ap, "hTg")
        hTg.b = hbufs[0 if i < 2 else 1 + (i - 2) // 4]
        items.append((1 if i < 2 else 0, xt[i % 2], hTg, i * 128))
    for idx in range(NT + 1):
        if idx < NT:
            a = xt[idx % 2]
            C.dma(a[:], D["xs"][idx * 128:(idx + 1) * 128, :], reads=[C.bxs[idx]], writes=[a])
            norm_part1(C, 0, a, nsc[idx % 2], C.psum[idx % 2])
        if idx >= 1:
            r_, a_, hTg_, hc_ = items[idx - 1]
            norm_part2(C, 0, r_, hTg_, hc_, C.psum[(idx - 1) % 2])
    if "d_h" in C.dbg and l == 0:
        C.dma(D["d_h"][:, :, :], hT[:], reads=hbufs, key="dbg")
    C.S.cur_scope = C.S.cur_scope.split("/")[0] + "/A2"
    wst = [C.sb("awst%d" % i, [128, 8, 128]) for i in range(2)]
    wb = [C.sb("awb%d" % i, [128, 8, 128], BF16) for i in range(2)]
    pfs = [C.sb("pf%d" % i, [128, PFW]) for i in range(2)]
    qv = C.sb("qv", [128, PFW])
    q = C.sb("q", [128, PFW - 2])
    o16 = C.sb("o16", [128, PFW - 2], BF16)
    cw = C.sb("cw", [128, 16, 3])
    cb = C.sb("cb", [128, 12])
    C.dma(cw[:, 0:4, :], D["scw"][:, l, :, :], writes=[cw])
    C.dma(cw[:, 4:16, :], D["hcw"][:, l, :, :], writes=[cw])
    C.dma(cb[:], D["hcb"][:, l, :], writes=[cb])
    for pf in pfs:
        C.op("gpsimd", "memset", writes=[pf], ap=pf[:], constant=0.0)
    C.op("gpsimd", "memset", writes=[qv], ap=qv[:], constant=0.0)
    tiles = [("s5", j, j * 128) for j in range(4)]
    for j in range(4):
        tiles += [("sc_c", j, OFF_SC + 1024 + j * 128), ("sc_x", j, OFF_SC + j * 128), ("sc_b", j, OFF_SC + 512 + j * 128)]
    for j in range(4):
        tiles += [("hy_v", j, OFF_HY + j * 128), ("hy_x1", j, OFF_HY + 512 + j * 128), ("hy_x0", j, OFF_HY + 1024 + j * 128)]
    wsrc = D["w_in"][l].rearrange("(k p) c -> p k c", p=128)
    nmm = 0
    pending = []

    class Rec:
        def op(self, *a, **kw):
            pending.append(lambda: C.op(*a, **kw))
        def dma(self, *a, **kw):
            pending.append(lambda: C.dma(*a, **kw))
    R = Rec()

    def rconv3(out_t, out_ap, in_t, in_ap_fn, wcol, bias=None):
        R.op("scalar", "activation", reads=[in_t, cw, cb], writes=[out_t], out=out_ap, in_=in_ap_fn(1), func=AF.Identity,
             bias=(bias if bias is not None else 0.0), scale=wcol[:, 1:2])
        for s_ in (0, 2):
            R.op("vector", "scalar_tensor_tensor", reads=[in_t, out_t, cw], writes=[out_t], out=out_ap, in0=in_ap_fn(s_),
                 scalar=wcol[:, s_:s_ + 1], in1=out_ap, op0=ALU.mult, op1=ALU.add)

    def flush(nmax):
        for _ in range(min(nmax, len(pending))):
            pending.pop(0)()

    for ti, (kind, j, col0) in enumerate(tiles):
        ws, w = wst[ti % 2], wb[ti % 2]
        C.dma(ws[:], wsrc[:, :, col0:col0 + 128], writes=[ws])
        C.op("gpsimd", "tensor_copy", reads=[ws], writes=[w], out=w[:], in_=ws[:])
        pf = pfs[ti % 2]
        dest = {"s5": o16, "sc_c": qv}.get(kind, pf)
        if kind in ("s5", "sc_c"):
            flush(len(pending))
        per = (len(pending) + len(GROUPS) - 1) // len(GROUPS)
        for gi, (g0, gn) in enumerate(GROUPS):
            ps = C.psum[2 + nmm % 4]
            nmm += 1
            for k in range(8):
                C.mm(ps, ps[:, 0:gn], lhsT=w[:, k, :], rhs=hT[:, k, g0:g0 + gn], reads=[w, hbufs[gi]], start=(k == 0), stop=(k == 7))
            if kind == "s5":
                c0, ncq = g0 // 8, gn // 8
                o_ap = dest[:, 0:TOK].rearrange("p (t c) -> p t c", t=8)[:, :, c0:c0 + ncq]
                i_ap = ps[:, 0:gn].rearrange("p (c t) -> p t c", t=8)
            else:
                dcol = pf_off(g0)
                o_ap = dest[:, dcol:dcol + gn]
                i_ap = ps[:, 0:gn]
            C.op("scalar", "copy", reads=[ps], writes=[dest], out=o_ap, in_=i_ap)
            flush(per)
        flush(len(pending))
        W = PFW - 2
        if kind == "s5":
            R.dma(D["p5"][j * 128:(j + 1) * 128, :], o16[:, 0:TOK], reads=[o16])
        elif kind == "sc_x":
            R.op("vector", "tensor_tensor", reads=[pf, qv], writes=[qv], out=qv[:], in0=pf[:], in1=qv[:], op=ALU.mult)
            rconv3(q, q[:, 0:W], qv, lambda s_: qv[:, s_:s_ + W], cw[:, j, :])
        elif kind == "sc_b":
            R.op("vector", "tensor_tensor", reads=[pf, q], writes=[o16], out=o16[:, 0:W], in0=q[:, 0:W], in1=pf[:, 1:1 + W], op=ALU.mult)
            R.dma(D["scm"][j * 128:(j + 1) * 128, 0:256], o16[:, 0:256], reads=[o16])
            R.dma(D["scm"][j * 128:(j + 1) * 128, 256:TOK], o16[:, 258:258 + SEQ], reads=[o16])
        elif kind == "hy_v":
            rconv3(qv, qv[:, 0:W], pf, lambda s_, pf=pf: pf[:, s_:s_ + W], cw[:, 4 + j, :], bias=cb[:, j:j + 1])
        elif kind == "hy_x1":
            rconv3(q, q[:, 0:W], pf, lambda s_, pf=pf: pf[:, s_:s_ + W], cw[:, 8 + j, :], bias=cb[:, 4 + j:5 + j])
            R.op("vector", "tensor_tensor", reads=[q, qv], writes=[o16], out=o16[:, 0:W], in0=q[:, 0:W], in1=qv[:, 0:W], op=ALU.mult)
            for i0_ in range(0, NT, 8):
                n = min(8, NT - i0_)
                ps = C.psum[6 + (i0_ // 8) % 2]
                pv = ps.ap.bitcast(BF16).rearrange("p (k t) -> p k t", k=8)
                for ii in range(n):
                    oc = ctile_cols(i0_ + ii) - 1
                    R.op("tensor", "transpose", reads=[o16, C.ident_bf], writes=[ps], out=pv[:, ii, :],
                         in_=o16[:, oc:oc + 128], identity=C.ident_bf[:])
                R.op("scalar", "copy", reads=[ps], writes=[utm], out=utm[:, i0_:i0_ + n, j * 128:(j + 1) * 128], in_=pv[:, 0:n, :])
        elif kind == "hy_x0":
            rconv3(q, q[:, 0:W], pf, lambda s_, pf=pf: pf[:, s_:s_ + W], cw[:, 12 + j, :], bias=cb[:, 8 + j:9 + j])
            R.op("gpsimd", "tensor_copy", reads=[q], writes=[o16], out=o16[:, 0:W], in_=q[:, 0:W])
            R.dma(D["x0s"][j * 128:(j + 1) * 128, 0:256], o16[:, 0:256], reads=[o16])
            R.dma(D["x0s"][j * 128:(j + 1) * 128, 256:TOK], o16[:, 258:258 + SEQ], reads=[o16])
    flush(len(pending))
    if l == 0:
        for nm, src in (("d_p5", "p5"), ("d_x0", "x0s"), ("d_scm", "scm")):
            if nm in C.dbg:
                C.S.barrier()
                C.dma(D[nm][:, :], D[src][:, :], key="dbg")
        if "d_utm" in C.dbg:
            C.dma(D["d_utm"][:, :, :], utm[:], reads=[utm], key="dbg")


def reduce_sin(C, eng, out_t, out_ap, in_t, in_ap, tmp_t, tmp_ap, ki_t, ki_ap, mul=None, add=None, reads=()):
    src_t, src_ap = in_t, in_ap
    if mul is not None or add is not None:
        C.op(eng, "tensor_scalar", reads=[in_t] + list(reads), writes=[tmp_t], out=tmp_ap, in0=in_ap,
             scalar1=(1.0 if mul is None else mul), scalar2=(0.0 if add is None else add), op0=ALU.mult, op1=ALU.add)
        src_t, src_ap = tmp_t, tmp_ap
    C.op(eng, "tensor_scalar", reads=[src_t], writes=[ki_t], out=ki_ap, in0=src_ap, scalar1=1.0 / TWO_PI, scalar2=None, op0=ALU.mult)
    C.op(eng, "scalar_tensor_tensor", reads=[ki_t, src_t], writes=[tmp_t], out=tmp_ap, in0=ki_ap, scalar=-TWO_PI, in1=src_ap,
         op0=ALU.mult, op1=ALU.add)
    C.op("scalar", "activation", reads=[tmp_t], writes=[out_t], out=out_ap, in_=tmp_ap, func=AF.Sin)


def hyena_seq(C, l, L, tile0, sfx, base_top):
    D = C.D
    nt = L // 128
    N2 = 2 * L
    C.top = base_top
    regy = C.top
    fs = C.sb("fs", [128, nt, 512], BF16)
    fd = C.sb("fd", [128, nt, 512], BF16)
    end_y = C.top
    C.top = regy
    yab = C.sb("hyab", [128, nt, 2, 512], BF16)
    C.top = max(C.top, end_y)
    regd = C.top
    dC = [C.sb("dC%d" % i, [128, nt, 128], BF16) for i in range(2)]
    dS = [C.sb("dS%d" % i, [128, nt, 128], BF16) for i in range(2)]
    end_d = C.top
    C.top = regd
    h3 = C.sb("hh3", [64, L])
    C.top = max(C.top, end_d)
    sm = {n: C.sb("h" + n, [128, 512]) for n in ("hf", "hb", "kc", "ks", "t1", "t2", "t3", "t4", "rn", "kl", "yl", "hyd", "w0", "w1")}
    afs = [C.sb("haf%d" % i, [128, 512], BF16) for i in range(2)]; abs_ = [C.sb("hab%d" % i, [128, 512], BF16) for i in range(2)]
    hfs = [C.sb("hhf%d" % i, [128, 512]) for i in range(2)]; hbs = [C.sb("hhb%d" % i, [128, 512]) for i in range(2)]
    yst = [C.sb("yst%d" % i, [128, 2, 512], BF16) for i in range(2)]
    zb = C.sb("hz", [128, 512], BF16)
    x0t = [C.sb("hx0%d" % i, [128, 4, 128], BF16) for i in range(2)]
    zxt = [C.sb("hzx%d" % i, [128, 4, 128], BF16) for i in range(2)]
    w1 = C.sb("hw1", [33, 64]); w2 = C.sb("hw2", [64, 64]); w3 = C.sb("hw3", [64, 64]); w4 = C.sb("hw4", [64, 1024])
    fb = C.sb("hfb", [64, 4]); fbias = C.sb("hfbias", [64, 3]); wfc = C.sb("hwf", [128, nt])
    tmpa = C.sb("htmpa", [64, 512]); kia = C.sb("hkia", [64, 512], I32)
    zc = [C.sb("hzc%d" % i, [33, 512]) for i in range(2)]
    ha = C.sb("hha", [64, 512]); hb_ = C.sb("hhb", [64, 512])
    C.dma(w1[:], D["hyw1"][l], writes=[w1]); C.dma(w2[:], D["hyw2"][l], writes=[w2]); C.dma(w3[:], D["hyw3"][l], writes=[w3])
    C.dma(w4[:], D["hyw4"][l], writes=[w4]); C.dma(fb[:], D["hyfb"][:, l, :], writes=[fb]); C.dma(wfc[:], D["wf" + sfx][:, :], writes=[wfc])
    C.dma(sm["hyd"][:], D["hyd_bc"][:, l, :], writes=[sm["hyd"]])
    for i in range(3):
        C.op("vector", "tensor_tensor", reads=[fb], writes=[fbias], out=fbias[:, i:i + 1], in0=fb[:, 0:1], in1=fb[:, i + 1:i + 2], op=ALU.mult)
    C.S.cur_scope = C.S.cur_scope.split("/")[0] + "/H1a"
    CH = min(512, L)
    nps = 0
    for ci, c0 in enumerate(range(0, L, CH)):
        z = zc[ci % 2]
        C.dma(z[:, 0:CH], D["zT" + sfx][:, c0:c0 + CH], writes=[z])
        for (w, src_t, src_ap, dst_t, dst_ap, bi) in ((w1, z, z[:, 0:CH], ha, ha[:, 0:CH], 0), (w2, ha, ha[:, 0:CH], hb_, hb_[:, 0:CH], 1),
                                                     (w3, hb_, hb_[:, 0:CH], h3, h3[:, c0:c0 + CH], 2)):
            ps = C.psum[nps % 2]; nps += 1
            C.mm(ps, ps[0:64, 0:CH], lhsT=w[:], rhs=src_ap, reads=[w, src_t], start=True, stop=True)
            reduce_sin(C, "vector", dst_t, dst_ap, ps, ps[0:64, 0:CH], tmpa, tmpa[:, 0:CH], kia, kia[:, 0:CH],
                       mul=fb[:, 0:1], add=fbias[:, bi:bi + 1], reads=[fb, fbias])
    C.S.cur_scope = C.S.cur_scope.split("/")[0] + "/H1b"
    pN, pK = C.psum[6], C.psum[7]
    for tt in range(nt):
        pa, pb = C.psum[2 + (tt % 2) * 2], C.psum[3 + (tt % 2) * 2]
        wt = sm["w%d" % (tt % 2)]
        af, ab = afs[tt % 2], abs_[tt % 2]
        sm["hf"], sm["hb"] = hfs[tt % 2], hbs[tt % 2]
        C.dma(wt[:], D["win" + sfx][tt * 128:(tt + 1) * 128, :], writes=[wt])
        C.mm(pa, pa[:, :], lhsT=h3[:, tt * 128:(tt + 1) * 128], rhs=w4[:, 0:512], reads=[h3, w4], start=True, stop=True)
        C.mm(pb, pb[:, :], lhsT=h3[:, tt * 128:(tt + 1) * 128], rhs=w4[:, 512:1024], reads=[h3, w4], start=True, stop=True)
        C.op("vector", "tensor_tensor", reads=[pa, wt], writes=[sm["hf"]], out=sm["hf"][:], in0=pa[:, :], in1=wt[:], op=ALU.mult)
        C.op("vector", "tensor_tensor", reads=[pb, wt], writes=[sm["hb"]], out=sm["hb"][:], in0=pb[:, :], in1=wt[:], op=ALU.mult)
        if tt == 0:
            C.op("vector", "memset", writes=[sm["hb"]], ap=sm["hb"][0:1, :], constant=0.0)
        C.op("gpsimd", "tensor_tensor", reads=[sm["hf"], sm["hb"]], writes=[fs], out=fs[:, tt, :], in0=sm["hf"][:], in1=sm["hb"][:], op=ALU.add)
        C.op("gpsimd", "tensor_tensor", reads=[sm["hf"], sm["hb"]], writes=[fd], out=fd[:, tt, :], in0=sm["hf"][:], in1=sm["hb"][:], op=ALU.subtract)
        C.op("scalar", "activation", reads=[sm["hf"]], writes=[af], out=af[:], in_=sm["hf"][:], func=AF.Abs)
        C.op("scalar", "activation", reads=[sm["hb"]], writes=[ab], out=ab[:], in_=sm["hb"][:], func=AF.Abs)
        C.mm(pN, pN[:, :], lhsT=C.ones_bf[:], rhs=af[:], reads=[C.ones_bf, af], start=(tt == 0), stop=False)
        C.mm(pN, pN[:, :], lhsT=C.ones_bf[:], rhs=ab[:], reads=[C.ones_bf, ab], start=False, stop=(tt == nt - 1))
        C.mm(pK, pK[:, :], lhsT=C.alt_bf[:], rhs=fs[:, tt, :], reads=[C.alt_bf, fs], start=(tt == 0), stop=(tt == nt - 1))
    C.op("vector", "reciprocal", reads=[pN], writes=[sm["rn"]], out=sm["rn"][:], in_=pN[:, :])
    C.op("scalar", "copy", reads=[pK], writes=[sm["kl"]], out=sm["kl"][:], in_=pK[:, :])
    pU = C.psum[6]
    for ch in range(nt):
        C.mm(pU, pU[:, :], lhsT=C.alt_bf[:], rhs=C.utm[:, tile0 + ch, :], reads=[C.alt_bf, C.utm], start=(ch == 0), stop=(ch == nt - 1))
    C.op("vector", "tensor_tensor", reads=[pU, sm["kl"]], writes=[sm["yl"]], out=sm["yl"][:], in0=pU[:, :], in1=sm["kl"][:], op=ALU.mult)
    C.op("vector", "scalar_tensor_tensor", reads=[sm["yl"], sm["rn"]], writes=[sm["yl"]], out=sm["yl"][:], in0=sm["yl"][:], scalar=1.0 / N2,
         in1=sm["rn"][:], op0=ALU.mult, op1=ALU.mult)
    C.S.cur_scope = C.S.cur_scope.split("/")[0] + "/Hf"
    byab = [Buf("yabd%d" % i) for i in range(nt)]
    for ft in range(nt):
        ct, st = dC[ft % 2], dS[ft % 2]
        C.dma(ct[:], D["dftc" + sfx][ft], writes=([ct, h3] if ft < 2 else [ct]))
        C.dma(st[:], D["dfts" + sfx][ft], writes=[st])
        pb4 = [C.psum[(ft % 2) * 4 + i] for i in range(4)]
        for bi, (mat, rhs_t, rhs_fn) in enumerate(((ct, C.utm, lambda ch: C.utm[:, tile0 + ch, :]), (st, C.utm, lambda ch: C.utm[:, tile0 + ch, :]),
                                                   (ct, fs, lambda ch: fs[:, ch, :]), (st, fd, lambda ch: fd[:, ch, :]))):
            for ch in range(nt):
                C.mm(pb4[bi], pb4[bi][:, :], lhsT=mat[:, ch, :], rhs=rhs_fn(ch), reads=[mat, rhs_t], start=(ch == 0), stop=(ch == nt - 1))
        pUr, pUs, pKc, pKs = pb4
        C.op("scalar", "copy", reads=[pKc], writes=[sm["kc"]], out=sm["kc"][:], in_=pKc[:, :])
        C.op("scalar", "copy", reads=[pKs], writes=[sm["ks"]], out=sm["ks"][:], in_=pKs[:, :])
        C.op("vector", "tensor_tensor", reads=[pUr, sm["kc"]], writes=[sm["t1"]], out=sm["t1"][:], in0=pUr[:, :], in1=sm["kc"][:], op=ALU.mult)
        C.op("vector", "tensor_tensor", reads=[pUs, sm["ks"]], writes=[sm["t2"]], out=sm["t2"][:], in0=pUs[:, :], in1=sm["ks"][:], op=ALU.mult)
        C.op("vector", "tensor_tensor", reads=[pUr, sm["ks"]], writes=[sm["t3"]], out=sm["t3"][:], in0=pUr[:, :], in1=sm["ks"][:], op=ALU.mult)
        C.op("vector", "tensor_tensor", reads=[pUs, sm["kc"]], writes=[sm["t4"]], out=sm["t4"][:], in0=pUs[:, :], in1=sm["kc"][:], op=ALU.mult)
        C.op("gpsimd", "tensor_tensor", reads=[sm["t1"], sm["t2"]], writes=[sm["t1"]], out=sm["t1"][:], in0=sm["t1"][:], in1=sm["t2"][:], op=ALU.subtract)
        C.op("gpsimd", "tensor_tensor", reads=[sm["t3"], sm["t4"]], writes=[sm["t3"]], out=sm["t3"][:], in0=sm["t3"][:], in1=sm["t4"][:], op=ALU.add)
        ys = yst[ft % 2]
        for ri, tn in ((0, "t1"), (1, "t3")):
            C.op("vector", "scalar_tensor_tensor", reads=[sm[tn], sm["rn"], wfc], writes=[ys], out=ys[:, ri, :], in0=sm[tn][:], scalar=wfc[:, ft:ft + 1],
                 in1=sm["rn"][:], op0=ALU.mult, op1=ALU.mult)
        C.dma(D["yab"][ft], ys[:], reads=[ys], writes=[byab[ft]], eng="scalar")
    C.S.cur_scope = C.S.cur_scope.split("/")[0] + "/H4"
    C.dma(yab[:], D["yab"][0:nt].rearrange("f p r c -> p f r c"), reads=byab, writes=[yab, fs, fd])
    x0v = D["x0s"].rearrange("(j c) t -> c j t", c=128)
    zxv = D["zxs"].rearrange("(j c) t -> c j t", c=128)
    for tt in range(nt):
        ct, st = dC[tt % 2], dS[tt % 2]
        C.dma(ct[:], D["dftc" + sfx][tt], writes=[ct])
        C.dma(st[:], D["dfts" + sfx][tt], writes=[st])
        tok0 = (tile0 + tt) * 128
        xt = x0t[tt % 2]
        C.dma(xt[:], x0v[:, :, tok0:tok0 + 128], writes=[xt])
        acc = C.psum[tt % 2]
        for ch in range(nt):
            C.mm(acc, acc[:, :], lhsT=ct[:, ch, :], rhs=yab[:, ch, 0, :], reads=[ct, yab], start=(ch == 0), stop=False)
        for ch in range(nt):
            C.mm(acc, acc[:, :], lhsT=st[:, ch, :], rhs=yab[:, ch, 1, :], reads=[st, yab], start=False, stop=(ch == nt - 1))
        C.op("vector", "scalar_tensor_tensor", reads=[sm["yl"], C.altcol, acc], writes=[sm["hf"]], out=sm["hf"][:], in0=sm["yl"][:],
             scalar=C.altcol[:, 0:1], in1=acc[:, :], op0=ALU.mult, op1=ALU.add)
        C.op("gpsimd", "tensor_tensor", reads=[C.utm, sm["hyd"]], writes=[sm["hb"]], out=sm["hb"][:], in0=C.utm[:, tile0 + tt, :], in1=sm["hyd"][:], op=ALU.mult)
        C.op("gpsimd", "tensor_tensor", reads=[sm["hf"], sm["hb"]], writes=[zb], out=zb[:], in0=sm["hf"][:], in1=sm["hb"][:], op=ALU.add)
        pT = C.psum[2 + tt % 2]
        pv = pT.ap.bitcast(BF16).rearrange("p (k t) -> p k t", k=8)
        for j in range(4):
            C.op("tensor", "transpose", reads=[zb, C.ident_bf], writes=[pT], out=pv[:, j, :], in_=zb[:, j * 128:(j + 1) * 128], identity=C.ident_bf[:])
        zx = zxt[tt % 2]
        C.op("vector", "tensor_tensor", reads=[pT, xt], writes=[zx], out=zx[:], in0=pv[:, 0:4, :], in1=xt[:], op=ALU.mult)
        C.dma(zxv[:, :, tok0:tok0 + 128], zx[:], reads=[zx], eng="scalar")
    if "d_filt" in C.dbg and l == 0 and sfx == "":
        C.dma(D["d_filt"][:, :, 0, :], fs[:], reads=[fs], key="dbg")
        C.dma(D["d_filt"][:, :, 1, :], fd[:], reads=[fd], key="dbg")
    if "d_misc" in C.dbg and l == 0 and sfx == "":
        for i, n in enumerate(("rn", "kl", "yl", "hyd")):
            C.dma(D["d_misc"][:, i, :], sm[n][:], reads=[sm[n]], key="dbg")


def phaseH(C, l, last):
    D = C.D
    C.utm = C.sb("utm", [128, NT, 512], BF16)
    base_top = C.top
    hyena_seq(C, l, SEQ, 2, "", base_top)
    if not last:
        C.S.barrier()
        hyena_seq(C, l, CTX, 0, "256", base_top)
    if "d_zx" in C.dbg and l == 0:
        C.S.barrier()
        C.dma(D["d_zx"][:, :], D["zxs"][:, :], key="dbg")


def cplx_outer(C, eng, out_t, out_r, out_i, Pr, Pi, Mr, Mi, reads, tmp, neg_i=False):
    sh = [128, 8, 16]
    pr, pi_ = Pr.unsqueeze(2).to_broadcast(sh), Pi.unsqueeze(2).to_broadcast(sh)
    mr, mi = Mr.unsqueeze(1).to_broadcast(sh), Mi.unsqueeze(1).to_broadcast(sh)
    ta, tb = tmp
    va = ta[:].rearrange("q (t p) -> q t p", t=8); vb = tb[:].rearrange("q (t p) -> q t p", t=8)
    orr = out_r.rearrange("q (t p) -> q t p", t=8); oi = out_i.rearrange("q (t p) -> q t p", t=8)
    C.op(eng, "tensor_tensor", reads=reads, writes=[ta], out=va, in0=pr, in1=mr, op=ALU.mult)
    C.op(eng, "tensor_tensor", reads=reads, writes=[tb], out=vb, in0=pi_, in1=mi, op=ALU.mult)
    C.op(eng, "tensor_tensor", reads=[ta, tb], writes=[out_t], out=orr, in0=va, in1=vb, op=ALU.subtract)
    C.op(eng, "tensor_tensor", reads=reads, writes=[ta], out=va, in0=pr, in1=mi, op=ALU.mult)
    C.op(eng, "tensor_tensor", reads=reads, writes=[tb], out=vb, in0=pi_, in1=mr, op=ALU.mult)
    if neg_i:
        C.op(eng, "tensor_scalar", reads=[ta], writes=[ta], out=va, in0=va, scalar1=-1.0, scalar2=None, op0=ALU.mult)
        C.op(eng, "tensor_tensor", reads=[ta, tb], writes=[out_t], out=oi, in0=va, in1=vb, op=ALU.subtract)
    else:
        C.op(eng, "tensor_tensor", reads=[ta, tb], writes=[out_t], out=oi, in0=va, in1=vb, op=ALU.add)


def phaseS(C, l, last):
    D = C.D
    NCH = TOK // 8
    HC = NCH // 2
    p5 = C.sb("s5p", [128, 4, TOK], BF16)
    C.dma(p5[:], D["p5"].rearrange("(j c) t -> c j t", c=128), writes=[p5])
    X = C.sb("s5X", [128, 32, NCH], BF16)
    Xb = [Buf("s5X%d" % g) for g in range(32)]
    sel = C.sb("s5sel", [128, 8, 8, 128], BF16)
    selT = C.sb("s5selT", [128, 8, 8, 128], BF16)
    C.dma(sel[:], D["sel8"][:, :, :, :], writes=[sel])
    C.dma(selT[:], D["selT8"][:, :, :, :], writes=[selT])
    masks = C.sb("s5mask", [128, 2, 128])
    C.dma(masks[:], D["cmask"][:, :, :], writes=[masks])
    kp = C.sb("s5kp", [128, NCH])
    C.dma(kp[:], D["kpos"][:, 0:NCH], writes=[kp])
    ysb = C.sb("s5ysb", [128, TOK], BF16)
    pa = C.sb("s5a_", [128, 3, 32]); pb = C.sb("s5b_", [128, 2, 32, 16]); pc = C.sb("s5c_", [128, 2, 32, 16]); dcol = C.sb("s5d_", [128, 4])
    C.dma(pa[:], D["s5a"][:, l], writes=[pa]); C.dma(pb[:], D["s5b"][:, l], writes=[pb]); C.dma(pc[:], D["s5c"][:, l], writes=[pc])
    C.dma(dcol[:], D["s5d"][:, l, :], writes=[dcol])
    sc = {n: C.sb("s5s" + n, [128, 32]) for n in ("dt", "ard", "ang", "mag", "sn", "cs", "abr", "abi", "den", "t0", "t1", "fre", "fim", "nfim", "phi", "rho8")}
    ski = C.sb("s5ski", [128, 32], I32)
    Bb = C.sb("s5bb", [128, 2, 32, 16])
    apw = C.sb("s5apw", [128, 2, 32, 9]); apr = C.sb("s5apr", [128, 2, 32, 9]); ang_ = C.sb("s5ang", [128, 2, 32, 8])
    V = lambda name, **kw: C.op("vector", name, **kw)
    ps = C.psum
    PS = C.psum_all
    C.op("scalar", "activation", reads=[pa], writes=[sc["dt"]], out=sc["dt"][:], in_=pa[:, 2, :], func=AF.Exp)
    V("tensor_tensor", reads=[pa, sc["dt"]], writes=[sc["ard"]], out=sc["ard"][:], in0=pa[:, 0, :], in1=sc["dt"][:], op=ALU.mult)
    V("tensor_tensor", reads=[pa, sc["dt"]], writes=[sc["ang"]], out=sc["ang"][:], in0=pa[:, 1, :], in1=sc["dt"][:], op=ALU.mult)
    for j in range(9):
        for (tab, jj, sgn) in ((apw, j, 1.0), (apr, 8 - j, 1.0), (ang_, j, -1.0)):
            if tab is ang_ and j == 8:
                continue
            C.op("scalar", "activation", reads=[sc["ard"]], writes=[sc["mag"]], out=sc["mag"][:], in_=sc["ard"][:], func=AF.Exp, scale=sgn * jj)
            reduce_sin(C, "vector", sc["sn"], sc["sn"][:], sc["ang"], sc["ang"][:], sc["t0"], sc["t0"][:], ski, ski[:], mul=float(jj), add=0.0)
            reduce_sin(C, "vector", sc["cs"], sc["cs"][:], sc["ang"], sc["ang"][:], sc["t0"], sc["t0"][:], ski, ski[:], mul=float(jj), add=math.pi / 2)
            V("tensor_tensor", reads=[sc["mag"], sc["cs"]], writes=[tab], out=tab[:, 0, :, j], in0=sc["mag"][:], in1=sc["cs"][:], op=ALU.mult)
            V("scalar_tensor_tensor", reads=[sc["mag"], sc["sn"]], writes=[tab], out=tab[:, 1, :, j], in0=sc["mag"][:], scalar=sgn, in1=sc["sn"][:],
              op0=ALU.mult, op1=ALU.mult)
    V("tensor_copy", reads=[apw], writes=[sc["abr"]], out=sc["abr"][:], in_=apw[:, 0, :, 1])
    V("tensor_copy", reads=[apw], writes=[sc["abi"]], out=sc["abi"][:], in_=apw[:, 1, :, 1])
    V("tensor_scalar", reads=[sc["ang"]], writes=[sc["phi"]], out=sc["phi"][:], in0=sc["ang"][:], scalar1=8.0, scalar2=None, op0=ALU.mult)
    C.op("scalar", "activation", reads=[sc["ard"]], writes=[sc["rho8"]], out=sc["rho8"][:], in_=sc["ard"][:], func=AF.Exp, scale=8.0)
    V("tensor_tensor", reads=[pa], writes=[sc["den"]], out=sc["den"][:], in0=pa[:, 0, :], in1=pa[:, 0, :], op=ALU.mult)
    V("tensor_tensor", reads=[pa], writes=[sc["t0"]], out=sc["t0"][:], in0=pa[:, 1, :], in1=pa[:, 1, :], op=ALU.mult)
    V("tensor_tensor", reads=[sc["den"], sc["t0"]], writes=[sc["den"]], out=sc["den"][:], in0=sc["den"][:], in1=sc["t0"][:], op=ALU.add)
    V("reciprocal", reads=[sc["den"]], writes=[sc["den"]], out=sc["den"][:], in_=sc["den"][:])
    V("tensor_scalar", reads=[sc["abr"]], writes=[sc["abr"]], out=sc["abr"][:], in0=sc["abr"][:], scalar1=-1.0, scalar2=None, op0=ALU.add)
    V("tensor_tensor", reads=[sc["abr"], pa], writes=[sc["t0"]], out=sc["t0"][:], in0=sc["abr"][:], in1=pa[:, 0, :], op=ALU.mult)
    V("tensor_tensor", reads=[sc["abi"], pa], writes=[sc["t1"]], out=sc["t1"][:], in0=sc["abi"][:], in1=pa[:, 1, :], op=ALU.mult)
    V("tensor_tensor", reads=[sc["t0"], sc["t1"]], writes=[sc["t0"]], out=sc["t0"][:], in0=sc["t0"][:], in1=sc["t1"][:], op=ALU.add)
    V("tensor_tensor", reads=[sc["t0"], sc["den"]], writes=[sc["fre"]], out=sc["fre"][:], in0=sc["t0"][:], in1=sc["den"][:], op=ALU.mult)
    V("tensor_tensor", reads=[sc["abi"], pa], writes=[sc["t0"]], out=sc["t0"][:], in0=sc["abi"][:], in1=pa[:, 0, :], op=ALU.mult)
    V("tensor_tensor", reads=[sc["abr"], pa], writes=[sc["t1"]], out=sc["t1"][:], in0=sc["abr"][:], in1=pa[:, 1, :], op=ALU.mult)
    V("tensor_tensor", reads=[sc["t0"], sc["t1"]], writes=[sc["t0"]], out=sc["t0"][:], in0=sc["t0"][:], in1=sc["t1"][:], op=ALU.subtract)
    V("tensor_tensor", reads=[sc["t0"], sc["den"]], writes=[sc["fim"]], out=sc["fim"][:], in0=sc["t0"][:], in1=sc["den"][:], op=ALU.mult)
    V("tensor_scalar", reads=[sc["fim"]], writes=[sc["nfim"]], out=sc["nfim"][:], in0=sc["fim"][:], scalar1=-1.0, scalar2=None, op0=ALU.mult)
    for kc in range(32):
        V("tensor_scalar", reads=[pb, sc["fre"]], writes=[Bb], out=Bb[:, 0, kc, :], in0=pb[:, 0, kc, :], scalar1=sc["fre"][:, kc:kc + 1], scalar2=None, op0=ALU.mult)
        V("scalar_tensor_tensor", reads=[pb, sc["nfim"], Bb], writes=[Bb], out=Bb[:, 0, kc, :], in0=pb[:, 1, kc, :], scalar=sc["nfim"][:, kc:kc + 1],
          in1=Bb[:, 0, kc, :], op0=ALU.mult, op1=ALU.add)
        V("tensor_scalar", reads=[pb, sc["fre"]], writes=[Bb], out=Bb[:, 1, kc, :], in0=pb[:, 1, kc, :], scalar1=sc["fre"][:, kc:kc + 1], scalar2=None, op0=ALU.mult)
        V("scalar_tensor_tensor", reads=[pb, sc["fim"], Bb], writes=[Bb], out=Bb[:, 1, kc, :], in0=pb[:, 0, kc, :], scalar=sc["fim"][:, kc:kc + 1],
          in1=Bb[:, 1, kc, :], op0=ALU.mult, op1=ALU.add)
    C.S.cur_scope = C.S.cur_scope.split("/")[0] + "/Srelay"
    for g in range(32):
        ct, gl8 = divmod(g, 8)
        b0 = (g % 2) * 2
        for h in range(2):
            bank = ps[b0 + h]
            for tau in range(8):
                C.mm(bank, bank[:, 0:HC], lhsT=sel[:, gl8, tau, :], rhs=p5[:, ct, tau * NCH + h * HC:tau * NCH + (h + 1) * HC], reads=[sel, p5],
                     start=(tau == 0), stop=(tau == 7))
        src = PS[:, b0 * 512:(b0 + 2) * 512].rearrange("q (b c) -> q b c", b=2)[:, :, 0:HC]
        dst = X[:, g, :].rearrange("q (b c) -> q b c", b=2)
        if g % 2 == 0:
            C.op("scalar", "copy", reads=[ps[b0], ps[b0 + 1]], writes=[Xb[g]], out=dst, in_=src)
        else:
            C.op("vector", "tensor_copy", reads=[ps[b0], ps[b0 + 1]], writes=[Xb[g]], out=dst, in_=src)
    C.S.cur_scope = C.S.cur_scope.split("/")[0] + "/Smain"
    wst = [C.sb("s5wst%d" % i, [128, 128]) for i in range(2)]
    xm = [C.sb("s5xm%d" % i, [128, 128]) for i in range(2)]
    ym = [C.sb("s5ym%d" % i, [128, 128]) for i in range(2)]
    tmpo = [C.sb("s5tmpo%d" % i, [128, 128]) for i in range(2)]
    tmpg = [C.sb("s5tmpg%d" % i, [128, 128]) for i in range(2)]
    Rm = [[C.sb("s5R%d%d" % (k, i), [128, 128], BF16) for i in range(2)] for k in range(2)]
    mstT = [C.sb("s5mstT%d" % i, [128, 128], BF16) for i in range(2)]
    mint = [[C.sb("s5mint%d%d" % (k, g2), [128, 128], BF16) for g2 in range(2)] for k in range(2)]
    Vs = [C.sb("s5Vs%d" % i, [128, NCH]) for i in range(2)]
    Vp = [C.sb("s5Vp%d" % i, [128, NCH]) for i in range(2)]
    Et = [C.sb("s5Et%d" % i, [128, NCH]) for i in range(2)]
    rh = C.sb("s5rh", [128, NCH])
    tq = [C.sb("s5tq%d" % i, [128, NCH]) for i in range(4)]
    kiq = C.sb("s5kiq", [128, NCH], I32)
    Sp = [[C.sb("s5Sp%d%d" % (k, i), [128, NCH], BF16) for i in range(2)] for k in range(2)]
    for k in range(2):
        for i in range(2):
            C.op("gpsimd", "memset", writes=[Sp[k][i]], ap=Sp[k][i][:], constant=0.0)
    ga = C.sb("s5ga", [128, 512]); gb = C.sb("s5gb", [128, 512]); gc = C.sb("s5gc", [128, 512])
    G = lambda name, **kw: C.op("gpsimd", name, **kw)
    for gp in range(16):
        ct = gp // 4
        for k in range(2):
            kc = k * 16 + gp
            Br, Bi = Bb[:, 0, kc, :], Bb[:, 1, kc, :]
            Cr, Ci = pc[:, 0, kc, :], pc[:, 1, kc, :]
            P = lambda tab, a, b: (tab[:, 0, kc, a:b], tab[:, 1, kc, a:b])
            if k == 0:
                pw_st, pw_x, pw_y, pw_r = P(apr, 1, 9), P(ang_, 0, 8), P(apw, 0, 8), P(apw, 1, 9)
            else:
                pw_st, pw_x, pw_y, pw_r = P(apw, 0, 8), P(apw, 0, 8), P(ang_, 0, 8), P(apr, 0, 8)
            cplx_outer(C, "vector", wst[0], wst[0][:], wst[1][:], pw_st[0], pw_st[1], Br, Bi, [apw, apr, ang_, Bb], tmpo)
            wst[1].b = wst[0].b
            if k == 0:
                cplx_outer(C, "vector", xm[0], xm[0][:], xm[1][:], pw_x[0], pw_x[1], Br, Bi, [apw, apr, ang_, Bb], tmpo)
                xm[1].b = xm[0].b
                xsrc = xm
            else:
                xsrc = wst
            cplx_outer(C, "gpsimd", ym[0], ym[0][:], ym[1][:], pw_y[0], pw_y[1], Cr, Ci, [apw, apr, ang_, pc], tmpg, neg_i=True)
            ym[1].b = ym[0].b
            cplx_outer(C, "gpsimd", Rm[k][0], Rm[k][0][:], Rm[k][1][:], pw_r[0], pw_r[1], Cr, Ci, [apw, apr, ang_, pc], tmpg, neg_i=True)
            Rm[k][1].b = Rm[k][0].b
            for ri in range(2):
                C.mm(ps[6], ps[6][:, ri * 128:(ri + 1) * 128], lhsT=wst[ri][:], rhs=C.ident_f[:], reads=[wst[0], C.ident_f], start=True, stop=True)
            C.op("scalar", "copy", reads=[ps[6]], writes=[mstT[0]], out=mstT[0][:], in_=ps[6][:, 0:128])
            C.op("scalar", "copy", reads=[ps[6]], writes=[mstT[1]], out=mstT[1][:], in_=ps[6][:, 128:256])
            for g2 in range(2):
                hs = slice(g2 * 64, (g2 + 1) * 64)
                o = ps[7][:, g2 * 128:(g2 + 1) * 128]
                C.mm(ps[7], o, lhsT=xsrc[0][hs, :], rhs=ym[0][hs, :], reads=[xsrc[0], ym[0]], start=True, stop=False)
                C.mm(ps[7], o, lhsT=xsrc[1][hs, :], rhs=ym[1][hs, :], reads=[xsrc[0], ym[0]], start=False, stop=True)
                V("tensor_tensor", reads=[ps[7], masks], writes=[mint[k][g2]], out=mint[k][g2][:], in0=o, in1=masks[:, k, :], op=ALU.mult)
            for ri in range(2):
                for g2 in range(2):
                    g = 2 * gp + g2
                    for h in range(2):
                        bank = ps[ri * 2 + h]
                        C.mm(bank, bank[g2 * 64:(g2 + 1) * 64, 0:HC], lhsT=mstT[ri][:, g2 * 64:(g2 + 1) * 64], rhs=X[:, g, h * HC:(h + 1) * HC],
                             reads=[mstT[ri], Xb[g]], start=True, stop=True)
                src = PS[:, ri * 1024:(ri + 1) * 1024].rearrange("q (b c) -> q b c", b=2)[:, :, 0:HC]
                C.op("scalar", "copy", reads=[ps[ri * 2], ps[ri * 2 + 1]], writes=[Vs[ri]], out=Vs[ri][:].rearrange("q (b c) -> q b c", b=2), in_=src)
            reduce_sin(C, "vector", Et[0], Et[0][:], kp, kp[:], tq[0], tq[0][:], kiq, kiq[:], mul=sc["phi"][:, kc:kc + 1], add=0.0, reads=[sc["phi"]])
            reduce_sin(C, "vector", Et[1], Et[1][:], kp, kp[:], tq[0], tq[0][:], kiq, kiq[:], mul=sc["phi"][:, kc:kc + 1], add=math.pi / 2, reads=[sc["phi"]])
            V("tensor_scalar", reads=[kp, sc["rho8"]], writes=[rh], out=rh[:], in0=kp[:], scalar1=0.0, scalar2=sc["rho8"][:, kc:kc + 1], op0=ALU.mult, op1=ALU.add)
            if k == 0:
                segs_in = [(slice(0, NCH), slice(0, NCH))]
                segs_out = [(slice(0, NCH - 1), slice(1, NCH))]
            else:
                segs_in = [(slice(0, 32), slice(31, None, -1)), (slice(32, NCH), slice(NCH - 1, 31, -1))]
                segs_out = [(slice(0, 31), slice(30, None, -1)), (slice(31, NCH - 1), slice(NCH - 1, 31, -1))]
            sn, cs = Et[0], Et[1]
            for (pp, cc) in segs_in:
                n = pp.stop - pp.start
                V("tensor_tensor", reads=[Vs[0], cs], writes=[tq[0]], out=tq[0][:, pp], in0=Vs[0][:, cc], in1=cs[:, pp], op=ALU.mult)
                V("tensor_tensor", reads=[Vs[1], sn], writes=[tq[1]], out=tq[1][:, pp], in0=Vs[1][:, cc], in1=sn[:, pp], op=ALU.mult)
                V("tensor_tensor", reads=[tq[0], tq[1]], writes=[Vp[0]], out=Vp[0][:, pp], in0=tq[0][:, pp], in1=tq[1][:, pp], op=ALU.add)
                G("tensor_tensor", reads=[Vs[1], cs], writes=[tq[2]], out=tq[2][:, pp], in0=Vs[1][:, cc], in1=cs[:, pp], op=ALU.mult)
                G("tensor_tensor", reads=[Vs[0], sn], writes=[tq[3]], out=tq[3][:, pp], in0=Vs[0][:, cc], in1=sn[:, pp], op=ALU.mult)
                G("tensor_tensor", reads=[tq[2], tq[3]], writes=[Vp[1]], out=Vp[1][:, pp], in0=tq[2][:, pp], in1=tq[3][:, pp], op=ALU.subtract)
            V("tensor_tensor_scan", reads=[rh, Vp[0]], writes=[Vp[0]], out=Vp[0][:], data0=rh[:], data1=Vp[0][:], initial=0.0, op0=ALU.mult, op1=ALU.add)
            V("tensor_tensor_scan", reads=[rh, Vp[1]], writes=[Vp[1]], out=Vp[1][:], data0=rh[:], data1=Vp[1][:], initial=0.0, op0=ALU.mult, op1=ALU.add)
            for (pp, cc) in segs_out:
                V("tensor_tensor", reads=[Vp[0], cs], writes=[tq[0]], out=tq[0][:, pp], in0=Vp[0][:, pp], in1=cs[:, pp], op=ALU.mult)
                V("tensor_tensor", reads=[Vp[1], sn], writes=[tq[1]], out=tq[1][:, pp], in0=Vp[1][:, pp], in1=sn[:, pp], op=ALU.mult)
                V("tensor_tensor", reads=[tq[0], tq[1]], writes=[Sp[k][0]], out=Sp[k][0][:, cc], in0=tq[0][:, pp], in1=tq[1][:, pp], op=ALU.subtract)
                G("tensor_tensor", reads=[Vp[1], cs], writes=[tq[2]], out=tq[2][:, pp], in0=Vp[1][:, pp], in1=cs[:, pp], op=ALU.mult)
                G("tensor_tensor", reads=[Vp[0], sn], writes=[tq[3]], out=tq[3][:, pp], in0=Vp[0][:, pp], in1=sn[:, pp], op=ALU.mult)
                G("tensor_tensor", reads=[tq[2], tq[3]], writes=[Sp[k][1]], out=Sp[k][1][:, cc], in0=tq[2][:, pp], in1=tq[3][:, pp], op=ALU.add)
        for g2 in range(2):
            g = 2 * gp + g2
            hs = slice(g2 * 64, (g2 + 1) * 64)
            for h in range(2):
                bank = ps[4 + h]
                cs_ = slice(h * HC, (h + 1) * HC)
                for k in range(2):
                    C.mm(bank, bank[:, 0:HC], lhsT=mint[k][g2][:], rhs=X[:, g, cs_], reads=[mint[k][g2], Xb[g]], start=(k == 0), stop=False)
                    C.mm(bank, bank[:, 0:HC], lhsT=Rm[k][0][hs, :], rhs=Sp[k][0][hs, cs_], reads=[Rm[k][0], Sp[k][0]], start=False, stop=False)
                    C.mm(bank, bank[:, 0:HC], lhsT=Rm[k][1][hs, :], rhs=Sp[k][1][hs, cs_], reads=[Rm[k][0], Sp[k][1]], start=False, stop=(k == 1))
            src = PS[:, 4 * 512:6 * 512].rearrange("q (b c) -> q b c", b=2)[:, :, 0:HC]
            C.op("scalar", "copy", reads=[ps[4], ps[5]], writes=[Xb[g]], out=X[:, g, :].rearrange("q (b c) -> q b c", b=2), in_=src)
    C.S.cur_scope = C.S.cur_scope.split("/")[0] + "/Sout"
    nb = 0
    for ct in range(4):
        for t0 in range(0, TOK, 512):
            bank = ps[nb % 4]; nb += 1
            c0 = t0 // 8
            nt_ = min(512, TOK - t0)
            ncq = nt_ // 8
            for tau in range(8):
                for gl8 in range(8):
                    g = ct * 8 + gl8
                    C.mm(bank, bank[:, tau:nt_:8], lhsT=selT[:, gl8, tau, :], rhs=X[:, g, c0:c0 + ncq], reads=[selT, Xb[g]], start=(gl8 == 0), stop=(gl8 == 7))
            cs_ = slice(t0, t0 + nt_)
            w_ = slice(0, nt_)
            u_ap = p5[:, ct, :].rearrange("p (t c) -> p c t", t=8)[:, c0:c0 + ncq, :]
            V("scalar_tensor_tensor", reads=[p5, dcol, bank], writes=[ga], out=ga[:, w_].rearrange("p (c t) -> p c t", t=8), in0=u_ap,
              scalar=dcol[:, ct:ct + 1], in1=bank[:, w_].rearrange("p (c t) -> p c t", t=8), op0=ALU.mult, op1=ALU.add)
            G("tensor_tensor", reads=[ga], writes=[gb], out=gb[:, w_], in0=ga[:, w_], in1=ga[:, w_], op=ALU.mult)
            G("tensor_scalar", reads=[gb], writes=[gb], out=gb[:, w_], in0=gb[:, w_], scalar1=0.044715, scalar2=1.0, op0=ALU.mult, op1=ALU.add)
            G("tensor_tensor", reads=[gb, ga], writes=[gb], out=gb[:, w_], in0=gb[:, w_], in1=ga[:, w_], op=ALU.mult)
            C.op("scalar", "activation", reads=[gb], writes=[gc], out=gc[:, w_], in_=gb[:, w_], func=AF.Tanh, scale=0.7978845608028654)
            C.op("scalar", "mul", reads=[ga], writes=[ga], out=ga[:, w_], in_=ga[:, w_], mul=0.5)
            V("scalar_tensor_tensor", reads=[gc, ga], writes=[ysb], out=ysb[:, cs_], in0=gc[:, w_], scalar=1.0, in1=ga[:, w_], op0=ALU.add, op1=ALU.mult)
        C.dma(D["yss"][ct * 128:(ct + 1) * 128, :], ysb[:], reads=[ysb], eng="scalar")
    if "d_ys" in C.dbg and l == 0:
        C.S.barrier()
        C.dma(D["d_ys"][:, :], D["yss"][:, :], key="dbg")


def load_weight_bf16(C, dst, src, nk, ncols, stg, col0=0, blk=None, cnt=[0]):
    blk = blk or stg[0].ap.shape[1]
    for k in range(nk):
        for c0 in range(0, ncols, blk):
            n = min(blk, ncols - c0)
            st = stg[cnt[0] % len(stg)]
            C.dma(st[:, 0:n], src[k * 128:(k + 1) * 128, col0 + c0:col0 + c0 + n], writes=[st])
            e = cnt[0] % 3
            if e == 0:
                C.op("gpsimd", "tensor_copy", reads=[st], writes=[dst], out=dst[:, k, c0:c0 + n], in_=st[:, 0:n])
            elif e == 1:
                C.op("scalar", "copy", reads=[st], writes=[dst], out=dst[:, k, c0:c0 + n], in_=st[:, 0:n])
            else:
                C.op("vector", "tensor_copy", reads=[st], writes=[dst], out=dst[:, k, c0:c0 + n], in_=st[:, 0:n])
            cnt[0] += 1


def bcast_mod(C, l, j, dst):
    row = C.sb("bcrow", [2, 1024])
    C.dma(row[:], C.D["mods"][l, :, j * 1024:(j + 1) * 1024], reads=[C.bmods], writes=[row])
    for r in range(2):
        for half in range(2):
            ps = C.psum[r * 2 + half]
            C.mm(ps, ps[:, :], lhsT=C.sel2[:, r, :], rhs=row[:, half * 512:(half + 1) * 512], reads=[C.sel2, row], start=True, stop=True)
            C.op("vector", "tensor_copy", reads=[ps], writes=[dst], out=dst[:, r, half * 512:(half + 1) * 512], in_=ps[:, :])


def phaseC(C, l, last):
    D = C.D
    WG = C.sb("cWG", [128, 8, 3072], BF16)
    glu = C.sb("cglu", [128, 4, 2048], BF16)
    scow = C.sb("cscow", [128, 4, 1024], BF16)
    hyow = C.sb("chyow", [128, 4, 1024], BF16)
    outw = C.sb("coutw", [128, 8, 1024], BF16)
    stg = [C.sb("cstg%d" % i, [128, 2048]) for i in range(2)]
    g1bc = C.sb("cg1bc", [128, 2, 1024])
    xt = C.sb("cxt", [128, 4, 1024])
    xtb = [Buf("cxt%d" % i) for i in range(4)]
    hT = C.sb("chT", [128, 8, 512], BF16)
    ysgs = [C.sb("cysg%d" % i, [128, 4, 512], BF16) for i in range(2)]
    scgs = [C.sb("cscg%d" % i, [128, 4, 512], BF16) for i in range(2)]
    zxgs = [C.sb("czxg%d" % i, [128, 4, 512], BF16) for i in range(2)]
    m = C.sb("cm", [128, 8, 512], BF16)
    nsc = make_norm_scratch(C)
    sg = [C.sb("csg%d" % i, [128, 512]) for i in range(2)]
    t1 = [C.sb("ct1%d" % i, [128, 512]) for i in range(2)]
    macc = [C.sb("cmacc%d" % i, [128, 512]) for i in range(2)]
    tmpx = [C.sb("ctmpx%d" % i, [128, 512]) for i in range(2)]
    bcast_mod(C, l, 2, g1bc)
    load_weight_bf16(C, WG, D["w_in"][l], 8, 3072, stg, col0=OFF_GATE)
    load_weight_bf16(C, glu, D["glu_w"][l], 4, 2048, stg)
    load_weight_bf16(C, scow, D["sc_out_w"][l], 4, 1024, stg)
    load_weight_bf16(C, hyow, D["hy_out_w"][l], 4, 1024, stg)
    load_weight_bf16(C, outw, D["out_w"][l], 8, 1024, stg)
    C.S.cur_scope = C.S.cur_scope.split("/")[0] + "/Cmain"
    ysv = D["yss"].rearrange("(j c) t -> c j t", c=128)
    scv = D["scm"].rearrange("(j c) t -> c j t", c=128)
    zxv = D["zxs"].rearrange("(j c) t -> c j t", c=128)
    groups = GROUPS[1:] if last else GROUPS
    cnt = 0
    ring = [0]
    for gidx, (g0, gn) in enumerate(groups):
        r = 1 if g0 < 256 else 0
        ntl = gn // 128
        ysg, scg, zxg = ysgs[gidx % 2], scgs[gidx % 2], zxgs[gidx % 2]
        items = []
        for ti in range(ntl):
            i = g0 // 128 + ti
            xti = T(xt[:, ti, :], "x"); xti.b = xtb[ti]
            C.dma(xt[:, ti, :], D["xs"][i * 128:(i + 1) * 128, :], reads=[C.bxs[i]], writes=[xtb[ti]])
            items.append((r, xti, hT, ti * 128))
        norm_tiles(C, 0, items, nsc, [C.psum[6], C.psum[7]])
        C.dma(ysg[:, :, 0:gn], ysv[:, :, g0:g0 + gn], writes=[ysg])
        C.dma(scg[:, :, 0:gn], scv[:, :, g0:g0 + gn], writes=[scg])
        C.dma(zxg[:, :, 0:gn], zxv[:, :, g0:g0 + gn], writes=[zxg])
        for j in range(8):
            js = slice(j * 128, (j + 1) * 128)
            mc = macc[j % 2]
            stages = [
                [(glu, ysg, lambda k: glu[:, k, js], 4), (glu, ysg, lambda k: glu[:, k, 1024 + j * 128:1024 + (j + 1) * 128], 4),
                 (WG, hT, lambda k: WG[:, k, j * 128:(j + 1) * 128], 8)],
                [(scow, scg, lambda k: scow[:, k, js], 4), (WG, hT, lambda k: WG[:, k, 1024 + j * 128:1024 + (j + 1) * 128], 8)],
                [(hyow, zxg, lambda k: hyow[:, k, js], 4), (WG, hT, lambda k: WG[:, k, 2048 + j * 128:2048 + (j + 1) * 128], 8)],
            ]
            for si, stage in enumerate(stages):
                bk = []
                for (wt_, rt_, lfn, nk) in stage:
                    b_ = C.psum[ring[0] % 8]; ring[0] += 1
                    bk.append(b_)
                    for k in range(nk):
                        C.mm(b_, b_[:, 0:gn], lhsT=lfn(k), rhs=rt_[:, k, 0:gn], reads=[wt_, rt_], start=(k == 0), stop=(k == nk - 1))
                sgt, tt = sg[cnt % 2], t1[cnt % 2]; cnt += 1
                if si == 0:
                    C.op("scalar", "activation", reads=[bk[1]], writes=[sgt], out=sgt[:, 0:gn], in_=bk[1][:, 0:gn], func=AF.Sigmoid)
                    C.op("vector", "tensor_tensor", reads=[bk[0], sgt], writes=[mc], out=mc[:, 0:gn], in0=bk[0][:, 0:gn], in1=sgt[:, 0:gn], op=ALU.mult)
                    sg2 = sg[cnt % 2]; cnt += 1
                    C.op("scalar", "activation", reads=[bk[2]], writes=[sg2], out=sg2[:, 0:gn], in_=bk[2][:, 0:gn], func=AF.Sigmoid)
                    C.op("gpsimd", "tensor_tensor", reads=[mc, sg2], writes=[mc], out=mc[:, 0:gn], in0=mc[:, 0:gn], in1=sg2[:, 0:gn], op=ALU.mult)
                else:
                    C.op("scalar", "activation", reads=[bk[1]], writes=[sgt], out=sgt[:, 0:gn], in_=bk[1][:, 0:gn], func=AF.Sigmoid)
                    C.op("vector", "tensor_tensor", reads=[bk[0], sgt], writes=[tt], out=tt[:, 0:gn], in0=bk[0][:, 0:gn], in1=sgt[:, 0:gn], op=ALU.mult)
                    if si == 1:
                        C.op("gpsimd", "tensor_tensor", reads=[mc, tt], writes=[mc], out=mc[:, 0:gn], in0=mc[:, 0:gn], in1=tt[:, 0:gn], op=ALU.add)
                    else:
                        C.op("gpsimd", "tensor_tensor", reads=[mc, tt], writes=[m], out=m[:, j, 0:gn], in0=mc[:, 0:gn], in1=tt[:, 0:gn], op=ALU.add)
        for ti in range(ntl):
            i = g0 // 128 + ti
            for half in range(2):
                ps = C.psum[ring[0] % 8]; ring[0] += 1
                hs = slice(half * 512, (half + 1) * 512)
                for k in range(8):
                    C.mm(ps, ps[:, :], lhsT=m[:, k, ti * 128:(ti + 1) * 128], rhs=outw[:, k, hs], reads=[m, outw], start=(k == 0), stop=(k == 7))
                tx = tmpx[half]
                C.op("vector", "tensor_tensor", reads=[ps, g1bc], writes=[tx], out=tx[:], in0=ps[:, :], in1=g1bc[:, r, hs], op=ALU.mult)
                C.op("gpsimd", "tensor_tensor", reads=[tx, xtb[ti]], writes=[xtb[ti]], out=xt[:, ti, hs], in0=xt[:, ti, hs], in1=tx[:], op=ALU.add)
            C.dma(D["xs"][i * 128:(i + 1) * 128, :], xt[:, ti, :], reads=[xtb[ti]], writes=[C.bxs[i]], eng="scalar")
    if "d_xs" in C.dbg and C.dbg_stop == ("phaseC", l):
        C.S.barrier()
        C.dma(D["d_xs"][:, :], D["xs"][:, :], key="dbg")


def phaseD(C, l, last):
    D = C.D
    w1b = C.sb("dw1b", [128, 8, 4096], BF16)
    w2b = C.sb("dw2b", [128, 32, 1024], BF16)
    stg = [C.sb("dstg%d" % i, [128, 1024]) for i in range(2)]
    g2bc = C.sb("dg2bc", [128, 2, 1024])
    xt = C.sb("dxt", [128, 2, 1024])
    xtb = [Buf("dxt%d" % i) for i in range(2)]
    h2 = C.sb("dh2", [128, 8, 256], BF16)
    rr_ = C.sb("dr", [128, 32, 256], BF16)
    nsc = make_norm_scratch(C)
    rt = [C.sb("drt%d" % i, [128, 256]) for i in range(2)]
    tmpx = [C.sb("dtmpx%d" % i, [128, 512]) for i in range(2)]
    if last:
        fg = C.sb("dfg", [128, 1024]); fo = C.sb("dfo", [128, 1024])
        fss = C.sb("dfss", [128, 1]); frs = C.sb("dfrs", [128, 1])
        C.dma(fg[:], D["finalg_bc"][:, :], writes=[fg])
    bcast_mod(C, l, 5, g2bc)
    load_weight_bf16(C, w1b, D["mlp_w1"][l], 8, 4096, stg)
    load_weight_bf16(C, w2b, D["mlp_w2"][l], 32, 1024, stg)
    C.S.cur_scope = C.S.cur_scope.split("/")[0] + "/Dmain"
    cnt = 0
    for gi in range(1 if last else 0, NT // 2):
        r = 1 if gi == 0 else 0
        items = []
        for ti in range(2):
            i = gi * 2 + ti
            xti = T(xt[:, ti, :], "x"); xti.b = xtb[ti]
            C.dma(xt[:, ti, :], D["xs"][i * 128:(i + 1) * 128, :], reads=[C.bxs[i]], writes=[xtb[ti]])
            items.append((r, xti, h2, ti * 128))
        norm_tiles(C, 1, items, nsc, [C.psum[6], C.psum[7]])
        for i in range(32):
            ps = C.psum[i % 4]
            for k in range(8):
                C.mm(ps, ps[:, 0:256], lhsT=w1b[:, k, i * 128:(i + 1) * 128], rhs=h2[:, k, :], reads=[w1b, h2], start=(k == 0), stop=(k == 7))
            rtt = rt[i % 2]
            C.op("scalar", "activation", reads=[ps], writes=[rtt], out=rtt[:], in_=ps[:, 0:256], func=AF.Relu)
            C.op("gpsimd", "tensor_tensor", reads=[rtt], writes=[rr_], out=rr_[:, i, :], in0=rtt[:], in1=rtt[:], op=ALU.mult)
        for ti in range(2):
            i = gi * 2 + ti
            for half in range(2):
                ps = C.psum[4 + (ti * 2 + half) % 2]
                hs = slice(half * 512, (half + 1) * 512)
                for k in range(32):
                    C.mm(ps, ps[:, :], lhsT=rr_[:, k, ti * 128:(ti + 1) * 128], rhs=w2b[:, k, hs], reads=[rr_, w2b], start=(k == 0), stop=(k == 31))
                tx = tmpx[half]
                C.op("vector", "tensor_tensor", reads=[ps, g2bc], writes=[tx], out=tx[:], in0=ps[:, :], in1=g2bc[:, r, hs], op=ALU.mult)
                C.op("gpsimd", "tensor_tensor", reads=[tx, xtb[ti]], writes=[xtb[ti]], out=xt[:, ti, hs], in0=xt[:, ti, hs], in1=tx[:], op=ALU.add)
            if not last:
                C.dma(D["xs"][i * 128:(i + 1) * 128, :], xt[:, ti, :], reads=[xtb[ti]], writes=[C.bxs[i]], eng="scalar")
            else:
                if "d_xs" in C.dbg:
                    C.dma(D["xs"][i * 128:(i + 1) * 128, :], xt[:, ti, :], reads=[xtb[ti]], writes=[C.bxs[i]], eng="scalar")
                C.op("scalar", "activation", reads=[xtb[ti]], writes=[fo, fss], out=fo[:], in_=xt[:, ti, :], func=AF.Square, accum_out=fss[:])
                C.op("vector", "tensor_scalar", reads=[fss], writes=[frs], out=frs[:], in0=fss[:], scalar1=1.0 / D_MODEL, scalar2=EPS, op0=ALU.mult, op1=ALU.add)
                C.op("scalar", "activation", reads=[frs], writes=[frs], out=frs[:], in_=frs[:], func=AF.Sqrt)
                C.op("vector", "reciprocal", reads=[frs], writes=[frs], out=frs[:], in_=frs[:])
                C.op("vector", "scalar_tensor_tensor", reads=[xtb[ti], frs, fg], writes=[fo], out=fo[:], in0=xt[:, ti, :], scalar=frs[:, 0:1], in1=fg[:],
                     op0=ALU.mult, op1=ALU.mult)
                C.dma(D["out"][(i - 2) * 128:(i - 1) * 128, :], fo[:], reads=[fo], eng="scalar")
    if "d_xs" in C.dbg and C.dbg_stop == ("phaseD", l):
        C.S.barrier()
        C.dma(D["d_xs"][:, :], D["xs"][:, :], key="dbg")


_CONST = {}


def _consts():
    if _CONST:
        return _CONST
    bf = ml_dtypes.bfloat16
    c = _CONST
    c["ident_bf"] = np.eye(128, dtype=np.float32).astype(bf)
    c["ident_f"] = np.eye(128, dtype=np.float32)
    c["ones_bf"] = np.ones((128, 128), np.float32).astype(bf)
    alt = (1.0 - 2.0 * (np.arange(128) % 2)).astype(np.float32)
    c["alt_bf"] = np.repeat(alt[:, None], 128, axis=1).astype(bf)
    c["altcol"] = alt[:, None].copy()
    sel = np.zeros((2, 2, 128), np.float32); sel[0, 0] = 1; sel[1, 1] = 1
    c["sel2"] = sel
    for L, sfx in ((SEQ, ""), (CTX, "256")):
        N = 2 * L
        nt = L // 128
        t = (np.arange(nt)[None, :, None, None] * 128 + np.arange(128)[None, None, :, None]).astype(np.int64)
        f = (np.arange(nt)[:, None, None, None] * 128 + np.arange(128)[None, None, None, :]).astype(np.int64)
        ph = ((t * f) % N).astype(np.float64) * (2.0 * np.pi / N)
        c["dftc" + sfx] = np.ascontiguousarray(np.cos(ph).transpose(0, 2, 1, 3)).astype(np.float32).astype(bf)
        c["dfts" + sfx] = np.ascontiguousarray(np.sin(ph).transpose(0, 2, 1, 3)).astype(np.float32).astype(bf)
        tl = np.linspace(0.0, 1.0, L, dtype=np.float32)[:, None]
        ang = (2.0 * np.float32(math.pi) * np.arange(L, dtype=np.float32)[:, None] / np.float32(L)).astype(np.float32)
        bands = np.linspace(1e-4, 15, 16, dtype=np.float32)[None, :]
        z = np.concatenate([tl, np.cos(bands * ang), -np.sin(bands * ang)], axis=-1).astype(np.float32)
        c["zT" + sfx] = np.ascontiguousarray(z.T)
        mx = math.log(1e-2) / 0.3; mn = math.log(1e-2) / 1.5
        deltas = np.abs(np.linspace(mn, mx, 512, dtype=np.float32))
        c["win" + sfx] = (np.exp(-tl * deltas[None, :]) + np.float32(0.05)).astype(np.float32)
        fidx = np.arange(nt)[None, :] * 128 + np.arange(128)[:, None]
        c["wf" + sfx] = np.where(fidx == 0, 1.0 / N, 2.0 / N).astype(np.float32)
    c["kpos"] = np.repeat(np.arange(TOK, dtype=np.float32)[None, :], 128, axis=0)
    sel = np.zeros((128, 8, 8, 128), np.float32)
    for g in range(8):
        for tau in range(8):
            for p in range(16):
                sel[g * 16 + p, g, tau, tau * 16 + p] = 1.0
    c["sel8"] = sel.astype(bf)
    c["selT8"] = np.ascontiguousarray(sel.transpose(3, 1, 2, 0)).astype(bf)
    ti = np.arange(128)[:, None] // 16; to = np.arange(128)[None, :] // 16
    c["cmask"] = np.ascontiguousarray(np.stack([(to >= ti), (to <= ti)], axis=1)).astype(np.float32)
    quarter = D_MODEL // 4
    omega = (1.0 / (10000.0 ** (np.arange(quarter, dtype=np.float32) / np.float32(quarter)))).astype(np.float32)
    rows = SEQ // 64
    er = np.arange(rows, dtype=np.float32)[:, None] * omega[None]
    ec = np.arange(64, dtype=np.float32)[:, None] * omega[None]
    er = np.concatenate([np.sin(er), np.cos(er)], -1); ec = np.concatenate([np.sin(ec), np.cos(ec)], -1)
    emb = np.concatenate([np.broadcast_to(er[:, None, :], (rows, 64, 512)), np.broadcast_to(ec[None, :, :], (rows, 64, 512))], -1)
    c["pos"] = np.ascontiguousarray(emb.reshape(SEQ, D_MODEL)).astype(np.float32)
    return c


def _colmajor(v, nk):
    v = np.asarray(v, np.float32)
    r = v.reshape(v.shape[:-1] + (nk, 128))
    return np.ascontiguousarray(np.moveaxis(r, -1, 0))


def prep_shared(inp):
    L = DEPTH
    sh = dict(_consts())
    f32 = lambda a: np.ascontiguousarray(np.asarray(a, np.float32))
    for k_src, k_dst in (("ada_w", "ada_w"), ("w_in", "w_in"), ("s5_glu_w", "glu_w"), ("sc_out_w", "sc_out_w"),
                         ("hy_out_w", "hy_out_w"), ("out_w", "out_w"), ("mlp_w1", "mlp_w1"), ("mlp_w2", "mlp_w2"),
                         ("hy_f_w1", "hyw1"), ("hy_f_w2", "hyw2"), ("hy_f_w3", "hyw3"), ("hy_f_w4", "hyw4")):
        sh[k_dst] = f32(inp[k_src])
    sh["adab2"] = np.ascontiguousarray(np.repeat(f32(inp["ada_b"])[:, None, :], 2, axis=1))
    n1 = _colmajor(inp["norm1_g"], 8); n2 = _colmajor(inp["norm2_g"], 8)
    sh["ncol"] = np.ascontiguousarray(np.stack([n1, n2], axis=2))
    sh["finalg_bc"] = np.ascontiguousarray(np.repeat(f32(inp["final_g"])[None, :], 128, axis=0))
    sh["scw"] = np.ascontiguousarray(_colmajor(inp["sc_conv_w"], 4).transpose(0, 1, 3, 2))
    sh["hcw"] = np.ascontiguousarray(_colmajor(inp["hy_conv_w"], 12).transpose(0, 1, 3, 2))
    sh["hcb"] = _colmajor(inp["hy_conv_b"], 12)
    def qlay(a):
        a = f32(a)
        s = a.shape
        a = a.reshape(s[0], 2, 16, 2, 64, *s[4:])
        a = np.moveaxis(a, (3, 4), (0, 1))
        return np.ascontiguousarray(a.reshape(128, s[0], 32, *s[4:]))
    ldt = np.repeat(f32(inp["s5_log_dt"])[:, :, :, None], 64, axis=3)
    sh["s5a"] = np.ascontiguousarray(np.stack([qlay(inp["s5_a_re"]), qlay(inp["s5_a_im"]), qlay(ldt)], axis=2))
    sh["s5b"] = np.ascontiguousarray(np.stack([qlay(inp["s5_b_re"]), qlay(inp["s5_b_im"])], axis=2))
    cre = np.swapaxes(f32(inp["s5_c_re"]), 3, 4); cim = np.swapaxes(f32(inp["s5_c_im"]), 3, 4)
    sh["s5c"] = np.ascontiguousarray(np.stack([qlay(cre), qlay(cim)], axis=2))
    sh["s5d"] = _colmajor(inp["s5_d"], 4)
    hyfb = np.stack([f32(inp["hy_f_freq"]), f32(inp["hy_f_b1"]), f32(inp["hy_f_b2"]), f32(inp["hy_f_b3"])], axis=-1)
    sh["hyfb"] = np.ascontiguousarray(hyfb.transpose(1, 0, 2))
    sh["hyd_bc"] = np.ascontiguousarray(np.repeat(f32(inp["hy_d"])[None, :, :], 128, axis=0))
    return sh


def prep_core(inp, sh, b):
    m = dict(sh)
    m["x_in"] = np.ascontiguousarray(np.asarray(inp["x"][b], np.float32))
    m["ctx_in"] = np.ascontiguousarray(np.asarray(inp["ctx"][b], np.float32))
    cc = np.stack([_colmajor(np.asarray(inp["c"][b]), 8), _colmajor(np.asarray(inp["c_ctx"]), 8)], axis=-1)
    m["cc"] = np.ascontiguousarray(cc)
    return m


_NC_CACHE = {}


def kernel(**inputs):
    inp = {k: np.asarray(v) for k, v in inputs.items()}
    sh = prep_shared(inp)
    if "nc" not in _NC_CACHE:
        _NC_CACHE["nc"] = build_program()
    nc = _NC_CACHE["nc"]
    real = [prep_core(inp, sh, b) for b in range(4)]
    consts = set(_consts().keys())
    zero = {k: (v if k in consts else np.zeros_like(v)) for k, v in real[0].items()}
    in_maps = [real[c // 2] if c % 2 == 0 else zero for c in range(8)]
    res = run_bass_kernel_spmd(nc, in_maps, core_ids=list(range(8)))
    out = np.stack([np.asarray(res.results[2 * b]["out"], np.float32) for b in range(4)], axis=0)
    return out
```

```python
import math
import numpy as np
import ml_dtypes
import concourse.bass as bass
import concourse.mybir as mybir
from concourse.bass_utils import run_bass_kernel_spmd
from contextlib import ExitStack

F32 = mybir.dt.float32
BF16 = mybir.dt.bfloat16
I32 = mybir.dt.int32
ALU = mybir.AluOpType
AF = mybir.ActivationFunctionType

D_MODEL = 1024; SEQ = 4096; CTX = 256; DEPTH = 2; TOK = SEQ + CTX; NT = TOK // 128
D_IN = 6656; OFF_SC = 512; OFF_HY = 2048; OFF_GATE = 3584; D_FF = 4096
EPS = 1e-6
TWO_PI = 2.0 * math.pi
ENGS = ["sync", "scalar", "vector", "gpsimd", "tensor"]
SB_BASE = 16512
SB_END = 229376


class Buf:
    __slots__ = ("name", "last_w", "readers")

    def __init__(self, name):
        self.name = name
        self.last_w = None
        self.readers = []


class Op:
    __slots__ = ("eng", "fn", "dma", "pos", "deps", "signal", "sig_idx", "sem_key", "dma_val", "scope")

    def __init__(self, eng, fn, dma):
        self.eng = eng; self.fn = fn; self.dma = dma
        self.deps = []; self.signal = False; self.sig_idx = 0; self.sem_key = None; self.dma_val = 0


class Sched:
    def __init__(self, nc, same_engine_sync=True, n_dma_sems=72):
        self.nc = nc
        self.ops = {e: [] for e in ENGS}
        self.same_engine_sync = same_engine_sync
        self.dma_cnt = {}
        self.dma_since_barrier = []
        self.key_map = {}
        self.n_dma_sems = n_dma_sems
        import os
        self.use_scopes = bool(os.environ.get("KSCOPES"))

    def op(self, eng, fn, reads=(), writes=(), dma=False, sem_key=None):
        o = Op(eng, fn, dma)
        o.scope = getattr(self, "cur_scope", None)
        o.pos = len(self.ops[eng])
        deps = []
        for b in reads:
            if b.last_w is not None:
                deps.append(b.last_w)
        for b in writes:
            if b.last_w is not None:
                deps.append(b.last_w)
            deps.extend(b.readers)
        seen = set()
        for d in deps:
            if id(d) in seen or d is o:
                continue
            seen.add(id(d))
            if not d.dma and d.eng == eng:
                if eng == "tensor" or not self.same_engine_sync:
                    continue
            o.deps.append(d)
        for b in reads:
            if not dma:
                b.readers = [r for r in b.readers if r.dma or r.eng != eng]
            b.readers.append(o)
        for b in writes:
            b.last_w = o
            b.readers = []
        if dma:
            key = sem_key or (writes[0].name if writes else "dma_misc")
            if key not in self.key_map:
                self.key_map[key] = "q%d" % len(self.key_map)
                assert len(self.key_map) <= self.n_dma_sems, ("too many DMA streams in one phase", len(self.key_map))
            key = self.key_map[key]
            o.sem_key = key
            self.dma_cnt[key] = self.dma_cnt.get(key, 0) + 16
            o.dma_val = self.dma_cnt[key]
            self.dma_since_barrier.append(o)
        self.ops[eng].append(o)
        return o

    def barrier(self):
        lasts = [self.ops[e][-1] for e in ENGS if self.ops[e]]
        dmas = list(self.dma_since_barrier)
        self.dma_since_barrier = []
        self.key_map = {}
        for e in ENGS:
            o = Op(e, None, False)
            o.pos = len(self.ops[e])
            o.deps = [d for d in lasts if d.eng != e and not d.dma and d.fn is not None] + dmas
            self.ops[e].append(o)

    def emit(self, es):
        nc = self.nc
        for e in ENGS:
            for o in self.ops[e]:
                for d in o.deps:
                    if not d.dma:
                        d.signal = True
        eng_sem = {}
        for e in ENGS:
            n = 0
            for o in self.ops[e]:
                if o.signal and not o.dma:
                    n += 1
                    o.sig_idx = n
            if n:
                eng_sem[e] = es.enter_context(nc.semaphore("s_" + e))
            self.stats = getattr(self, "stats", {})
            self.stats[e] = (len(self.ops[e]), n)
        dma_sem = {}
        for key in self.dma_cnt:
            dma_sem[key] = es.enter_context(nc.semaphore("d_" + key))
        self.stats["dma"] = dict(self.dma_cnt)
        block = es.enter_context(nc.Block())

        def run(e, eng):
            waited = {}
            for o in self.ops[e]:
                need = {}
                for d in o.deps:
                    if d.dma:
                        k, v, sem = ("d", d.sem_key), d.dma_val, dma_sem[d.sem_key]
                    else:
                        k, v, sem = ("e", d.eng), d.sig_idx, eng_sem[d.eng]
                    if k not in need or need[k][0] < v:
                        need[k] = (v, sem)
                for k, (v, sem) in need.items():
                    if waited.get(k, 0) >= v:
                        continue
                    waited[k] = v
                    eng.wait_ge(sem, v)
                if o.fn is None:
                    continue
                if self.use_scopes and o.scope:
                    with nc.named_scope(o.scope):
                        ins = o.fn(eng)
                else:
                    ins = o.fn(eng)
                if o.dma:
                    ins.then_inc(dma_sem[o.sem_key], 16)
                elif o.signal:
                    ins.then_inc(eng_sem[e], 1)

        @block.sync
        def _(eng):
            run("sync", eng)

        @block.scalar
        def _(eng):
            run("scalar", eng)

        @block.vector
        def _(eng):
            run("vector", eng)

        @block.gpsimd
        def _(eng):
            run("gpsimd", eng)

        @block.tensor
        def _(eng):
            run("tensor", eng)


class T:
    def __init__(self, ap, name, nbuf=1):
        self.ap = ap
        self.b = Buf(name)

    def __getitem__(self, k):
        return self.ap[k]


class Ctx:
    def __init__(self, nc):
        self.nc = nc
        self.S = Sched(nc)
        self.uid = 0
        self.persist = SB_BASE
        self.top = SB_BASE
        self.dq = 0

    def reset_arena(self):
        self.top = self.persist

    def sb(self, name, shape, dt=F32, persist=False):
        esz = 4 if dt in (F32, I32) else 2
        nbytes = int(np.prod(shape[1:])) * esz
        nbytes = (nbytes + 63) // 64 * 64
        self.uid += 1
        off = self.top
        assert off + nbytes <= SB_END, ("SBUF overflow", name, off, nbytes)
        t = self.nc.alloc_sbuf_tensor_at("%s_%d" % (name, self.uid), list(shape), dt, offset=off)
        self.top += nbytes
        if persist:
            assert self.persist == off
            self.persist = self.top
        return T(t.ap(), "%s_%d" % (name, self.uid))

    def op(self, eng, name, reads=(), writes=(), **kw):
        rb = [x.b if isinstance(x, T) else x for x in reads]
        wb = [x.b if isinstance(x, T) else x for x in writes]
        return self.S.op(eng, lambda e: getattr(e, name)(**kw), rb, wb)

    def dma(self, out, in_, reads=(), writes=(), key=None, eng=None):
        rb = [x.b if isinstance(x, T) else x for x in reads]
        wb = [x.b if isinstance(x, T) else x for x in writes]
        if eng is None:
            eng = "sync"
        if key is None:
            dram = ("xs", "mods", "yabd")
            if wb and not wb[0].name.startswith(dram):
                key = wb[0].name
            elif rb:
                key = "st_" + rb[0].name
        return self.S.op(eng, lambda e: e.dma_start(out=out, in_=in_), rb, wb, dma=True, sem_key=key)

    def mm(self, out_t, out_ap, lhsT, rhs, reads, start, stop):
        return self.op("tensor", "matmul", reads=reads, writes=[out_t], out=out_ap, lhsT=lhsT, rhs=rhs,
                       start=start, stop=stop)


def ctile_cols(i):
    return 1 + i * 128 if i < 2 else 259 + (i - 2) * 128


PFW = 4356
GROUPS = [(0, 256)] + [(256 + 512 * i, 512) for i in range(8)]


def pf_off(tok):
    return 1 + tok if tok < 256 else 259 + (tok - 256)


def build_program(dbg=(), stop_after=None, n_layers=DEPTH):
    nc = bass.Bass("TRN2", target_bir_lowering=False)
    D = {}

    def din(name, shape, dt=F32):
        D[name] = nc.dram_tensor(name, list(shape), dt, kind="ExternalInput").ap()

    def dscr(name, shape, dt=F32):
        D[name] = nc.dram_tensor(name, list(shape), dt, kind="Internal").ap()

    def dout(name, shape, dt=F32):
        D[name] = nc.dram_tensor(name, list(shape), dt, kind="ExternalOutput").ap()

    for name, shape, dt in input_specs():
        din(name, shape, dt)
    dout("out", [SEQ, D_MODEL])
    dscr("xs", [TOK, D_MODEL])
    dscr("mods", [DEPTH, 2, 6 * D_MODEL])
    dscr("p5", [512, TOK], BF16)
    dscr("x0s", [512, TOK], BF16)
    dscr("scm", [512, TOK], BF16)
    dscr("zxs", [512, TOK], BF16)
    dscr("yss", [512, TOK], BF16)
    dscr("yab", [32, 128, 2, 512], BF16)
    for name, shape, dt in dbg_specs(dbg):
        dout(name, shape, dt)

    es = ExitStack()
    with es:
        C = Ctx(nc)
        C.D = D
        C.dbg = set(dbg)
        C.dbg_stop = stop_after
        C.psum_all = nc.alloc_psum_tensor("psall", [128, 4096], F32).ap()
        C.psum = [T(C.psum_all[:, i * 512:(i + 1) * 512], "pb%d" % i) for i in range(8)]
        C.ident_bf = C.sb("identbf", [128, 128], BF16, persist=True)
        C.ident_f = C.sb("identf", [128, 128], F32, persist=True)
        C.ones_bf = C.sb("onesbf", [128, 128], BF16, persist=True)
        C.alt_bf = C.sb("altbf", [128, 128], BF16, persist=True)
        C.altcol = C.sb("altcol", [128, 1], F32, persist=True)
        C.sel2 = C.sb("sel2", [2, 2, 128], F32, persist=True)
        C.scv = C.sb("scv", [128, 8, 2], F32, persist=True)
        C.cols = C.sb("cols", [128, 48, 2], F32, persist=True)
        C.scale1 = C.sb("scale1", [128, 8, 2], F32, persist=True)
        C.scale2 = C.sb("scale2", [128, 8, 2], F32, persist=True)
        C.ncol = C.sb("ncol", [128, DEPTH, 2, 8], F32, persist=True)
        C.dma(C.ident_bf[:], D["ident_bf"][:, :], writes=[C.ident_bf])
        C.dma(C.ident_f[:], D["ident_f"][:, :], writes=[C.ident_f])
        C.dma(C.ones_bf[:], D["ones_bf"][:, :], writes=[C.ones_bf])
        C.dma(C.alt_bf[:], D["alt_bf"][:, :], writes=[C.alt_bf])
        C.dma(C.altcol[:], D["altcol"][:, :], writes=[C.altcol])
        C.dma(C.sel2[:], D["sel2"][:, :, :], writes=[C.sel2])
        C.dma(C.ncol[:], D["ncol"][:, :, :, :], writes=[C.ncol])

        phase0(C)
        done = False
        for l in range(n_layers):
            last = l == DEPTH - 1
            for ph in (phaseP, phaseA, phaseH, phaseS, phaseC, phaseD):
                C.S.barrier()
                C.reset_arena()
                C.S.cur_scope = "L%d_%s" % (l, ph.__name__)
                ph(C, l, last)
                if stop_after == (ph.__name__, l):
                    done = True
                    break
            if done:
                break
        C.S.barrier()
        C.S.emit(es)
        import os
        if os.environ.get("KSTATS"):
            print("STATS", C.S.stats)
    return nc


def dbg_specs(dbg):
    specs = {
        "d_h": ([128, 8, TOK], BF16),
        "d_mod": ([2, 6 * D_MODEL], F32),
        "d_cols": ([128, 48, 2], F32),
        "d_utm": ([128, NT, 512], BF16),
        "d_p5": ([512, TOK], BF16),
        "d_x0": ([512, TOK], BF16),
        "d_scm": ([512, TOK], BF16),
        "d_zx": ([512, TOK], BF16),
        "d_ys": ([512, TOK], BF16),
        "d_xs": ([TOK, D_MODEL], F32),
        "d_filt": ([128, 32, 2, 512], BF16),
        "d_misc": ([128, 4, 512], F32),
        "d_s5": ([128, 6, TOK], F32),
        "d_s5p": ([128, 8, 32], F32),
    }
    return [(k, specs[k][0], specs[k][1]) for k in dbg]


def input_specs():
    L = DEPTH
    return [
        ("x_in", [SEQ, D_MODEL], F32), ("pos", [SEQ, D_MODEL], F32), ("ctx_in", [CTX, D_MODEL], F32),
        ("cc", [128, 8, 2], F32), ("ada_w", [L, D_MODEL, 6 * D_MODEL], F32), ("adab2", [L, 2, 6 * D_MODEL], F32),
        ("ncol", [128, L, 2, 8], F32), ("finalg_bc", [128, D_MODEL], F32),
        ("w_in", [L, D_MODEL, D_IN], F32), ("glu_w", [L, 512, 2048], F32), ("sc_out_w", [L, 512, 1024], F32),
        ("hy_out_w", [L, 512, 1024], F32), ("out_w", [L, 1024, 1024], F32),
        ("mlp_w1", [L, 1024, D_FF], F32), ("mlp_w2", [L, D_FF, 1024], F32),
        ("scw", [128, L, 4, 3], F32), ("hcw", [128, L, 12, 3], F32), ("hcb", [128, L, 12], F32),
        ("s5a", [128, L, 3, 32], F32),
        ("s5b", [128, L, 2, 32, 16], F32),
        ("s5c", [128, L, 2, 32, 16], F32),
        ("s5d", [128, L, 4], F32),
        ("hyw1", [L, 33, 64], F32), ("hyw2", [L, 64, 64], F32), ("hyw3", [L, 64, 64], F32),
        ("hyw4", [L, 64, 1024], F32), ("hyfb", [64, L, 4], F32),
        ("hyd_bc", [128, L, 512], F32),
        ("ident_bf", [128, 128], BF16), ("ident_f", [128, 128], F32), ("ones_bf", [128, 128], BF16),
        ("alt_bf", [128, 128], BF16), ("altcol", [128, 1], F32), ("sel2", [2, 2, 128], F32),
        ("dftc", [32, 128, 32, 128], BF16), ("dfts", [32, 128, 32, 128], BF16),
        ("dftc256", [2, 128, 2, 128], BF16), ("dfts256", [2, 128, 2, 128], BF16),
        ("zT", [33, SEQ], F32), ("zT256", [33, CTX], F32),
        ("win", [SEQ, 512], F32), ("win256", [CTX, 512], F32),
        ("wf", [128, 32], F32), ("wf256", [128, 2], F32),
        ("kpos", [128, TOK], F32),
        ("sel8", [128, 8, 8, 128], BF16), ("selT8", [128, 8, 8, 128], BF16), ("cmask", [128, 2, 128], F32),
    ]


def phase0(C):
    D = C.D
    C.bxs = [Buf("xs%d" % i) for i in range(NT)]
    C.bmods = Buf("mods")
    cc = C.sb("cc", [128, 8, 2])
    C.dma(cc[:], D["cc"][:, :, :], writes=[cc])
    C.op("scalar", "activation", reads=[cc], writes=[C.scv], out=C.scv[:], in_=cc[:], func=AF.Silu)
    for i in range(2):
        C.dma(D["xs"][i * 128:(i + 1) * 128, :], D["ctx_in"][i * 128:(i + 1) * 128, :], writes=[C.bxs[i]], key="xsst%d" % (i % 2), eng="scalar")
    xt = [C.sb("p0x%d" % i, [128, 1024]) for i in range(2)]
    pt = [C.sb("p0p%d" % i, [128, 1024]) for i in range(2)]
    for i in range(2, NT):
        a, b = xt[i % 2], pt[i % 2]
        r0 = (i - 2) * 128
        C.dma(a[:], D["x_in"][r0:r0 + 128, :], writes=[a])
        C.dma(b[:], D["pos"][r0:r0 + 128, :], writes=[b])
        C.op("vector", "tensor_tensor", reads=[a, b], writes=[a], out=a[:], in0=a[:], in1=b[:], op=ALU.add)
        C.dma(D["xs"][i * 128:(i + 1) * 128, :], a[:], reads=[a], writes=[C.bxs[i]], eng="scalar")


def phaseP(C, l, last):
    D = C.D
    adab = C.sb("adab", [2, 6144])
    modrow = C.sb("modrow", [2, 6144])
    C.dma(adab[:], D["adab2"][l], writes=[adab])
    wst = [C.sb("adaw%d" % i, [128, 8, 512]) for i in range(2)]
    wsrc = D["ada_w"][l].rearrange("(k p) c -> p k c", p=128)
    for j in range(12):
        w = wst[j % 2]
        C.dma(w[:], wsrc[:, :, j * 512:(j + 1) * 512], writes=[w])
        ps = C.psum[j % 2]
        for k in range(8):
            C.mm(ps, ps[0:2, :], lhsT=C.scv[:, k, :], rhs=w[:, k, :], reads=[C.scv, w], start=(k == 0), stop=(k == 7))
        C.op("vector", "tensor_tensor", reads=[ps, adab], writes=[modrow], out=modrow[:, j * 512:(j + 1) * 512],
             in0=ps[0:2, :], in1=adab[:, j * 512:(j + 1) * 512], op=ALU.add)
    C.dma(D["mods"][l], modrow[:], reads=[modrow], writes=[C.bmods])
    if "d_mod" in C.dbg and l == 0:
        C.dma(D["d_mod"][:, :], modrow[:], reads=[modrow], key="dbg")
    ps = C.psum[2]
    for c in range(48):
        C.mm(ps, ps[:, 2 * c:2 * c + 2], lhsT=modrow[:, c * 128:(c + 1) * 128], rhs=C.ident_f[0:2, 0:2],
             reads=[modrow, C.ident_f], start=True, stop=True)
    C.op("vector", "tensor_copy", reads=[ps], writes=[C.cols], out=C.cols[:].rearrange("p c r -> p (c r)"), in_=ps[:, 0:96])
    for r in range(2):
        C.op("vector", "scalar_tensor_tensor", reads=[C.cols, C.ncol], writes=[C.scale1], out=C.scale1[:, :, r],
             in0=C.cols[:, 8:16, r], scalar=1.0, in1=C.ncol[:, l, 0, :], op0=ALU.add, op1=ALU.mult)
        C.op("vector", "scalar_tensor_tensor", reads=[C.cols, C.ncol], writes=[C.scale2], out=C.scale2[:, :, r],
             in0=C.cols[:, 32:40, r], scalar=1.0, in1=C.ncol[:, l, 1, :], op0=ALU.add, op1=ALU.mult)
    if "d_cols" in C.dbg and l == 0:
        C.dma(D["d_cols"][:, :, :], C.cols[:], reads=[C.cols], key="dbg")


def make_norm_scratch(C, n=2):
    st = []
    for i in range(n):
        st.append(dict(ss=C.sb("nss%d" % i, [128, 1]), rs=C.sb("nrs%d" % i, [128, 1]), xn=C.sb("nxn%d" % i, [128, 1024], BF16)))
    return st


def norm_part1(C, which, xt, st, ps):
    C.op("scalar", "activation", reads=[xt], writes=[st["xn"], st["ss"]], out=st["xn"][:], in_=xt[:],
         func=AF.Square, accum_out=st["ss"][:])
    C.op("vector", "tensor_scalar", reads=[st["ss"]], writes=[st["rs"]], out=st["rs"][:], in0=st["ss"][:],
         scalar1=1.0 / D_MODEL, scalar2=EPS, op0=ALU.mult, op1=ALU.add)
    C.op("scalar", "activation", reads=[st["rs"]], writes=[st["rs"]], out=st["rs"][:], in_=st["rs"][:], func=AF.Sqrt)
    C.op("vector", "reciprocal", reads=[st["rs"]], writes=[st["rs"]], out=st["rs"][:], in_=st["rs"][:])
    C.op("vector", "tensor_scalar", reads=[xt, st["rs"]], writes=[st["xn"]], out=st["xn"][:], in0=xt[:],
         scalar1=st["rs"][:, 0:1], scalar2=None, op0=ALU.mult)
    pv = ps.ap.bitcast(BF16).rearrange("p (k t) -> p k t", k=8)
    for k in range(8):
        C.op("tensor", "transpose", reads=[st["xn"], C.ident_bf], writes=[ps], out=pv[:, k, :],
             in_=st["xn"][:, k * 128:(k + 1) * 128], identity=C.ident_bf[:])


def norm_part2(C, which, r, hT, hcol, ps):
    scale = C.scale1 if which == 0 else C.scale2
    shj = 0 if which == 0 else 3
    pv = ps.ap.bitcast(BF16).rearrange("p (k t) -> p k t", k=8)
    for k in range(8):
        if k % 2 == 0:
            C.op("vector", "tensor_scalar", reads=[ps, scale, C.cols], writes=[hT], out=hT[:, k, hcol:hcol + 128],
                 in0=pv[:, k, :], scalar1=scale[:, k, r:r + 1], scalar2=C.cols[:, shj * 8 + k, r:r + 1],
                 op0=ALU.mult, op1=ALU.add)
        else:
            C.op("scalar", "activation", reads=[ps, scale, C.cols], writes=[hT], out=hT[:, k, hcol:hcol + 128],
                 in_=pv[:, k, :], func=AF.Identity, bias=C.cols[:, shj * 8 + k, r:r + 1], scale=scale[:, k, r:r + 1])


def norm_tiles(C, which, items, sts, pss):
    n = len(items)
    for idx in range(n + 1):
        if idx < n:
            r, xt, hT, hcol = items[idx]
            norm_part1(C, which, xt, sts[idx % 2], pss[idx % 2])
        if idx >= 1:
            r, xt, hT, hcol = items[idx - 1]
            norm_part2(C, which, r, hT, hcol, pss[(idx - 1) % 2])


def norm_tile(C, which, r, xt, hT, hcol, st, ps):
    norm_part1(C, which, xt, st, ps)
    norm_part2(C, which, r, hT, hcol, ps)


def conv3(C, eng, out_t, out_ap, in_t, in_ap_fn, wcol, bias=None, n=PFW - 2):
    C.op("scalar", "activation", reads=[in_t], writes=[out_t], out=out_ap, in_=in_ap_fn(1), func=AF.Identity,
         bias=(bias if bias is not None else 0.0), scale=wcol[:, 1:2])
    for s in (0, 2):
        C.op(eng, "scalar_tensor_tensor", reads=[in_t, out_t], writes=[out_t], out=out_ap, in0=in_ap_fn(s),
             scalar=wcol[:, s:s + 1], in1=out_ap, op0=ALU.mult, op1=ALU.add)


def phaseA(C, l, last):
    D = C.D
    C.utm_off = C.top
    utm = C.utm = C.sb("utm", [128, NT, 512], BF16)
    hT = C.sb("hT", [128, 8, TOK], BF16)
    nsc = make_norm_scratch(C)
    xt = [C.sb("ax%d" % i, [128, 1024]) for i in range(2)]
    hbufs = [Buf("hTg%d" % g) for g in range(len(GROUPS))]
    items = []
    for i in range(NT):
        hTg = T(hT.ap, "hTg")
        hTg.b = hbufs[0 if i < 2 else 1 + (i - 2) // 4]
        items.append((1 if i < 2 else 0, xt[i % 2], hTg, i * 128))
    for idx in range(NT + 1):
        if idx < NT:
            a = xt[idx % 2]
            C.dma(a[:], D["xs"][idx * 128:(idx + 1) * 128, :], reads=[C.bxs[idx]], writes=[a])
            norm_part1(C, 0, a, nsc[idx % 2], C.psum[idx % 2])
        if idx >= 1:
            r_, a_, hTg_, hc_ = items[idx - 1]
            norm_part2(C, 0, r_, hTg_, hc_, C.psum[(idx - 1) % 2])
    if "d_h" in C.dbg and l == 0:
        C.dma(D["d_h"][:, :, :], hT[:], reads=hbufs, key="dbg")
    C.S.cur_scope = C.S.cur_scope.split("/")[0] + "/A2"
    wst = [C.sb("awst%d" % i, [128, 8, 128]) for i in range(2)]
    wb = [C.sb("awb%d" % i, [128, 8, 128], BF16) for i in range(2)]
    pfs = [C.sb("pf%d" % i, [128, PFW]) for i in range(2)]
    qv = C.sb("qv", [128, PFW])
    q = C.sb("q", [128, PFW - 2])
    o16 = C.sb("o16", [128, PFW - 2], BF16)
    cw = C.sb("cw", [128, 16, 3])
    cb = C.sb("cb", [128, 12])
    C.dma(cw[:, 0:4, :], D["scw"][:, l, :, :], writes=[cw])
    C.dma(cw[:, 4:16, :], D["hcw"][:, l, :, :], writes=[cw])
    C.dma(cb[:], D["hcb"][:, l, :], writes=[cb])
    for pf in pfs:
        C.op("gpsimd", "memset", writes=[pf], ap=pf[:], constant=0.0)
    C.op("gpsimd", "memset", writes=[qv], ap=qv[:], constant=0.0)
    tiles = [("s5", j, j * 128) for j in range(4)]
    for j in range(4):
        tiles += [("sc_c", j, OFF_SC + 1024 + j * 128), ("sc_x", j, OFF_SC + j * 128), ("sc_b", j, OFF_SC + 512 + j * 128)]
    for j in range(4):
        tiles += [("hy_v", j, OFF_HY + j * 128), ("hy_x1", j, OFF_HY + 512 + j * 128), ("hy_x0", j, OFF_HY + 1024 + j * 128)]
    wsrc = D["w_in"][l].rearrange("(k p) c -> p k c", p=128)
    nmm = 0
    pending = []

    class Rec:
        def op(self, *a, **kw):
            pending.append(lambda: C.op(*a, **kw))
        def dma(self, *a, **kw):
            pending.append(lambda: C.dma(*a, **kw))
    R = Rec()

    def rconv3(out_t, out_ap, in_t, in_ap_fn, wcol, bias=None):
        R.op("scalar", "activation", reads=[in_t, cw, cb], writes=[out_t], out=out_ap, in_=in_ap_fn(1), func=AF.Identity,
             bias=(bias if bias is not None else 0.0), scale=wcol[:, 1:2])
        for s_ in (0, 2):
            R.op("vector", "scalar_tensor_tensor", reads=[in_t, out_t, cw], writes=[out_t], out=out_ap, in0=in_ap_fn(s_),
                 scalar=wcol[:, s_:s_ + 1], in1=out_ap, op0=ALU.mult, op1=ALU.add)

    def flush(nmax):
        for _ in range(min(nmax, len(pending))):
            pending.pop(0)()

    for ti, (kind, j, col0) in enumerate(tiles):
        ws, w = wst[ti % 2], wb[ti % 2]
        C.dma(ws[:], wsrc[:, :, col0:col0 + 128], writes=[ws])
        C.op("gpsimd", "tensor_copy", reads=[ws], writes=[w], out=w[:], in_=ws[:])
        pf = pfs[ti % 2]
        dest = {"s5": o16, "sc_c": qv}.get(kind, pf)
        if kind in ("s5", "sc_c"):
            flush(len(pending))
        per = (len(pending) + len(GROUPS) - 1) // len(GROUPS)
        for gi, (g0, gn) in enumerate(GROUPS):
            ps = C.psum[2 + nmm % 4]
            nmm += 1
            for k in range(8):
                C.mm(ps, ps[:, 0:gn], lhsT=w[:, k, :], rhs=hT[:, k, g0:g0 + gn], reads=[w, hbufs[gi]], start=(k == 0), stop=(k == 7))
            if kind == "s5":
                c0, ncq = g0 // 8, gn // 8
                o_ap = dest[:, 0:TOK].rearrange("p (t c) -> p t c", t=8)[:, :, c0:c0 + ncq]
                i_ap = ps[:, 0:gn].rearrange("p (c t) -> p t c", t=8)
            else:
                dcol = pf_off(g0)
                o_ap = dest[:, dcol:dcol + gn]
                i_ap = ps[:, 0:gn]
            C.op("scalar", "copy", reads=[ps], writes=[dest], out=o_ap, in_=i_ap)
            flush(per)
        flush(len(pending))
        W = PFW - 2
        if kind == "s5":
            R.dma(D["p5"][j * 128:(j + 1) * 128, :], o16[:, 0:TOK], reads=[o16])
        elif kind == "sc_x":
            R.op("vector", "tensor_tensor", reads=[pf, qv], writes=[qv], out=qv[:], in0=pf[:], in1=qv[:], op=ALU.mult)
            rconv3(q, q[:, 0:W], qv, lambda s_: qv[:, s_:s_ + W], cw[:, j, :])
        elif kind == "sc_b":
            R.op("vector", "tensor_tensor", reads=[pf, q], writes=[o16], out=o16[:, 0:W], in0=q[:, 0:W], in1=pf[:, 1:1 + W], op=ALU.mult)
            R.dma(D["scm"][j * 128:(j + 1) * 128, 0:256], o16[:, 0:256], reads=[o16])
            R.dma(D["scm"][j * 128:(j + 1) * 128, 256:TOK], o16[:, 258:258 + SEQ], reads=[o16])
        elif kind == "hy_v":
            rconv3(qv, qv[:, 0:W], pf, lambda s_, pf=pf: pf[:, s_:s_ + W], cw[:, 4 + j, :], bias=cb[:, j:j + 1])
        elif kind == "hy_x1":
            rconv3(q, q[:, 0:W], pf, lambda s_, pf=pf: pf[:, s_:s_ + W], cw[:, 8 + j, :], bias=cb[:, 4 + j:5 + j])
            R.op("vector", "tensor_tensor", reads=[q, qv], writes=[o16], out=o16[:, 0:W], in0=q[:, 0:W], in1=qv[:, 0:W], op=ALU.mult)
            for i0_ in range(0, NT, 8):
                n = min(8, NT - i0_)
                ps = C.psum[6 + (i0_ // 8) % 2]
                pv = ps.ap.bitcast(BF16).rearrange("p (k t) -> p k t", k=8)
                for ii in range(n):
                    oc = ctile_cols(i0_ + ii) - 1
                    R.op("tensor", "transpose", reads=[o16, C.ident_bf], writes=[ps], out=pv[:, ii, :],
                         in_=o16[:, oc:oc + 128], identity=C.ident_bf[:])
                R.op("scalar", "copy", reads=[ps], writes=[utm], out=utm[:, i0_:i0_ + n, j * 128:(j + 1) * 128], in_=pv[:, 0:n, :])
        elif kind == "hy_x0":
            rconv3(q, q[:, 0:W], pf, lambda s_, pf=pf: pf[:, s_:s_ + W], cw[:, 12 + j, :], bias=cb[:, 8 + j:9 + j])
            R.op("gpsimd", "tensor_copy", reads=[q], writes=[o16], out=o16[:, 0:W], in_=q[:, 0:W])
            R.dma(D["x0s"][j * 128:(j + 1) * 128, 0:256], o16[:, 0:256], reads=[o16])
            R.dma(D["x0s"][j * 128:(j + 1) * 128, 256:TOK], o16[:, 258:258 + SEQ], reads=[o16])
    flush(len(pending))
    if l == 0:
        for nm, src in (("d_p5", "p5"), ("d_x0", "x0s"), ("d_scm", "scm")):
            if nm in C.dbg:
                C.S.barrier()
                C.dma(D[nm][:, :], D[src][:, :], key="dbg")
        if "d_utm" in C.dbg:
            C.dma(D["d_utm"][:, :, :], utm[:], reads=[utm], key="dbg")


def reduce_sin(C, eng, out_t, out_ap, in_t, in_ap, tmp_t, tmp_ap, ki_t, ki_ap, mul=None, add=None, reads=()):
    src_t, src_ap = in_t, in_ap
    if mul is not None or add is not None:
        C.op(eng, "tensor_scalar", reads=[in_t] + list(reads), writes=[tmp_t], out=tmp_ap, in0=in_ap,
             scalar1=(1.0 if mul is None else mul), scalar2=(0.0 if add is None else add), op0=ALU.mult, op1=ALU.add)
        src_t, src_ap = tmp_t, tmp_ap
    C.op(eng, "tensor_scalar", reads=[src_t], writes=[ki_t], out=ki_ap, in0=src_ap, scalar1=1.0 / TWO_PI, scalar2=None, op0=ALU.mult)
    C.op(eng, "scalar_tensor_tensor", reads=[ki_t, src_t], writes=[tmp_t], out=tmp_ap, in0=ki_ap, scalar=-TWO_PI, in1=src_ap,
         op0=ALU.mult, op1=ALU.add)
    C.op("scalar", "activation", reads=[tmp_t], writes=[out_t], out=out_ap, in_=tmp_ap, func=AF.Sin)


def hyena_seq(C, l, L, tile0, sfx, base_top):
    D = C.D
    nt = L // 128
    N2 = 2 * L
    C.top = base_top
    regy = C.top
    fs = C.sb("fs", [128, nt, 512], BF16)
    fd = C.sb("fd", [128, nt, 512], BF16)
    end_y = C.top
    C.top = regy
    yab = C.sb("hyab", [128, nt, 2, 512], BF16)
    C.top = max(C.top, end_y)
    regd = C.top
    dC = [C.sb("dC%d" % i, [128, nt, 128], BF16) for i in range(2)]
    dS = [C.sb("dS%d" % i, [128, nt, 128], BF16) for i in range(2)]
    end_d = C.top
    C.top = regd
    h3 = C.sb("hh3", [64, L])
    C.top = max(C.top, end_d)
    sm = {n: C.sb("h" + n, [128, 512]) for n in ("hf", "hb", "kc", "ks", "t1", "t2", "t3", "t4", "rn", "kl", "yl", "hyd", "w0", "w1")}
    afs = [C.sb("haf%d" % i, [128, 512], BF16) for i in range(2)]; abs_ = [C.sb("hab%d" % i, [128, 512], BF16) for i in range(2)]
    hfs = [C.sb("hhf%d" % i, [128, 512]) for i in range(2)]; hbs = [C.sb("hhb%d" % i, [128, 512]) for i in range(2)]
    yst = [C.sb("yst%d" % i, [128, 2, 512], BF16) for i in range(2)]
    zb = C.sb("hz", [128, 512], BF16)
    x0t = [C.sb("hx0%d" % i, [128, 4, 128], BF16) for i in range(2)]
    zxt = [C.sb("hzx%d" % i, [128, 4, 128], BF16) for i in range(2)]
    w1 = C.sb("hw1", [33, 64]); w2 = C.sb("hw2", [64, 64]); w3 = C.sb("hw3", [64, 64]); w4 = C.sb("hw4", [64, 1024])
    fb = C.sb("hfb", [64, 4]); fbias = C.sb("hfbias", [64, 3]); wfc = C.sb("hwf", [128, nt])
    tmpa = C.sb("htmpa", [64, 512]); kia = C.sb("hkia", [64, 512], I32)
    zc = [C.sb("hzc%d" % i, [33, 512]) for i in range(2)]
    ha = C.sb("hha", [64, 512]); hb_ = C.sb("hhb", [64, 512])
    C.dma(w1[:], D["hyw1"][l], writes=[w1]); C.dma(w2[:], D["hyw2"][l], writes=[w2]); C.dma(w3[:], D["hyw3"][l], writes=[w3])
    C.dma(w4[:], D["hyw4"][l], writes=[w4]); C.dma(fb[:], D["hyfb"][:, l, :], writes=[fb]); C.dma(wfc[:], D["wf" + sfx][:, :], writes=[wfc])
    C.dma(sm["hyd"][:], D["hyd_bc"][:, l, :], writes=[sm["hyd"]])
    for i in range(3):
        C.op("vector", "tensor_tensor", reads=[fb], writes=[fbias], out=fbias[:, i:i + 1], in0=fb[:, 0:1], in1=fb[:, i + 1:i + 2], op=ALU.mult)
    C.S.cur_scope = C.S.cur_scope.split("/")[0] + "/H1a"
    CH = min(512, L)
    nps = 0
    for ci, c0 in enumerate(range(0, L, CH)):
        z = zc[ci % 2]
        C.dma(z[:, 0:CH], D["zT" + sfx][:, c0:c0 + CH], writes=[z])
        for (w, src_t, src_ap, dst_t, dst_ap, bi) in ((w1, z, z[:, 0:CH], ha, ha[:, 0:CH], 0), (w2, ha, ha[:, 0:CH], hb_, hb_[:, 0:CH], 1),
                                                     (w3, hb_, hb_[:, 0:CH], h3, h3[:, c0:c0 + CH], 2)):
            ps = C.psum[nps % 2]; nps += 1
            C.mm(ps, ps[0:64, 0:CH], lhsT=w[:], rhs=src_ap, reads=[w, src_t], start=True, stop=True)
            reduce_sin(C, "vector", dst_t, dst_ap, ps, ps[0:64, 0:CH], tmpa, tmpa[:, 0:CH], kia, kia[:, 0:CH],
                       mul=fb[:, 0:1], add=fbias[:, bi:bi + 1], reads=[fb, fbias])
    C.S.cur_scope = C.S.cur_scope.split("/")[0] + "/H1b"
    pN, pK = C.psum[6], C.psum[7]
    for tt in range(nt):
        pa, pb = C.psum[2 + (tt % 2) * 2], C.psum[3 + (tt % 2) * 2]
        wt = sm["w%d" % (tt % 2)]
        af, ab = afs[tt % 2], abs_[tt % 2]
        sm["hf"], sm["hb"] = hfs[tt % 2], hbs[tt % 2]
        C.dma(wt[:], D["win" + sfx][tt * 128:(tt + 1) * 128, :], writes=[wt])
        C.mm(pa, pa[:, :], lhsT=h3[:, tt * 128:(tt + 1) * 128], rhs=w4[:, 0:512], reads=[h3, w4], start=True, stop=True)
        C.mm(pb, pb[:, :], lhsT=h3[:, tt * 128:(tt + 1) * 128], rhs=w4[:, 512:1024], reads=[h3, w4], start=True, stop=True)
        C.op("vector", "tensor_tensor", reads=[pa, wt], writes=[sm["hf"]], out=sm["hf"][:], in0=pa[:, :], in1=wt[:], op=ALU.mult)
        C.op("vector", "tensor_tensor", reads=[pb, wt], writes=[sm["hb"]], out=sm["hb"][:], in0=pb[:, :], in1=wt[:], op=ALU.mult)
        if tt == 0:
            C.op("vector", "memset", writes=[sm["hb"]], ap=sm["hb"][0:1, :], constant=0.0)
        C.op("gpsimd", "tensor_tensor", reads=[sm["hf"], sm["hb"]], writes=[fs], out=fs[:, tt, :], in0=sm["hf"][:], in1=sm["hb"][:], op=ALU.add)
        C.op("gpsimd", "tensor_tensor", reads=[sm["hf"], sm["hb"]], writes=[fd], out=fd[:, tt, :], in0=sm["hf"][:], in1=sm["hb"][:], op=ALU.subtract)
        C.op("scalar", "activation", reads=[sm["hf"]], writes=[af], out=af[:], in_=sm["hf"][:], func=AF.Abs)
        C.op("scalar", "activation", reads=[sm["hb"]], writes=[ab], out=ab[:], in_=sm["hb"][:], func=AF.Abs)
        C.mm(pN, pN[:, :], lhsT=C.ones_bf[:], rhs=af[:], reads=[C.ones_bf, af], start=(tt == 0), stop=False)
        C.mm(pN, pN[:, :], lhsT=C.ones_bf[:], rhs=ab[:], reads=[C.ones_bf, ab], start=False, stop=(tt == nt - 1))
        C.mm(pK, pK[:, :], lhsT=C.alt_bf[:], rhs=fs[:, tt, :], reads=[C.alt_bf, fs], start=(tt == 0), stop=(tt == nt - 1))
    C.op("vector", "reciprocal", reads=[pN], writes=[sm["rn"]], out=sm["rn"][:], in_=pN[:, :])
    C.op("scalar", "copy", reads=[pK], writes=[sm["kl"]], out=sm["kl"][:], in_=pK[:, :])
    pU = C.psum[6]
    for ch in range(nt):
        C.mm(pU, pU[:, :], lhsT=C.alt_bf[:], rhs=C.utm[:, tile0 + ch, :], reads=[C.alt_bf, C.utm], start=(ch == 0), stop=(ch == nt - 1))
    C.op("vector", "tensor_tensor", reads=[pU, sm["kl"]], writes=[sm["yl"]], out=sm["yl"][:], in0=pU[:, :], in1=sm["kl"][:], op=ALU.mult)
    C.op("vector", "scalar_tensor_tensor", reads=[sm["yl"], sm["rn"]], writes=[sm["yl"]], out=sm["yl"][:], in0=sm["yl"][:], scalar=1.0 / N2,
         in1=sm["rn"][:], op0=ALU.mult, op1=ALU.mult)
    C.S.cur_scope = C.S.cur_scope.split("/")[0] + "/Hf"
    byab = [Buf("yabd%d" % i) for i in range(nt)]
    for ft in range(nt):
        ct, st = dC[ft % 2], dS[ft % 2]
        C.dma(ct[:], D["dftc" + sfx][ft], writes=([ct, h3] if ft < 2 else [ct]))
        C.dma(st[:], D["dfts" + sfx][ft], writes=[st])
        pb4 = [C.psum[(ft % 2) * 4 + i] for i in range(4)]
        for bi, (mat, rhs_t, rhs_fn) in enumerate(((ct, C.utm, lambda ch: C.utm[:, tile0 + ch, :]), (st, C.utm, lambda ch: C.utm[:, tile0 + ch, :]),
                                                   (ct, fs, lambda ch: fs[:, ch, :]), (st, fd, lambda ch: fd[:, ch, :]))):
            for ch in range(nt):
                C.mm(pb4[bi], pb4[bi][:, :], lhsT=mat[:, ch, :], rhs=rhs_fn(ch), reads=[mat, rhs_t], start=(ch == 0), stop=(ch == nt - 1))
        pUr, pUs, pKc, pKs = pb4
        C.op("scalar", "copy", reads=[pKc], writes=[sm["kc"]], out=sm["kc"][:], in_=pKc[:, :])
        C.op("scalar", "copy", reads=[pKs], writes=[sm["ks"]], out=sm["ks"][:], in_=pKs[:, :])
        C.op("vector", "tensor_tensor", reads=[pUr, sm["kc"]], writes=[sm["t1"]], out=sm["t1"][:], in0=pUr[:, :], in1=sm["kc"][:], op=ALU.mult)
        C.op("vector", "tensor_tensor", reads=[pUs, sm["ks"]], writes=[sm["t2"]], out=sm["t2"][:], in0=pUs[:, :], in1=sm["ks"][:], op=ALU.mult)
        C.op("vector", "tensor_tensor", reads=[pUr, sm["ks"]], writes=[sm["t3"]], out=sm["t3"][:], in0=pUr[:, :], in1=sm["ks"][:], op=ALU.mult)
        C.op("vector", "tensor_tensor", reads=[pUs, sm["kc"]], writes=[sm["t4"]], out=sm["t4"][:], in0=pUs[:, :], in1=sm["kc"][:], op=ALU.mult)
        C.op("gpsimd", "tensor_tensor", reads=[sm["t1"], sm["t2"]], writes=[sm["t1"]], out=sm["t1"][:], in0=sm["t1"][:], in1=sm["t2"][:], op=ALU.subtract)
        C.op("gpsimd", "tensor_tensor", reads=[sm["t3"], sm["t4"]], writes=[sm["t3"]], out=sm["t3"][:], in0=sm["t3"][:], in1=sm["t4"][:], op=ALU.add)
        ys = yst[ft % 2]
        for ri, tn in ((0, "t1"), (1, "t3")):
            C.op("vector", "scalar_tensor_tensor", reads=[sm[tn], sm["rn"], wfc], writes=[ys], out=ys[:, ri, :], in0=sm[tn][:], scalar=wfc[:, ft:ft + 1],
                 in1=sm["rn"][:], op0=ALU.mult, op1=ALU.mult)
        C.dma(D["yab"][ft], ys[:], reads=[ys], writes=[byab[ft]], eng="scalar")
    C.S.cur_scope = C.S.cur_scope.split("/")[0] + "/H4"
    C.dma(yab[:], D["yab"][0:nt].rearrange("f p r c -> p f r c"), reads=byab, writes=[yab, fs, fd])
    x0v = D["x0s"].rearrange("(j c) t -> c j t", c=128)
    zxv = D["zxs"].rearrange("(j c) t -> c j t", c=128)
    for tt in range(nt):
        ct, st = dC[tt % 2], dS[tt % 2]
        C.dma(ct[:], D["dftc" + sfx][tt], writes=[ct])
        C.dma(st[:], D["dfts" + sfx][tt], writes=[st])
        tok0 = (tile0 + tt) * 128
        xt = x0t[tt % 2]
        C.dma(xt[:], x0v[:, :, tok0:tok0 + 128], writes=[xt])
        acc = C.psum[tt % 2]
        for ch in range(nt):
            C.mm(acc, acc[:, :], lhsT=ct[:, ch, :], rhs=yab[:, ch, 0, :], reads=[ct, yab], start=(ch == 0), stop=False)
        for ch in range(nt):
            C.mm(acc, acc[:, :], lhsT=st[:, ch, :], rhs=yab[:, ch, 1, :], reads=[st, yab], start=False, stop=(ch == nt - 1))
        C.op("vector", "scalar_tensor_tensor", reads=[sm["yl"], C.altcol, acc], writes=[sm["hf"]], out=sm["hf"][:], in0=sm["yl"][:],
             scalar=C.altcol[:, 0:1], in1=acc[:, :], op0=ALU.mult, op1=ALU.add)
        C.op("gpsimd", "tensor_tensor", reads=[C.utm, sm["hyd"]], writes=[sm["hb"]], out=sm["hb"][:], in0=C.utm[:, tile0 + tt, :], in1=sm["hyd"][:], op=ALU.mult)
        C.op("gpsimd", "tensor_tensor", reads=[sm["hf"], sm["hb"]], writes=[zb], out=zb[:], in0=sm["hf"][:], in1=sm["hb"][:], op=ALU.add)
        pT = C.psum[2 + tt % 2]
        pv = pT.ap.bitcast(BF16).rearrange("p (k t) -> p k t", k=8)
        for j in range(4):
            C.op("tensor", "transpose", reads=[zb, C.ident_bf], writes=[pT], out=pv[:, j, :], in_=zb[:, j * 128:(j + 1) * 128], identity=C.ident_bf[:])
        zx = zxt[tt % 2]
        C.op("vector", "tensor_tensor", reads=[pT, xt], writes=[zx], out=zx[:], in0=pv[:, 0:4, :], in1=xt[:], op=ALU.mult)
        C.dma(zxv[:, :, tok0:tok0 + 128], zx[:], reads=[zx], eng="scalar")
    if "d_filt" in C.dbg and l == 0 and sfx == "":
        C.dma(D["d_filt"][:, :, 0, :], fs[:], reads=[fs], key="dbg")
        C.dma(D["d_filt"][:, :, 1, :], fd[:], reads=[fd], key="dbg")
    if "d_misc" in C.dbg and l == 0 and sfx == "":
        for i, n in enumerate(("rn", "kl", "yl", "hyd")):
            C.dma(D["d_misc"][:, i, :], sm[n][:], reads=[sm[n]], key="dbg")


def phaseH(C, l, last):
    D = C.D
    assert C.top == C.utm_off
    C.top += NT * 512 * 2
    base_top = C.top
    hyena_seq(C, l, SEQ, 2, "", base_top)
    if not last:
        C.S.barrier()
        hyena_seq(C, l, CTX, 0, "256", base_top)
    if "d_zx" in C.dbg and l == 0:
        C.S.barrier()
        C.dma(D["d_zx"][:, :], D["zxs"][:, :], key="dbg")


def cplx_outer(C, eng, out_t, out_r, out_i, Pr, Pi, Mr, Mi, reads, tmp, neg_i=False):
    sh = [128, 8, 16]
    pr, pi_ = Pr.unsqueeze(2).to_broadcast(sh), Pi.unsqueeze(2).to_broadcast(sh)
    mr, mi = Mr.unsqueeze(1).to_broadcast(sh), Mi.unsqueeze(1).to_broadcast(sh)
    ta, tb = tmp
    va = ta[:].rearrange("q (t p) -> q t p", t=8); vb = tb[:].rearrange("q (t p) -> q t p", t=8)
    orr = out_r.rearrange("q (t p) -> q t p", t=8); oi = out_i.rearrange("q (t p) -> q t p", t=8)
    C.op(eng, "tensor_tensor", reads=reads, writes=[ta], out=va, in0=pr, in1=mr, op=ALU.mult)
    C.op(eng, "tensor_tensor", reads=reads, writes=[tb], out=vb, in0=pi_, in1=mi, op=ALU.mult)
    C.op(eng, "tensor_tensor", reads=[ta, tb], writes=[out_t], out=orr, in0=va, in1=vb, op=ALU.subtract)
    C.op(eng, "tensor_tensor", reads=reads, writes=[ta], out=va, in0=pr, in1=mi, op=ALU.mult)
    C.op(eng, "tensor_tensor", reads=reads, writes=[tb], out=vb, in0=pi_, in1=mr, op=ALU.mult)
    if neg_i:
        C.op(eng, "tensor_scalar", reads=[ta], writes=[ta], out=va, in0=va, scalar1=-1.0, scalar2=None, op0=ALU.mult)
        C.op(eng, "tensor_tensor", reads=[ta, tb], writes=[out_t], out=oi, in0=va, in1=vb, op=ALU.subtract)
    else:
        C.op(eng, "tensor_tensor", reads=[ta, tb], writes=[out_t], out=oi, in0=va, in1=vb, op=ALU.add)


def phaseS(C, l, last):
    D = C.D
    NCH = TOK // 8
    HC = NCH // 2
    p5 = C.sb("s5p", [128, 4, TOK], BF16)
    C.dma(p5[:], D["p5"].rearrange("(j c) t -> c j t", c=128), writes=[p5])
    X = C.sb("s5X", [128, 32, NCH], BF16)
    Xb = [Buf("s5X%d" % g) for g in range(32)]
    sel = C.sb("s5sel", [128, 8, 8, 128], BF16)
    selT = C.sb("s5selT", [128, 8, 8, 128], BF16)
    C.dma(sel[:], D["sel8"][:, :, :, :], writes=[sel])
    C.dma(selT[:], D["selT8"][:, :, :, :], writes=[selT])
    masks = C.sb("s5mask", [128, 2, 128])
    C.dma(masks[:], D["cmask"][:, :, :], writes=[masks])
    kp = C.sb("s5kp", [128, NCH])
    C.dma(kp[:], D["kpos"][:, 0:NCH], writes=[kp])
    ysb = C.sb("s5ysb", [128, TOK], BF16)
    pa = C.sb("s5a_", [128, 3, 32]); pb = C.sb("s5b_", [128, 2, 32, 16]); pc = C.sb("s5c_", [128, 2, 32, 16]); dcol = C.sb("s5d_", [128, 4])
    C.dma(pa[:], D["s5a"][:, l], writes=[pa]); C.dma(pb[:], D["s5b"][:, l], writes=[pb]); C.dma(pc[:], D["s5c"][:, l], writes=[pc])
    C.dma(dcol[:], D["s5d"][:, l, :], writes=[dcol])
    sc = {n: C.sb("s5s" + n, [128, 32]) for n in ("dt", "ard", "ang", "mag", "sn", "cs", "abr", "abi", "den", "t0", "t1", "fre", "fim", "nfim", "phi", "rho8")}
    ski = C.sb("s5ski", [128, 32], I32)
    Bb = C.sb("s5bb", [128, 2, 32, 16])
    apw = C.sb("s5apw", [128, 2, 32, 9]); apr = C.sb("s5apr", [128, 2, 32, 9]); ang_ = C.sb("s5ang", [128, 2, 32, 8])
    V = lambda name, **kw: C.op("vector", name, **kw)
    ps = C.psum
    PS = C.psum_all
    C.op("scalar", "activation", reads=[pa], writes=[sc["dt"]], out=sc["dt"][:], in_=pa[:, 2, :], func=AF.Exp)
    V("tensor_tensor", reads=[pa, sc["dt"]], writes=[sc["ard"]], out=sc["ard"][:], in0=pa[:, 0, :], in1=sc["dt"][:], op=ALU.mult)
    V("tensor_tensor", reads=[pa, sc["dt"]], writes=[sc["ang"]], out=sc["ang"][:], in0=pa[:, 1, :], in1=sc["dt"][:], op=ALU.mult)
    for j in range(9):
        for (tab, jj, sgn) in ((apw, j, 1.0), (apr, 8 - j, 1.0), (ang_, j, -1.0)):
            if tab is ang_ and j == 8:
                continue
            C.op("scalar", "activation", reads=[sc["ard"]], writes=[sc["mag"]], out=sc["mag"][:], in_=sc["ard"][:], func=AF.Exp, scale=sgn * jj)
            reduce_sin(C, "vector", sc["sn"], sc["sn"][:], sc["ang"], sc["ang"][:], sc["t0"], sc["t0"][:], ski, ski[:], mul=float(jj), add=0.0)
            reduce_sin(C, "vector", sc["cs"], sc["cs"][:], sc["ang"], sc["ang"][:], sc["t0"], sc["t0"][:], ski, ski[:], mul=float(jj), add=math.pi / 2)
            V("tensor_tensor", reads=[sc["mag"], sc["cs"]], writes=[tab], out=tab[:, 0, :, j], in0=sc["mag"][:], in1=sc["cs"][:], op=ALU.mult)
            V("scalar_tensor_tensor", reads=[sc["mag"], sc["sn"]], writes=[tab], out=tab[:, 1, :, j], in0=sc["mag"][:], scalar=sgn, in1=sc["sn"][:],
              op0=ALU.mult, op1=ALU.mult)
    V("tensor_copy", reads=[apw], writes=[sc["abr"]], out=sc["abr"][:], in_=apw[:, 0, :, 1])
    V("tensor_copy", reads=[apw], writes=[sc["abi"]], out=sc["abi"][:], in_=apw[:, 1, :, 1])
    V("tensor_scalar", reads=[sc["ang"]], writes=[sc["phi"]], out=sc["phi"][:], in0=sc["ang"][:], scalar1=8.0, scalar2=None, op0=ALU.mult)
    C.op("scalar", "activation", reads=[sc["ard"]], writes=[sc["rho8"]], out=sc["rho8"][:], in_=sc["ard"][:], func=AF.Exp, scale=8.0)
    V("tensor_tensor", reads=[pa], writes=[sc["den"]], out=sc["den"][:], in0=pa[:, 0, :], in1=pa[:, 0, :], op=ALU.mult)
    V("tensor_tensor", reads=[pa], writes=[sc["t0"]], out=sc["t0"][:], in0=pa[:, 1, :], in1=pa[:, 1, :], op=ALU.mult)
    V("tensor_tensor", reads=[sc["den"], sc["t0"]], writes=[sc["den"]], out=sc["den"][:], in0=sc["den"][:], in1=sc["t0"][:], op=ALU.add)
    V("reciprocal", reads=[sc["den"]], writes=[sc["den"]], out=sc["den"][:], in_=sc["den"][:])
    V("tensor_scalar", reads=[sc["abr"]], writes=[sc["abr"]], out=sc["abr"][:], in0=sc["abr"][:], scalar1=-1.0, scalar2=None, op0=ALU.add)
    V("tensor_tensor", reads=[sc["abr"], pa], writes=[sc["t0"]], out=sc["t0"][:], in0=sc["abr"][:], in1=pa[:, 0, :], op=ALU.mult)
    V("tensor_tensor", reads=[sc["abi"], pa], writes=[sc["t1"]], out=sc["t1"][:], in0=sc["abi"][:], in1=pa[:, 1, :], op=ALU.mult)
    V("tensor_tensor", reads=[sc["t0"], sc["t1"]], writes=[sc["t0"]], out=sc["t0"][:], in0=sc["t0"][:], in1=sc["t1"][:], op=ALU.add)
    V("tensor_tensor", reads=[sc["t0"], sc["den"]], writes=[sc["fre"]], out=sc["fre"][:], in0=sc["t0"][:], in1=sc["den"][:], op=ALU.mult)
    V("tensor_tensor", reads=[sc["abi"], pa], writes=[sc["t0"]], out=sc["t0"][:], in0=sc["abi"][:], in1=pa[:, 0, :], op=ALU.mult)
    V("tensor_tensor", reads=[sc["abr"], pa], writes=[sc["t1"]], out=sc["t1"][:], in0=sc["abr"][:], in1=pa[:, 1, :], op=ALU.mult)
    V("tensor_tensor", reads=[sc["t0"], sc["t1"]], writes=[sc["t0"]], out=sc["t0"][:], in0=sc["t0"][:], in1=sc["t1"][:], op=ALU.subtract)
    V("tensor_tensor", reads=[sc["t0"], sc["den"]], writes=[sc["fim"]], out=sc["fim"][:], in0=sc["t0"][:], in1=sc["den"][:], op=ALU.mult)
    V("tensor_scalar", reads=[sc["fim"]], writes=[sc["nfim"]], out=sc["nfim"][:], in0=sc["fim"][:], scalar1=-1.0, scalar2=None, op0=ALU.mult)
    for kc in range(32):
        V("tensor_scalar", reads=[pb, sc["fre"]], writes=[Bb], out=Bb[:, 0, kc, :], in0=pb[:, 0, kc, :], scalar1=sc["fre"][:, kc:kc + 1], scalar2=None, op0=ALU.mult)
        V("scalar_tensor_tensor", reads=[pb, sc["nfim"], Bb], writes=[Bb], out=Bb[:, 0, kc, :], in0=pb[:, 1, kc, :], scalar=sc["nfim"][:, kc:kc + 1],
          in1=Bb[:, 0, kc, :], op0=ALU.mult, op1=ALU.add)
        V("tensor_scalar", reads=[pb, sc["fre"]], writes=[Bb], out=Bb[:, 1, kc, :], in0=pb[:, 1, kc, :], scalar1=sc["fre"][:, kc:kc + 1], scalar2=None, op0=ALU.mult)
        V("scalar_tensor_tensor", reads=[pb, sc["fim"], Bb], writes=[Bb], out=Bb[:, 1, kc, :], in0=pb[:, 0, kc, :], scalar=sc["fim"][:, kc:kc + 1],
          in1=Bb[:, 1, kc, :], op0=ALU.mult, op1=ALU.add)
    C.S.cur_scope = C.S.cur_scope.split("/")[0] + "/Srelay"
    for g in range(32):
        ct, gl8 = divmod(g, 8)
        b0 = (g % 2) * 2
        for h in range(2):
            bank = ps[b0 + h]
            for tau in range(8):
                C.mm(bank, bank[:, 0:HC], lhsT=sel[:, gl8, tau, :], rhs=p5[:, ct, tau * NCH + h * HC:tau * NCH + (h + 1) * HC], reads=[sel, p5],
                     start=(tau == 0), stop=(tau == 7))
        src = PS[:, b0 * 512:(b0 + 2) * 512].rearrange("q (b c) -> q b c", b=2)[:, :, 0:HC]
        dst = X[:, g, :].rearrange("q (b c) -> q b c", b=2)
        if g % 2 == 0:
            C.op("scalar", "copy", reads=[ps[b0], ps[b0 + 1]], writes=[Xb[g]], out=dst, in_=src)
        else:
            C.op("vector", "tensor_copy", reads=[ps[b0], ps[b0 + 1]], writes=[Xb[g]], out=dst, in_=src)
    C.S.cur_scope = C.S.cur_scope.split("/")[0] + "/Smain"
    wst = [C.sb("s5wst%d" % i, [128, 128]) for i in range(2)]
    xm = [C.sb("s5xm%d" % i, [128, 128]) for i in range(2)]
    ym = [C.sb("s5ym%d" % i, [128, 128]) for i in range(2)]
    tmpo = [C.sb("s5tmpo%d" % i, [128, 128]) for i in range(2)]
    tmpg = [C.sb("s5tmpg%d" % i, [128, 128]) for i in range(2)]
    Rm = [[C.sb("s5R%d%d" % (k, i), [128, 128], BF16) for i in range(2)] for k in range(2)]
    mstT = [C.sb("s5mstT%d" % i, [128, 128], BF16) for i in range(2)]
    mint = [[C.sb("s5mint%d%d" % (k, g2), [128, 128], BF16) for g2 in range(2)] for k in range(2)]
    Vs = [C.sb("s5Vs%d" % i, [128, NCH]) for i in range(2)]
    Vp = [C.sb("s5Vp%d" % i, [128, NCH]) for i in range(2)]
    Et = [C.sb("s5Et%d" % i, [128, NCH]) for i in range(2)]
    rh = C.sb("s5rh", [128, NCH])
    tq = [C.sb("s5tq%d" % i, [128, NCH]) for i in range(4)]
    kiq = C.sb("s5kiq", [128, NCH], I32)
    Sp = [[C.sb("s5Sp%d%d" % (k, i), [128, NCH], BF16) for i in range(2)] for k in range(2)]
    for k in range(2):
        for i in range(2):
            C.op("gpsimd", "memset", writes=[Sp[k][i]], ap=Sp[k][i][:], constant=0.0)
    ga = C.sb("s5ga", [128, 512]); gb = C.sb("s5gb", [128, 512]); gc = C.sb("s5gc", [128, 512])
    G = lambda name, **kw: C.op("gpsimd", name, **kw)
    for gp in range(16):
        ct = gp // 4
        for k in range(2):
            kc = k * 16 + gp
            Br, Bi = Bb[:, 0, kc, :], Bb[:, 1, kc, :]
            Cr, Ci = pc[:, 0, kc, :], pc[:, 1, kc, :]
            P = lambda tab, a, b: (tab[:, 0, kc, a:b], tab[:, 1, kc, a:b])
            if k == 0:
                pw_st, pw_x, pw_y, pw_r = P(apr, 1, 9), P(ang_, 0, 8), P(apw, 0, 8), P(apw, 1, 9)
            else:
                pw_st, pw_x, pw_y, pw_r = P(apw, 0, 8), P(apw, 0, 8), P(ang_, 0, 8), P(apr, 0, 8)
            cplx_outer(C, "vector", wst[0], wst[0][:], wst[1][:], pw_st[0], pw_st[1], Br, Bi, [apw, apr, ang_, Bb], tmpo)
            wst[1].b = wst[0].b
            if k == 0:
                cplx_outer(C, "vector", xm[0], xm[0][:], xm[1][:], pw_x[0], pw_x[1], Br, Bi, [apw, apr, ang_, Bb], tmpo)
                xm[1].b = xm[0].b
                xsrc = xm
            else:
                xsrc = wst
            cplx_outer(C, "gpsimd", ym[0], ym[0][:], ym[1][:], pw_y[0], pw_y[1], Cr, Ci, [apw, apr, ang_, pc], tmpg, neg_i=True)
            ym[1].b = ym[0].b
            cplx_outer(C, "gpsimd", Rm[k][0], Rm[k][0][:], Rm[k][1][:], pw_r[0], pw_r[1], Cr, Ci, [apw, apr, ang_, pc], tmpg, neg_i=True)
            Rm[k][1].b = Rm[k][0].b
            for ri in range(2):
                C.mm(ps[6], ps[6][:, ri * 128:(ri + 1) * 128], lhsT=wst[ri][:], rhs=C.ident_f[:], reads=[wst[0], C.ident_f], start=True, stop=True)
            C.op("scalar", "copy", reads=[ps[6]], writes=[mstT[0]], out=mstT[0][:], in_=ps[6][:, 0:128])
            C.op("scalar", "copy", reads=[ps[6]], writes=[mstT[1]], out=mstT[1][:], in_=ps[6][:, 128:256])
            for g2 in range(2):
                hs = slice(g2 * 64, (g2 + 1) * 64)
                o = ps[7][:, g2 * 128:(g2 + 1) * 128]
                C.mm(ps[7], o, lhsT=xsrc[0][hs, :], rhs=ym[0][hs, :], reads=[xsrc[0], ym[0]], start=True, stop=False)
                C.mm(ps[7], o, lhsT=xsrc[1][hs, :], rhs=ym[1][hs, :], reads=[xsrc[0], ym[0]], start=False, stop=True)
                V("tensor_tensor", reads=[ps[7], masks], writes=[mint[k][g2]], out=mint[k][g2][:], in0=o, in1=masks[:, k, :], op=ALU.mult)
            for ri in range(2):
                for g2 in range(2):
                    g = 2 * gp + g2
                    for h in range(2):
                        bank = ps[ri * 2 + h]
                        C.mm(bank, bank[g2 * 64:(g2 + 1) * 64, 0:HC], lhsT=mstT[ri][:, g2 * 64:(g2 + 1) * 64], rhs=X[:, g, h * HC:(h + 1) * HC],
                             reads=[mstT[ri], Xb[g]], start=True, stop=True)
                src = PS[:, ri * 1024:(ri + 1) * 1024].rearrange("q (b c) -> q b c", b=2)[:, :, 0:HC]
                C.op("scalar", "copy", reads=[ps[ri * 2], ps[ri * 2 + 1]], writes=[Vs[ri]], out=Vs[ri][:].rearrange("q (b c) -> q b c", b=2), in_=src)
            reduce_sin(C, "vector", Et[0], Et[0][:], kp, kp[:], tq[0], tq[0][:], kiq, kiq[:], mul=sc["phi"][:, kc:kc + 1], add=0.0, reads=[sc["phi"]])
            reduce_sin(C, "vector", Et[1], Et[1][:], kp, kp[:], tq[0], tq[0][:], kiq, kiq[:], mul=sc["phi"][:, kc:kc + 1], add=math.pi / 2, reads=[sc["phi"]])
            V("tensor_scalar", reads=[kp, sc["rho8"]], writes=[rh], out=rh[:], in0=kp[:], scalar1=0.0, scalar2=sc["rho8"][:, kc:kc + 1], op0=ALU.mult, op1=ALU.add)
            if k == 0:
                segs_in = [(slice(0, NCH), slice(0, NCH))]
                segs_out = [(slice(0, NCH - 1), slice(1, NCH))]
            else:
                segs_in = [(slice(0, 32), slice(31, None, -1)), (slice(32, NCH), slice(NCH - 1, 31, -1))]
                segs_out = [(slice(0, 31), slice(30, None, -1)), (slice(31, NCH - 1), slice(NCH - 1, 31, -1))]
            sn, cs = Et[0], Et[1]
            for (pp, cc) in segs_in:
                n = pp.stop - pp.start
                V("tensor_tensor", reads=[Vs[0], cs], writes=[tq[0]], out=tq[0][:, pp], in0=Vs[0][:, cc], in1=cs[:, pp], op=ALU.mult)
                V("tensor_tensor", reads=[Vs[1], sn], writes=[tq[1]], out=tq[1][:, pp], in0=Vs[1][:, cc], in1=sn[:, pp], op=ALU.mult)
                V("tensor_tensor", reads=[tq[0], tq[1]], writes=[Vp[0]], out=Vp[0][:, pp], in0=tq[0][:, pp], in1=tq[1][:, pp], op=ALU.add)
                G("tensor_tensor", reads=[Vs[1], cs], writes=[tq[2]], out=tq[2][:, pp], in0=Vs[1][:, cc], in1=cs[:, pp], op=ALU.mult)
                G("tensor_tensor", reads=[Vs[0], sn], writes=[tq[3]], out=tq[3][:, pp], in0=Vs[0][:, cc], in1=sn[:, pp], op=ALU.mult)
                G("tensor_tensor", reads=[tq[2], tq[3]], writes=[Vp[1]], out=Vp[1][:, pp], in0=tq[2][:, pp], in1=tq[3][:, pp], op=ALU.subtract)
            V("tensor_tensor_scan", reads=[rh, Vp[0]], writes=[Vp[0]], out=Vp[0][:], data0=rh[:], data1=Vp[0][:], initial=0.0, op0=ALU.mult, op1=ALU.add)
            V("tensor_tensor_scan", reads=[rh, Vp[1]], writes=[Vp[1]], out=Vp[1][:], data0=rh[:], data1=Vp[1][:], initial=0.0, op0=ALU.mult, op1=ALU.add)
            for (pp, cc) in segs_out:
                V("tensor_tensor", reads=[Vp[0], cs], writes=[tq[0]], out=tq[0][:, pp], in0=Vp[0][:, pp], in1=cs[:, pp], op=ALU.mult)
                V("tensor_tensor", reads=[Vp[1], sn], writes=[tq[1]], out=tq[1][:, pp], in0=Vp[1][:, pp], in1=sn[:, pp], op=ALU.mult)
                V("tensor_tensor", reads=[tq[0], tq[1]], writes=[Sp[k][0]], out=Sp[k][0][:, cc], in0=tq[0][:, pp], in1=tq[1][:, pp], op=ALU.subtract)
                G("tensor_tensor", reads=[Vp[1], cs], writes=[tq[2]], out=tq[2][:, pp], in0=Vp[1][:, pp], in1=cs[:, pp], op=ALU.mult)
                G("tensor_tensor", reads=[Vp[0], sn], writes=[tq[3]], out=tq[3][:, pp], in0=Vp[0][:, pp], in1=sn[:, pp], op=ALU.mult)
                G("tensor_tensor", reads=[tq[2], tq[3]], writes=[Sp[k][1]], out=Sp[k][1][:, cc], in0=tq[2][:, pp], in1=tq[3][:, pp], op=ALU.add)
        for g2 in range(2):
            g = 2 * gp + g2
            hs = slice(g2 * 64, (g2 + 1) * 64)
            for h in range(2):
                bank = ps[4 + h]
                cs_ = slice(h * HC, (h + 1) * HC)
                for k in range(2):
                    C.mm(bank, bank[:, 0:HC], lhsT=mint[k][g2][:], rhs=X[:, g, cs_], reads=[mint[k][g2], Xb[g]], start=(k == 0), stop=False)
                    C.mm(bank, bank[:, 0:HC], lhsT=Rm[k][0][hs, :], rhs=Sp[k][0][hs, cs_], reads=[Rm[k][0], Sp[k][0]], start=False, stop=False)
                    C.mm(bank, bank[:, 0:HC], lhsT=Rm[k][1][hs, :], rhs=Sp[k][1][hs, cs_], reads=[Rm[k][0], Sp[k][1]], start=False, stop=(k == 1))
            src = PS[:, 4 * 512:6 * 512].rearrange("q (b c) -> q b c", b=2)[:, :, 0:HC]
            C.op("scalar", "copy", reads=[ps[4], ps[5]], writes=[Xb[g]], out=X[:, g, :].rearrange("q (b c) -> q b c", b=2), in_=src)
    C.S.cur_scope = C.S.cur_scope.split("/")[0] + "/Sout"
    nb = 0
    for ct in range(4):
        for t0 in range(0, TOK, 512):
            bank = ps[nb % 4]; nb += 1
            c0 = t0 // 8
            nt_ = min(512, TOK - t0)
            ncq = nt_ // 8
            for tau in range(8):
                for gl8 in range(8):
                    g = ct * 8 + gl8
                    C.mm(bank, bank[:, tau:nt_:8], lhsT=selT[:, gl8, tau, :], rhs=X[:, g, c0:c0 + ncq], reads=[selT, Xb[g]], start=(gl8 == 0), stop=(gl8 == 7))
            cs_ = slice(t0, t0 + nt_)
            w_ = slice(0, nt_)
            u_ap = p5[:, ct, :].rearrange("p (t c) -> p c t", t=8)[:, c0:c0 + ncq, :]
            V("scalar_tensor_tensor", reads=[p5, dcol, bank], writes=[ga], out=ga[:, w_].rearrange("p (c t) -> p c t", t=8), in0=u_ap,
              scalar=dcol[:, ct:ct + 1], in1=bank[:, w_].rearrange("p (c t) -> p c t", t=8), op0=ALU.mult, op1=ALU.add)
            G("tensor_tensor", reads=[ga], writes=[gb], out=gb[:, w_], in0=ga[:, w_], in1=ga[:, w_], op=ALU.mult)
            G("tensor_scalar", reads=[gb], writes=[gb], out=gb[:, w_], in0=gb[:, w_], scalar1=0.044715, scalar2=1.0, op0=ALU.mult, op1=ALU.add)
            G("tensor_tensor", reads=[gb, ga], writes=[gb], out=gb[:, w_], in0=gb[:, w_], in1=ga[:, w_], op=ALU.mult)
            C.op("scalar", "activation", reads=[gb], writes=[gc], out=gc[:, w_], in_=gb[:, w_], func=AF.Tanh, scale=0.7978845608028654)
            C.op("scalar", "mul", reads=[ga], writes=[ga], out=ga[:, w_], in_=ga[:, w_], mul=0.5)
            V("scalar_tensor_tensor", reads=[gc, ga], writes=[ysb], out=ysb[:, cs_], in0=gc[:, w_], scalar=1.0, in1=ga[:, w_], op0=ALU.add, op1=ALU.mult)
        C.dma(D["yss"][ct * 128:(ct + 1) * 128, :], ysb[:], reads=[ysb], eng="scalar")
    if "d_ys" in C.dbg and l == 0:
        C.S.barrier()
        C.dma(D["d_ys"][:, :], D["yss"][:, :], key="dbg")


def load_weight_bf16(C, dst, src, nk, ncols, stg, col0=0, blk=None, cnt=[0]):
    blk = blk or stg[0].ap.shape[1]
    for k in range(nk):
        for c0 in range(0, ncols, blk):
            n = min(blk, ncols - c0)
            st = stg[cnt[0] % len(stg)]
            C.dma(st[:, 0:n], src[k * 128:(k + 1) * 128, col0 + c0:col0 + c0 + n], writes=[st])
            e = cnt[0] % 3
            if e == 0:
                C.op("gpsimd", "tensor_copy", reads=[st], writes=[dst], out=dst[:, k, c0:c0 + n], in_=st[:, 0:n])
            elif e == 1:
                C.op("scalar", "copy", reads=[st], writes=[dst], out=dst[:, k, c0:c0 + n], in_=st[:, 0:n])
            else:
                C.op("vector", "tensor_copy", reads=[st], writes=[dst], out=dst[:, k, c0:c0 + n], in_=st[:, 0:n])
            cnt[0] += 1


def bcast_mod(C, l, j, dst):
    row = C.sb("bcrow", [2, 1024])
    C.dma(row[:], C.D["mods"][l, :, j * 1024:(j + 1) * 1024], reads=[C.bmods], writes=[row])
    for r in range(2):
        for half in range(2):
            ps = C.psum[r * 2 + half]
            C.mm(ps, ps[:, :], lhsT=C.sel2[:, r, :], rhs=row[:, half * 512:(half + 1) * 512], reads=[C.sel2, row], start=True, stop=True)
            C.op("vector", "tensor_copy", reads=[ps], writes=[dst], out=dst[:, r, half * 512:(half + 1) * 512], in_=ps[:, :])


def phaseC(C, l, last):
    D = C.D
    WG = C.sb("cWG", [128, 8, 3072], BF16)
    glu = C.sb("cglu", [128, 4, 2048], BF16)
    scow = C.sb("cscow", [128, 4, 1024], BF16)
    hyow = C.sb("chyow", [128, 4, 1024], BF16)
    outw = C.sb("coutw", [128, 8, 1024], BF16)
    stg = [C.sb("cstg%d" % i, [128, 2048]) for i in range(2)]
    g1bc = C.sb("cg1bc", [128, 2, 1024])
    xt = C.sb("cxt", [128, 4, 1024])
    xtb = [Buf("cxt%d" % i) for i in range(4)]
    hT = C.sb("chT", [128, 8, 512], BF16)
    ysgs = [C.sb("cysg%d" % i, [128, 4, 512], BF16) for i in range(2)]
    scgs = [C.sb("cscg%d" % i, [128, 4, 512], BF16) for i in range(2)]
    zxgs = [C.sb("czxg%d" % i, [128, 4, 512], BF16) for i in range(2)]
    m = C.sb("cm", [128, 8, 512], BF16)
    nsc = make_norm_scratch(C)
    sg = [C.sb("csg%d" % i, [128, 512]) for i in range(2)]
    t1 = [C.sb("ct1%d" % i, [128, 512]) for i in range(2)]
    macc = [C.sb("cmacc%d" % i, [128, 512]) for i in range(2)]
    tmpx = [C.sb("ctmpx%d" % i, [128, 512]) for i in range(2)]
    bcast_mod(C, l, 2, g1bc)
    load_weight_bf16(C, WG, D["w_in"][l], 8, 3072, stg, col0=OFF_GATE)
    load_weight_bf16(C, glu, D["glu_w"][l], 4, 2048, stg)
    load_weight_bf16(C, scow, D["sc_out_w"][l], 4, 1024, stg)
    load_weight_bf16(C, hyow, D["hy_out_w"][l], 4, 1024, stg)
    load_weight_bf16(C, outw, D["out_w"][l], 8, 1024, stg)
    C.S.cur_scope = C.S.cur_scope.split("/")[0] + "/Cmain"
    ysv = D["yss"].rearrange("(j c) t -> c j t", c=128)
    scv = D["scm"].rearrange("(j c) t -> c j t", c=128)
    zxv = D["zxs"].rearrange("(j c) t -> c j t", c=128)
    groups = GROUPS[1:] if last else GROUPS
    cnt = 0
    ring = [0]
    for gidx, (g0, gn) in enumerate(groups):
        r = 1 if g0 < 256 else 0
        ntl = gn // 128
        ysg, scg, zxg = ysgs[gidx % 2], scgs[gidx % 2], zxgs[gidx % 2]
        items = []
        for ti in range(ntl):
            i = g0 // 128 + ti
            xti = T(xt[:, ti, :], "x"); xti.b = xtb[ti]
            C.dma(xt[:, ti, :], D["xs"][i * 128:(i + 1) * 128, :], reads=[C.bxs[i]], writes=[xtb[ti]])
            items.append((r, xti, hT, ti * 128))
        norm_tiles(C, 0, items, nsc, [C.psum[6], C.psum[7]])
        C.dma(ysg[:, :, 0:gn], ysv[:, :, g0:g0 + gn], writes=[ysg])
        C.dma(scg[:, :, 0:gn], scv[:, :, g0:g0 + gn], writes=[scg])
        C.dma(zxg[:, :, 0:gn], zxv[:, :, g0:g0 + gn], writes=[zxg])
        for j in range(8):
            js = slice(j * 128, (j + 1) * 128)
            mc = macc[j % 2]
            stages = [
                [(glu, ysg, lambda k: glu[:, k, js], 4), (glu, ysg, lambda k: glu[:, k, 1024 + j * 128:1024 + (j + 1) * 128], 4),
                 (WG, hT, lambda k: WG[:, k, j * 128:(j + 1) * 128], 8)],
                [(scow, scg, lambda k: scow[:, k, js], 4), (WG, hT, lambda k: WG[:, k, 1024 + j * 128:1024 + (j + 1) * 128], 8)],
                [(hyow, zxg, lambda k: hyow[:, k, js], 4), (WG, hT, lambda k: WG[:, k, 2048 + j * 128:2048 + (j + 1) * 128], 8)],
            ]
            for si, stage in enumerate(stages):
                bk = []
                for (wt_, rt_, lfn, nk) in stage:
                    b_ = C.psum[ring[0] % 8]; ring[0] += 1
                    bk.append(b_)
                    for k in range(nk):
                        C.mm(b_, b_[:, 0:gn], lhsT=lfn(k), rhs=rt_[:, k, 0:gn], reads=[wt_, rt_], start=(k == 0), stop=(k == nk - 1))
                sgt, tt = sg[cnt % 2], t1[cnt % 2]; cnt += 1
                if si == 0:
                    C.op("scalar", "activation", reads=[bk[1]], writes=[sgt], out=sgt[:, 0:gn], in_=bk[1][:, 0:gn], func=AF.Sigmoid)
                    C.op("vector", "tensor_tensor", reads=[bk[0], sgt], writes=[mc], out=mc[:, 0:gn], in0=bk[0][:, 0:gn], in1=sgt[:, 0:gn], op=ALU.mult)
                    sg2 = sg[cnt % 2]; cnt += 1
                    C.op("scalar", "activation", reads=[bk[2]], writes=[sg2], out=sg2[:, 0:gn], in_=bk[2][:, 0:gn], func=AF.Sigmoid)
                    C.op("gpsimd", "tensor_tensor", reads=[mc, sg2], writes=[mc], out=mc[:, 0:gn], in0=mc[:, 0:gn], in1=sg2[:, 0:gn], op=ALU.mult)
                else:
                    C.op("scalar", "activation", reads=[bk[1]], writes=[sgt], out=sgt[:, 0:gn], in_=bk[1][:, 0:gn], func=AF.Sigmoid)
                    C.op("vector", "tensor_tensor", reads=[bk[0], sgt], writes=[tt], out=tt[:, 0:gn], in0=bk[0][:, 0:gn], in1=sgt[:, 0:gn], op=ALU.mult)
                    if si == 1:
                        C.op("gpsimd", "tensor_tensor", reads=[mc, tt], writes=[mc], out=mc[:, 0:gn], in0=mc[:, 0:gn], in1=tt[:, 0:gn], op=ALU.add)
                    else:
                        C.op("gpsimd", "tensor_tensor", reads=[mc, tt], writes=[m], out=m[:, j, 0:gn], in0=mc[:, 0:gn], in1=tt[:, 0:gn], op=ALU.add)
        for ti in range(ntl):
            i = g0 // 128 + ti
            for half in range(2):
                ps = C.psum[ring[0] % 8]; ring[0] += 1
                hs = slice(half * 512, (half + 1) * 512)
                for k in range(8):
                    C.mm(ps, ps[:, :], lhsT=m[:, k, ti * 128:(ti + 1) * 128], rhs=outw[:, k, hs], reads=[m, outw], start=(k == 0), stop=(k == 7))
                tx = tmpx[half]
                C.op("vector", "tensor_tensor", reads=[ps, g1bc], writes=[tx], out=tx[:], in0=ps[:, :], in1=g1bc[:, r, hs], op=ALU.mult)
                C.op("gpsimd", "tensor_tensor", reads=[tx, xtb[ti]], writes=[xtb[ti]], out=xt[:, ti, hs], in0=xt[:, ti, hs], in1=tx[:], op=ALU.add)
            C.dma(D["xs"][i * 128:(i + 1) * 128, :], xt[:, ti, :], reads=[xtb[ti]], writes=[C.bxs[i]], eng="scalar")
    if "d_xs" in C.dbg and C.dbg_stop == ("phaseC", l):
        C.S.barrier()
        C.dma(D["d_xs"][:, :], D["xs"][:, :], key="dbg")


def phaseD(C, l, last):
    D = C.D
    w1b = C.sb("dw1b", [128, 8, 4096], BF16)
    w2b = C.sb("dw2b", [128, 32, 1024], BF16)
    stg = [C.sb("dstg%d" % i, [128, 512]) for i in range(2)]
    g2bc = C.sb("dg2bc", [128, 2, 1024])
    xts = [C.sb("dxt%d" % i, [128, 2, 1024]) for i in range(2)]
    xtbs = [[Buf("dxt%d_%d" % (s_, i)) for i in range(2)] for s_ in range(2)]
    h2s = [C.sb("dh2%d" % i, [128, 8, 256], BF16) for i in range(2)]
    rr_ = C.sb("dr", [128, 32, 256], BF16)
    nsc = make_norm_scratch(C)
    rt = [C.sb("drt%d" % i, [128, 256]) for i in range(2)]
    tmpx = [C.sb("dtmpx%d" % i, [128, 512]) for i in range(2)]
    if last:
        fg = C.sb("dfg", [128, 1024]); fo = C.sb("dfo", [128, 1024])
        fss = C.sb("dfss", [128, 1]); frs = C.sb("dfrs", [128, 1])
        C.dma(fg[:], D["finalg_bc"][:, :], writes=[fg])
    bcast_mod(C, l, 5, g2bc)
    load_weight_bf16(C, w1b, D["mlp_w1"][l], 8, 4096, stg)
    load_weight_bf16(C, w2b, D["mlp_w2"][l], 32, 1024, stg)
    C.S.cur_scope = C.S.cur_scope.split("/")[0] + "/Dmain"
    glist = list(range(1 if last else 0, NT // 2))

    def prep(n):
        gi = glist[n]
        r = 1 if gi == 0 else 0
        xt, xtb, h2 = xts[n % 2], xtbs[n % 2], h2s[n % 2]
        items = []
        for ti in range(2):
            i = gi * 2 + ti
            xti = T(xt[:, ti, :], "x"); xti.b = xtb[ti]
            C.dma(xt[:, ti, :], D["xs"][i * 128:(i + 1) * 128, :], reads=[C.bxs[i]], writes=[xtb[ti]])
            items.append((r, xti, h2, ti * 128))
        norm_tiles(C, 1, items, nsc, [C.psum[6], C.psum[7]])

    prep(0)
    for n, gi in enumerate(glist):
        r = 1 if gi == 0 else 0
        xt, xtb, h2 = xts[n % 2], xtbs[n % 2], h2s[n % 2]
        for i in range(32):
            ps = C.psum[i % 4]
            for k in range(8):
                C.mm(ps, ps[:, 0:256], lhsT=w1b[:, k, i * 128:(i + 1) * 128], rhs=h2[:, k, :], reads=[w1b, h2], start=(k == 0), stop=(k == 7))
            rtt = rt[i % 2]
            C.op("scalar", "activation", reads=[ps], writes=[rtt], out=rtt[:], in_=ps[:, 0:256], func=AF.Relu)
            C.op("gpsimd", "tensor_tensor", reads=[rtt], writes=[rr_], out=rr_[:, i, :], in0=rtt[:], in1=rtt[:], op=ALU.mult)
        if n + 1 < len(glist):
            prep(n + 1)
        for ti in range(2):
            i = gi * 2 + ti
            for half in range(2):
                ps = C.psum[4 + (ti * 2 + half) % 2]
                hs = slice(half * 512, (half + 1) * 512)
                for k in range(32):
                    C.mm(ps, ps[:, :], lhsT=rr_[:, k, ti * 128:(ti + 1) * 128], rhs=w2b[:, k, hs], reads=[rr_, w2b], start=(k == 0), stop=(k == 31))
                tx = tmpx[half]
                C.op("vector", "tensor_tensor", reads=[ps, g2bc], writes=[tx], out=tx[:], in0=ps[:, :], in1=g2bc[:, r, hs], op=ALU.mult)
                C.op("gpsimd", "tensor_tensor", reads=[tx, xtb[ti]], writes=[xtb[ti]], out=xt[:, ti, hs], in0=xt[:, ti, hs], in1=tx[:], op=ALU.add)
            if not last:
                C.dma(D["xs"][i * 128:(i + 1) * 128, :], xt[:, ti, :], reads=[xtb[ti]], writes=[C.bxs[i]], eng="scalar")
            else:
                if "d_xs" in C.dbg:
                    C.dma(D["xs"][i * 128:(i + 1) * 128, :], xt[:, ti, :], reads=[xtb[ti]], writes=[C.bxs[i]], eng="scalar")
                C.op("scalar", "activation", reads=[xtb[ti]], writes=[fo, fss], out=fo[:], in_=xt[:, ti, :], func=AF.Square, accum_out=fss[:])
                C.op("vector", "tensor_scalar", reads=[fss], writes=[frs], out=frs[:], in0=fss[:], scalar1=1.0 / D_MODEL, scalar2=EPS, op0=ALU.mult, op1=ALU.add)
                C.op("scalar", "activation", reads=[frs], writes=[frs], out=frs[:], in_=frs[:], func=AF.Sqrt)
                C.op("vector", "reciprocal", reads=[frs], writes=[frs], out=frs[:], in_=frs[:])
                C.op("vector", "scalar_tensor_tensor", reads=[xtb[ti], frs, fg], writes=[fo], out=fo[:], in0=xt[:, ti, :], scalar=frs[:, 0:1], in1=fg[:],
                     op0=ALU.mult, op1=ALU.mult)
                C.dma(D["out"][(i - 2) * 128:(i - 1) * 128, :], fo[:], reads=[fo], eng="scalar")
    if "d_xs" in C.dbg and C.dbg_stop == ("phaseD", l):
        C.S.barrier()
        C.dma(D["d_xs"][:, :], D["xs"][:, :], key="dbg")


_CONST = {}


def _consts():
    if _CONST:
        return _CONST
    bf = ml_dtypes.bfloat16
    c = _CONST
    c["ident_bf"] = np.eye(128, dtype=np.float32).astype(bf)
    c["ident_f"] = np.eye(128, dtype=np.float32)
    c["ones_bf"] = np.ones((128, 128), np.float32).astype(bf)
    alt = (1.0 - 2.0 * (np.arange(128) % 2)).astype(np.float32)
    c["alt_bf"] = np.repeat(alt[:, None], 128, axis=1).astype(bf)
    c["altcol"] = alt[:, None].copy()
    sel = np.zeros((2, 2, 128), np.float32); sel[0, 0] = 1; sel[1, 1] = 1
    c["sel2"] = sel
    for L, sfx in ((SEQ, ""), (CTX, "256")):
        N = 2 * L
        nt = L // 128
        t = (np.arange(nt)[None, :, None, None] * 128 + np.arange(128)[None, None, :, None]).astype(np.int64)
        f = (np.arange(nt)[:, None, None, None] * 128 + np.arange(128)[None, None, None, :]).astype(np.int64)
        ph = ((t * f) % N).astype(np.float64) * (2.0 * np.pi / N)
        c["dftc" + sfx] = np.ascontiguousarray(np.cos(ph).transpose(0, 2, 1, 3)).astype(np.float32).astype(bf)
        c["dfts" + sfx] = np.ascontiguousarray(np.sin(ph).transpose(0, 2, 1, 3)).astype(np.float32).astype(bf)
        tl = np.linspace(0.0, 1.0, L, dtype=np.float32)[:, None]
        ang = (2.0 * np.float32(math.pi) * np.arange(L, dtype=np.float32)[:, None] / np.float32(L)).astype(np.float32)
        bands = np.linspace(1e-4, 15, 16, dtype=np.float32)[None, :]
        z = np.concatenate([tl, np.cos(bands * ang), -np.sin(bands * ang)], axis=-1).astype(np.float32)
        c["zT" + sfx] = np.ascontiguousarray(z.T)
        mx = math.log(1e-2) / 0.3; mn = math.log(1e-2) / 1.5
        deltas = np.abs(np.linspace(mn, mx, 512, dtype=np.float32))
        c["win" + sfx] = (np.exp(-tl * deltas[None, :]) + np.float32(0.05)).astype(np.float32)
        fidx = np.arange(nt)[None, :] * 128 + np.arange(128)[:, None]
        c["wf" + sfx] = np.where(fidx == 0, 1.0 / N, 2.0 / N).astype(np.float32)
    c["kpos"] = np.repeat(np.arange(TOK, dtype=np.float32)[None, :], 128, axis=0)
    sel = np.zeros((128, 8, 8, 128), np.float32)
    for g in range(8):
        for tau in range(8):
            for p in range(16):
                sel[g * 16 + p, g, tau, tau * 16 + p] = 1.0
    c["sel8"] = sel.astype(bf)
    c["selT8"] = np.ascontiguousarray(sel.transpose(3, 1, 2, 0)).astype(bf)
    ti = np.arange(128)[:, None] // 16; to = np.arange(128)[None, :] // 16
    c["cmask"] = np.ascontiguousarray(np.stack([(to >= ti), (to <= ti)], axis=1)).astype(np.float32)
    quarter = D_MODEL // 4
    omega = (1.0 / (10000.0 ** (np.arange(quarter, dtype=np.float32) / np.float32(quarter)))).astype(np.float32)
    rows = SEQ // 64
    er = np.arange(rows, dtype=np.float32)[:, None] * omega[None]
    ec = np.arange(64, dtype=np.float32)[:, None] * omega[None]
    er = np.concatenate([np.sin(er), np.cos(er)], -1); ec = np.concatenate([np.sin(ec), np.cos(ec)], -1)
    emb = np.concatenate([np.broadcast_to(er[:, None, :], (rows, 64, 512)), np.broadcast_to(ec[None, :, :], (rows, 64, 512))], -1)
    c["pos"] = np.ascontiguousarray(emb.reshape(SEQ, D_MODEL)).astype(np.float32)
    return c


def _colmajor(v, nk):
    v = np.asarray(v, np.float32)
    r = v.reshape(v.shape[:-1] + (nk, 128))
    return np.ascontiguousarray(np.moveaxis(r, -1, 0))


def prep_shared(inp):
    L = DEPTH
    sh = dict(_consts())
    f32 = lambda a: np.ascontiguousarray(np.asarray(a, np.float32))
    for k_src, k_dst in (("ada_w", "ada_w"), ("w_in", "w_in"), ("s5_glu_w", "glu_w"), ("sc_out_w", "sc_out_w"),
                         ("hy_out_w", "hy_out_w"), ("out_w", "out_w"), ("mlp_w1", "mlp_w1"), ("mlp_w2", "mlp_w2"),
                         ("hy_f_w1", "hyw1"), ("hy_f_w2", "hyw2"), ("hy_f_w3", "hyw3"), ("hy_f_w4", "hyw4")):
        sh[k_dst] = f32(inp[k_src])
    sh["adab2"] = np.ascontiguousarray(np.repeat(f32(inp["ada_b"])[:, None, :], 2, axis=1))
    n1 = _colmajor(inp["norm1_g"], 8); n2 = _colmajor(inp["norm2_g"], 8)
    sh["ncol"] = np.ascontiguousarray(np.stack([n1, n2], axis=2))
    sh["finalg_bc"] = np.ascontiguousarray(np.repeat(f32(inp["final_g"])[None, :], 128, axis=0))
    sh["scw"] = np.ascontiguousarray(_colmajor(inp["sc_conv_w"], 4).transpose(0, 1, 3, 2))
    sh["hcw"] = np.ascontiguousarray(_colmajor(inp["hy_conv_w"], 12).transpose(0, 1, 3, 2))
    sh["hcb"] = _colmajor(inp["hy_conv_b"], 12)
    def qlay(a):
        a = f32(a)
        s = a.shape
        a = a.reshape(s[0], 2, 16, 2, 64, *s[4:])
        a = np.moveaxis(a, (3, 4), (0, 1))
        return np.ascontiguousarray(a.reshape(128, s[0], 32, *s[4:]))
    ldt = np.repeat(f32(inp["s5_log_dt"])[:, :, :, None], 64, axis=3)
    sh["s5a"] = np.ascontiguousarray(np.stack([qlay(inp["s5_a_re"]), qlay(inp["s5_a_im"]), qlay(ldt)], axis=2))
    sh["s5b"] = np.ascontiguousarray(np.stack([qlay(inp["s5_b_re"]), qlay(inp["s5_b_im"])], axis=2))
    cre = np.swapaxes(f32(inp["s5_c_re"]), 3, 4); cim = np.swapaxes(f32(inp["s5_c_im"]), 3, 4)
    sh["s5c"] = np.ascontiguousarray(np.stack([qlay(cre), qlay(cim)], axis=2))
    sh["s5d"] = _colmajor(inp["s5_d"], 4)
    hyfb = np.stack([f32(inp["hy_f_freq"]), f32(inp["hy_f_b1"]), f32(inp["hy_f_b2"]), f32(inp["hy_f_b3"])], axis=-1)
    sh["hyfb"] = np.ascontiguousarray(hyfb.transpose(1, 0, 2))
    sh["hyd_bc"] = np.ascontiguousarray(np.repeat(f32(inp["hy_d"])[None, :, :], 128, axis=0))
    return sh


def prep_core(inp, sh, b):
    m = dict(sh)
    m["x_in"] = np.ascontiguousarray(np.asarray(inp["x"][b], np.float32))
    m["ctx_in"] = np.ascontiguousarray(np.asarray(inp["ctx"][b], np.float32))
    cc = np.stack([_colmajor(np.asarray(inp["c"][b]), 8), _colmajor(np.asarray(inp["c_ctx"]), 8)], axis=-1)
    m["cc"] = np.ascontiguousarray(cc)
    return m


_NC_CACHE = {}


def kernel(**inputs):
    inp = {k: np.asarray(v) for k, v in inputs.items()}
    sh = prep_shared(inp)
    if "nc" not in _NC_CACHE:
        _NC_CACHE["nc"] = build_program()
    nc = _NC_CACHE["nc"]
    real = [prep_core(inp, sh, b) for b in range(4)]
    consts = set(_consts().keys())
    zero = {k: (v if k in consts else np.zeros_like(v)) for k, v in real[0].items()}
    in_maps = [real[c // 2] if c % 2 == 0 else zero for c in range(8)]
    res = run_bass_kernel_spmd(nc, in_maps, core_ids=list(range(8)))
    out = np.stack([np.asarray(res.results[2 * b]["out"], np.float32) for b in range(4)], axis=0)
    return out
```

```python
import math
import numpy as np
import ml_dtypes
import concourse.bass as bass
import concourse.mybir as mybir
from concourse.bass_utils import run_bass_kernel_spmd
from contextlib import ExitStack

F32 = mybir.dt.float32
BF16 = mybir.dt.bfloat16
I32 = mybir.dt.int32
ALU = mybir.AluOpType
AF = mybir.ActivationFunctionType

D_MODEL = 1024; SEQ = 4096; CTX = 256; DEPTH = 2; TOK = SEQ + CTX; NT = TOK // 128
D_IN = 6656; OFF_SC = 512; OFF_HY = 2048; OFF_GATE = 3584; D_FF = 4096
EPS = 1e-6
TWO_PI = 2.0 * math.pi
ENGS = ["sync", "scalar", "vector", "gpsimd", "tensor"]
SB_BASE = 16512
SB_END = 229376


class Buf:
    __slots__ = ("name", "last_w", "readers")

    def __init__(self, name):
        self.name = name
        self.last_w = None
        self.readers = []


class Op:
    __slots__ = ("eng", "fn", "dma", "pos", "deps", "signal", "sig_idx", "sem_key", "dma_val", "scope")

    def __init__(self, eng, fn, dma):
        self.eng = eng; self.fn = fn; self.dma = dma
        self.deps = []; self.signal = False; self.sig_idx = 0; self.sem_key = None; self.dma_val = 0


class Sched:
    def __init__(self, nc, same_engine_sync=True, n_dma_sems=72):
        self.nc = nc
        self.ops = {e: [] for e in ENGS}
        self.same_engine_sync = same_engine_sync
        self.dma_cnt = {}
        self.dma_since_barrier = []
        self.key_map = {}
        self.n_dma_sems = n_dma_sems
        import os
        self.use_scopes = bool(os.environ.get("KSCOPES"))

    def op(self, eng, fn, reads=(), writes=(), dma=False, sem_key=None):
        o = Op(eng, fn, dma)
        o.scope = getattr(self, "cur_scope", None)
        o.pos = len(self.ops[eng])
        deps = []
        for b in reads:
            if b.last_w is not None:
                deps.append(b.last_w)
        for b in writes:
            if b.last_w is not None:
                deps.append(b.last_w)
            deps.extend(b.readers)
        seen = set()
        for d in deps:
            if id(d) in seen or d is o:
                continue
            seen.add(id(d))
            if not d.dma and d.eng == eng:
                if eng == "tensor" or not self.same_engine_sync:
                    continue
            o.deps.append(d)
        for b in reads:
            if not dma:
                b.readers = [r for r in b.readers if r.dma or r.eng != eng]
            b.readers.append(o)
        for b in writes:
            b.last_w = o
            b.readers = []
        if dma:
            key = sem_key or (writes[0].name if writes else "dma_misc")
            if key not in self.key_map:
                self.key_map[key] = "q%d" % len(self.key_map)
                assert len(self.key_map) <= self.n_dma_sems, ("too many DMA streams in one phase", len(self.key_map))
            key = self.key_map[key]
            o.sem_key = key
            self.dma_cnt[key] = self.dma_cnt.get(key, 0) + 16
            o.dma_val = self.dma_cnt[key]
            self.dma_since_barrier.append(o)
        self.ops[eng].append(o)
        return o

    def barrier(self):
        lasts = [self.ops[e][-1] for e in ENGS if self.ops[e]]
        dmas = list(self.dma_since_barrier)
        self.dma_since_barrier = []
        self.key_map = {}
        for e in ENGS:
            o = Op(e, None, False)
            o.pos = len(self.ops[e])
            o.deps = [d for d in lasts if d.eng != e and not d.dma and d.fn is not None] + dmas
            self.ops[e].append(o)

    def emit(self, es):
        nc = self.nc
        for e in ENGS:
            for o in self.ops[e]:
                for d in o.deps:
                    if not d.dma:
                        d.signal = True
        eng_sem = {}
        for e in ENGS:
            n = 0
            for o in self.ops[e]:
                if o.signal and not o.dma:
                    n += 1
                    o.sig_idx = n
            if n:
                eng_sem[e] = es.enter_context(nc.semaphore("s_" + e))
            self.stats = getattr(self, "stats", {})
            self.stats[e] = (len(self.ops[e]), n)
        dma_sem = {}
        for key in self.dma_cnt:
            dma_sem[key] = es.enter_context(nc.semaphore("d_" + key))
        self.stats["dma"] = dict(self.dma_cnt)
        block = es.enter_context(nc.Block())

        def run(e, eng):
            waited = {}
            for o in self.ops[e]:
                need = {}
                for d in o.deps:
                    if d.dma:
                        k, v, sem = ("d", d.sem_key), d.dma_val, dma_sem[d.sem_key]
                    else:
                        k, v, sem = ("e", d.eng), d.sig_idx, eng_sem[d.eng]
                    if k not in need or need[k][0] < v:
                        need[k] = (v, sem)
                for k, (v, sem) in need.items():
                    if waited.get(k, 0) >= v:
                        continue
                    waited[k] = v
                    eng.wait_ge(sem, v)
                if o.fn is None:
                    continue
                if self.use_scopes and o.scope:
                    with nc.named_scope(o.scope):
                        ins = o.fn(eng)
                else:
                    ins = o.fn(eng)
                if o.dma:
                    ins.then_inc(dma_sem[o.sem_key], 16)
                elif o.signal:
                    ins.then_inc(eng_sem[e], 1)

        @block.sync
        def _(eng):
            run("sync", eng)

        @block.scalar
        def _(eng):
            run("scalar", eng)

        @block.vector
        def _(eng):
            run("vector", eng)

        @block.gpsimd
        def _(eng):
            run("gpsimd", eng)

        @block.tensor
        def _(eng):
            run("tensor", eng)


class T:
    def __init__(self, ap, name, nbuf=1):
        self.ap = ap
        self.b = Buf(name)

    def __getitem__(self, k):
        return self.ap[k]


class Ctx:
    def __init__(self, nc):
        self.nc = nc
        self.S = Sched(nc)
        self.uid = 0
        self.persist = SB_BASE
        self.top = SB_BASE
        self.dq = 0

    def reset_arena(self):
        self.top = self.persist

    def sb(self, name, shape, dt=F32, persist=False):
        esz = 4 if dt in (F32, I32) else 2
        nbytes = int(np.prod(shape[1:])) * esz
        nbytes = (nbytes + 63) // 64 * 64
        self.uid += 1
        off = self.top
        assert off + nbytes <= SB_END, ("SBUF overflow", name, off, nbytes)
        t = self.nc.alloc_sbuf_tensor_at("%s_%d" % (name, self.uid), list(shape), dt, offset=off)
        self.top += nbytes
        if persist:
            assert self.persist == off
            self.persist = self.top
        return T(t.ap(), "%s_%d" % (name, self.uid))

    def op(self, eng, name, reads=(), writes=(), **kw):
        rb = [x.b if isinstance(x, T) else x for x in reads]
        wb = [x.b if isinstance(x, T) else x for x in writes]
        return self.S.op(eng, lambda e: getattr(e, name)(**kw), rb, wb)

    def dma(self, out, in_, reads=(), writes=(), key=None, eng=None):
        rb = [x.b if isinstance(x, T) else x for x in reads]
        wb = [x.b if isinstance(x, T) else x for x in writes]
        if eng is None:
            eng = "sync"
        if key is None:
            dram = ("xs", "mods", "yabd")
            if wb and not wb[0].name.startswith(dram):
                key = wb[0].name
            elif rb:
                key = "st_" + rb[0].name
        return self.S.op(eng, lambda e: e.dma_start(out=out, in_=in_), rb, wb, dma=True, sem_key=key)

    def mm(self, out_t, out_ap, lhsT, rhs, reads, start, stop):
        return self.op("tensor", "matmul", reads=reads, writes=[out_t], out=out_ap, lhsT=lhsT, rhs=rhs,
                       start=start, stop=stop)


def ctile_cols(i):
    return 1 + i * 128 if i < 2 else 259 + (i - 2) * 128


PFW = 4356
GROUPS = [(0, 256)] + [(256 + 512 * i, 512) for i in range(8)]


def pf_off(tok):
    return 1 + tok if tok < 256 else 259 + (tok - 256)


def build_program(dbg=(), stop_after=None, n_layers=DEPTH):
    nc = bass.Bass("TRN2", target_bir_lowering=False)
    D = {}

    def din(name, shape, dt=F32):
        D[name] = nc.dram_tensor(name, list(shape), dt, kind="ExternalInput").ap()

    def dscr(name, shape, dt=F32):
        D[name] = nc.dram_tensor(name, list(shape), dt, kind="Internal").ap()

    def dout(name, shape, dt=F32):
        D[name] = nc.dram_tensor(name, list(shape), dt, kind="ExternalOutput").ap()

    for name, shape, dt in input_specs():
        din(name, shape, dt)
    dout("out", [SEQ, D_MODEL])
    dscr("xs", [TOK, D_MODEL])
    dscr("mods", [DEPTH, 2, 6 * D_MODEL])
    dscr("p5", [512, TOK], BF16)
    dscr("x0s", [512, TOK], BF16)
    dscr("scm", [512, TOK], BF16)
    dscr("zxs", [512, TOK], BF16)
    dscr("yss", [512, TOK], BF16)
    dscr("yab", [32, 128, 2, 512], BF16)
    for name, shape, dt in dbg_specs(dbg):
        dout(name, shape, dt)

    es = ExitStack()
    with es:
        C = Ctx(nc)
        C.D = D
        C.dbg = set(dbg)
        C.dbg_stop = stop_after
        C.psum_all = nc.alloc_psum_tensor("psall", [128, 4096], F32).ap()
        C.psum = [T(C.psum_all[:, i * 512:(i + 1) * 512], "pb%d" % i) for i in range(8)]
        C.ident_bf = C.sb("identbf", [128, 128], BF16, persist=True)
        C.ident_f = C.sb("identf", [128, 128], F32, persist=True)
        C.ones_bf = C.sb("onesbf", [128, 128], BF16, persist=True)
        C.alt_bf = C.sb("altbf", [128, 128], BF16, persist=True)
        C.altcol = C.sb("altcol", [128, 1], F32, persist=True)
        C.sel2 = C.sb("sel2", [2, 2, 128], F32, persist=True)
        C.scv = C.sb("scv", [128, 8, 2], F32, persist=True)
        C.cols = C.sb("cols", [128, 48, 2], F32, persist=True)
        C.scale1 = C.sb("scale1", [128, 8, 2], F32, persist=True)
        C.scale2 = C.sb("scale2", [128, 8, 2], F32, persist=True)
        C.ncol = C.sb("ncol", [128, DEPTH, 2, 8], F32, persist=True)
        C.dma(C.ident_bf[:], D["ident_bf"][:, :], writes=[C.ident_bf])
        C.dma(C.ident_f[:], D["ident_f"][:, :], writes=[C.ident_f])
        C.dma(C.ones_bf[:], D["ones_bf"][:, :], writes=[C.ones_bf])
        C.dma(C.alt_bf[:], D["alt_bf"][:, :], writes=[C.alt_bf])
        C.dma(C.altcol[:], D["altcol"][:, :], writes=[C.altcol])
        C.dma(C.sel2[:], D["sel2"][:, :, :], writes=[C.sel2])
        C.dma(C.ncol[:], D["ncol"][:, :, :, :], writes=[C.ncol])

        phase0(C)
        done = False
        for l in range(n_layers):
            last = l == DEPTH - 1
            for ph in (phaseP, phaseA, phaseH, phaseS, phaseC, phaseD):
                C.S.barrier()
                C.reset_arena()
                C.S.cur_scope = "L%d_%s" % (l, ph.__name__)
                ph(C, l, last)
                if stop_after == (ph.__name__, l):
                    done = True
                    break
            if done:
                break
        C.S.barrier()
        C.S.emit(es)
        import os
        if os.environ.get("KSTATS"):
            print("STATS", C.S.stats)
    return nc


def dbg_specs(dbg):
    specs = {
        "d_h": ([128, 8, TOK], BF16),
        "d_mod": ([2, 6 * D_MODEL], F32),
        "d_cols": ([128, 48, 2], F32),
        "d_utm": ([128, NT, 512], BF16),
        "d_p5": ([512, TOK], BF16),
        "d_x0": ([512, TOK], BF16),
        "d_scm": ([512, TOK], BF16),
        "d_zx": ([512, TOK], BF16),
        "d_ys": ([512, TOK], BF16),
        "d_xs": ([TOK, D_MODEL], F32),
        "d_filt": ([128, 32, 2, 512], BF16),
        "d_misc": ([128, 4, 512], F32),
        "d_s5": ([128, 6, TOK], F32),
        "d_s5p": ([128, 8, 32], F32),
    }
    return [(k, specs[k][0], specs[k][1]) for k in dbg]


def input_specs():
    L = DEPTH
    return [
        ("x_in", [SEQ, D_MODEL], F32), ("pos", [SEQ, D_MODEL], F32), ("ctx_in", [CTX, D_MODEL], F32),
        ("cc", [128, 8, 2], F32), ("ada_w", [L, D_MODEL, 6 * D_MODEL], F32), ("adab2", [L, 2, 6 * D_MODEL], F32),
        ("ncol", [128, L, 2, 8], F32), ("finalg_bc", [128, D_MODEL], F32),
        ("w_in", [L, D_MODEL, D_IN], F32), ("glu_w", [L, 512, 2048], F32), ("sc_out_w", [L, 512, 1024], F32),
        ("hy_out_w", [L, 512, 1024], F32), ("out_w", [L, 1024, 1024], F32),
        ("mlp_w1", [L, 1024, D_FF], F32), ("mlp_w2", [L, D_FF, 1024], F32),
        ("scw", [128, L, 4, 3], F32), ("hcw", [128, L, 12, 3], F32), ("hcb", [128, L, 12], F32),
        ("s5a", [128, L, 3, 32], F32),
        ("s5b", [128, L, 2, 32, 16], F32),
        ("s5c", [128, L, 2, 32, 16], F32),
        ("s5d", [128, L, 4], F32),
        ("hyw1", [L, 33, 64], F32), ("hyw2", [L, 64, 64], F32), ("hyw3", [L, 64, 64], F32),
        ("hyw4", [L, 64, 1024], F32), ("hyfb", [64, L, 4], F32),
        ("hyd_bc", [128, L, 512], F32),
        ("ident_bf", [128, 128], BF16), ("ident_f", [128, 128], F32), ("ones_bf", [128, 128], BF16),
        ("alt_bf", [128, 128], BF16), ("altcol", [128, 1], F32), ("sel2", [2, 2, 128], F32),
        ("dftc", [32, 128, 32, 128], BF16), ("dfts", [32, 128, 32, 128], BF16),
        ("dftc256", [2, 128, 2, 128], BF16), ("dfts256", [2, 128, 2, 128], BF16),
        ("zT", [33, SEQ], F32), ("zT256", [33, CTX], F32),
        ("win", [SEQ, 512], F32), ("win256", [CTX, 512], F32),
        ("wf", [128, 32], F32), ("wf256", [128, 2], F32),
        ("kpos", [128, TOK], F32),
        ("sel8", [128, 8, 8, 128], BF16), ("selT8", [128, 8, 8, 128], BF16), ("cmask", [128, 2, 128], F32),
    ]


def phase0(C):
    D = C.D
    C.bxs = [Buf("xs%d" % i) for i in range(NT)]
    C.bmods = Buf("mods")
    cc = C.sb("cc", [128, 8, 2])
    C.dma(cc[:], D["cc"][:, :, :], writes=[cc])
    C.op("scalar", "activation", reads=[cc], writes=[C.scv], out=C.scv[:], in_=cc[:], func=AF.Silu)
    for i in range(2):
        C.dma(D["xs"][i * 128:(i + 1) * 128, :], D["ctx_in"][i * 128:(i + 1) * 128, :], writes=[C.bxs[i]], key="xsst%d" % (i % 2), eng="scalar")
    xt = [C.sb("p0x%d" % i, [128, 1024]) for i in range(2)]
    pt = [C.sb("p0p%d" % i, [128, 1024]) for i in range(2)]
    for i in range(2, NT):
        a, b = xt[i % 2], pt[i % 2]
        r0 = (i - 2) * 128
        C.dma(a[:], D["x_in"][r0:r0 + 128, :], writes=[a])
        C.dma(b[:], D["pos"][r0:r0 + 128, :], writes=[b])
        C.op("vector", "tensor_tensor", reads=[a, b], writes=[a], out=a[:], in0=a[:], in1=b[:], op=ALU.add)
        C.dma(D["xs"][i * 128:(i + 1) * 128, :], a[:], reads=[a], writes=[C.bxs[i]], eng="scalar")


def phaseP(C, l, last):
    D = C.D
    adab = C.sb("adab", [2, 6144])
    modrow = C.sb("modrow", [2, 6144])
    C.dma(adab[:], D["adab2"][l], writes=[adab])
    wst = [C.sb("adaw%d" % i, [128, 8, 512]) for i in range(2)]
    wsrc = D["ada_w"][l].rearrange("(k p) c -> p k c", p=128)
    for j in range(12):
        w = wst[j % 2]
        C.dma(w[:], wsrc[:, :, j * 512:(j + 1) * 512], writes=[w])
        ps = C.psum[j % 2]
        for k in range(8):
            C.mm(ps, ps[0:2, :], lhsT=C.scv[:, k, :], rhs=w[:, k, :], reads=[C.scv, w], start=(k == 0), stop=(k == 7))
        C.op("vector", "tensor_tensor", reads=[ps, adab], writes=[modrow], out=modrow[:, j * 512:(j + 1) * 512],
             in0=ps[0:2, :], in1=adab[:, j * 512:(j + 1) * 512], op=ALU.add)
    C.dma(D["mods"][l], modrow[:], reads=[modrow], writes=[C.bmods])
    if "d_mod" in C.dbg and l == 0:
        C.dma(D["d_mod"][:, :], modrow[:], reads=[modrow], key="dbg")
    ps = C.psum[2]
    for c in range(48):
        C.mm(ps, ps[:, 2 * c:2 * c + 2], lhsT=modrow[:, c * 128:(c + 1) * 128], rhs=C.ident_f[0:2, 0:2],
             reads=[modrow, C.ident_f], start=True, stop=True)
    C.op("vector", "tensor_copy", reads=[ps], writes=[C.cols], out=C.cols[:].rearrange("p c r -> p (c r)"), in_=ps[:, 0:96])
    for r in range(2):
        C.op("vector", "scalar_tensor_tensor", reads=[C.cols, C.ncol], writes=[C.scale1], out=C.scale1[:, :, r],
             in0=C.cols[:, 8:16, r], scalar=1.0, in1=C.ncol[:, l, 0, :], op0=ALU.add, op1=ALU.mult)
        C.op("vector", "scalar_tensor_tensor", reads=[C.cols, C.ncol], writes=[C.scale2], out=C.scale2[:, :, r],
             in0=C.cols[:, 32:40, r], scalar=1.0, in1=C.ncol[:, l, 1, :], op0=ALU.add, op1=ALU.mult)
    if "d_cols" in C.dbg and l == 0:
        C.dma(D["d_cols"][:, :, :], C.cols[:], reads=[C.cols], key="dbg")


def make_norm_scratch(C, n=2):
    st = []
    for i in range(n):
        st.append(dict(ss=C.sb("nss%d" % i, [128, 1]), rs=C.sb("nrs%d" % i, [128, 1]), xn=C.sb("nxn%d" % i, [128, 1024], BF16)))
    return st


def norm_part1(C, which, xt, st, ps):
    C.op("scalar", "activation", reads=[xt], writes=[st["xn"], st["ss"]], out=st["xn"][:], in_=xt[:],
         func=AF.Square, accum_out=st["ss"][:])
    C.op("vector", "tensor_scalar", reads=[st["ss"]], writes=[st["rs"]], out=st["rs"][:], in0=st["ss"][:],
         scalar1=1.0 / D_MODEL, scalar2=EPS, op0=ALU.mult, op1=ALU.add)
    C.op("scalar", "activation", reads=[st["rs"]], writes=[st["rs"]], out=st["rs"][:], in_=st["rs"][:], func=AF.Sqrt)
    C.op("vector", "reciprocal", reads=[st["rs"]], writes=[st["rs"]], out=st["rs"][:], in_=st["rs"][:])
    C.op("vector", "tensor_scalar", reads=[xt, st["rs"]], writes=[st["xn"]], out=st["xn"][:], in0=xt[:],
         scalar1=st["rs"][:, 0:1], scalar2=None, op0=ALU.mult)
    pv = ps.ap.bitcast(BF16).rearrange("p (k t) -> p k t", k=8)
    for k in range(8):
        C.op("tensor", "transpose", reads=[st["xn"], C.ident_bf], writes=[ps], out=pv[:, k, :],
             in_=st["xn"][:, k * 128:(k + 1) * 128], identity=C.ident_bf[:])


def norm_part2(C, which, r, hT, hcol, ps):
    scale = C.scale1 if which == 0 else C.scale2
    shj = 0 if which == 0 else 3
    pv = ps.ap.bitcast(BF16).rearrange("p (k t) -> p k t", k=8)
    for k in range(8):
        if k % 2 == 0:
            C.op("vector", "tensor_scalar", reads=[ps, scale, C.cols], writes=[hT], out=hT[:, k, hcol:hcol + 128],
                 in0=pv[:, k, :], scalar1=scale[:, k, r:r + 1], scalar2=C.cols[:, shj * 8 + k, r:r + 1],
                 op0=ALU.mult, op1=ALU.add)
        else:
            C.op("scalar", "activation", reads=[ps, scale, C.cols], writes=[hT], out=hT[:, k, hcol:hcol + 128],
                 in_=pv[:, k, :], func=AF.Identity, bias=C.cols[:, shj * 8 + k, r:r + 1], scale=scale[:, k, r:r + 1])


def norm_tiles(C, which, items, sts, pss):
    n = len(items)
    for idx in range(n + 1):
        if idx < n:
            r, xt, hT, hcol = items[idx]
            norm_part1(C, which, xt, sts[idx % 2], pss[idx % 2])
        if idx >= 1:
            r, xt, hT, hcol = items[idx - 1]
            norm_part2(C, which, r, hT, hcol, pss[(idx - 1) % 2])


def norm_tile(C, which, r, xt, hT, hcol, st, ps):
    norm_part1(C, which, xt, st, ps)
    norm_part2(C, which, r, hT, hcol, ps)


def conv3(C, eng, out_t, out_ap, in_t, in_ap_fn, wcol, bias=None, n=PFW - 2):
    C.op("scalar", "activation", reads=[in_t], writes=[out_t], out=out_ap, in_=in_ap_fn(1), func=AF.Identity,
         bias=(bias if bias is not None else 0.0), scale=wcol[:, 1:2])
    for s in (0, 2):
        C.op(eng, "scalar_tensor_tensor", reads=[in_t, out_t], writes=[out_t], out=out_ap, in0=in_ap_fn(s),
             scalar=wcol[:, s:s + 1], in1=out_ap, op0=ALU.mult, op1=ALU.add)


def phaseA(C, l, last):
    D = C.D
    C.utm_off = C.top
    utm = C.utm = C.sb("utm", [128, NT, 512], BF16)
    hT = C.sb("hT", [128, 8, TOK], BF16)
    nsc = make_norm_scratch(C)
    xt = [C.sb("ax%d" % i, [128, 1024]) for i in range(2)]
    hbufs = [Buf("hTg%d" % g) for g in range(len(GROUPS))]
    items = []
    for i in range(NT):
        hTg = T(hT.ap, "hTg")
        hTg.b = hbufs[0 if i < 2 else 1 + (i - 2) // 4]
        items.append((1 if i < 2 else 0, xt[i % 2], hTg, i * 128))
    for idx in range(NT + 1):
        if idx < NT:
            a = xt[idx % 2]
            C.dma(a[:], D["xs"][idx * 128:(idx + 1) * 128, :], reads=[C.bxs[idx]], writes=[a])
            norm_part1(C, 0, a, nsc[idx % 2], C.psum[idx % 2])
        if idx >= 1:
            r_, a_, hTg_, hc_ = items[idx - 1]
            norm_part2(C, 0, r_, hTg_, hc_, C.psum[(idx - 1) % 2])
    if "d_h" in C.dbg and l == 0:
        C.dma(D["d_h"][:, :, :], hT[:], reads=hbufs, key="dbg")
    C.S.cur_scope = C.S.cur_scope.split("/")[0] + "/A2"
    wst = [C.sb("awst%d" % i, [128, 8, 128]) for i in range(2)]
    wb = [C.sb("awb%d" % i, [128, 8, 128], BF16) for i in range(2)]
    pfs = [C.sb("pf%d" % i, [128, PFW]) for i in range(2)]
    qv = C.sb("qv", [128, PFW])
    q = C.sb("q", [128, PFW - 2])
    o16 = C.sb("o16", [128, PFW - 2], BF16)
    cw = C.sb("cw", [128, 16, 3])
    cb = C.sb("cb", [128, 12])
    C.dma(cw[:, 0:4, :], D["scw"][:, l, :, :], writes=[cw])
    C.dma(cw[:, 4:16, :], D["hcw"][:, l, :, :], writes=[cw])
    C.dma(cb[:], D["hcb"][:, l, :], writes=[cb])
    for pf in pfs:
        C.op("gpsimd", "memset", writes=[pf], ap=pf[:], constant=0.0)
    C.op("gpsimd", "memset", writes=[qv], ap=qv[:], constant=0.0)
    tiles = [("s5", j, j * 128) for j in range(4)]
    for j in range(4):
        tiles += [("sc_c", j, OFF_SC + 1024 + j * 128), ("sc_x", j, OFF_SC + j * 128), ("sc_b", j, OFF_SC + 512 + j * 128)]
    for j in range(4):
        tiles += [("hy_v", j, OFF_HY + j * 128), ("hy_x1", j, OFF_HY + 512 + j * 128), ("hy_x0", j, OFF_HY + 1024 + j * 128)]
    wsrc = D["w_in"][l].rearrange("(k p) c -> p k c", p=128)
    nmm = 0
    pending = []

    class Rec:
        def op(self, *a, **kw):
            pending.append(lambda: C.op(*a, **kw))
        def dma(self, *a, **kw):
            pending.append(lambda: C.dma(*a, **kw))
    R = Rec()

    def rconv3(out_t, out_ap, in_t, in_ap_fn, wcol, bias=None):
        R.op("scalar", "activation", reads=[in_t, cw, cb], writes=[out_t], out=out_ap, in_=in_ap_fn(1), func=AF.Identity,
             bias=(bias if bias is not None else 0.0), scale=wcol[:, 1:2])
        for s_ in (0, 2):
            R.op("vector", "scalar_tensor_tensor", reads=[in_t, out_t, cw], writes=[out_t], out=out_ap, in0=in_ap_fn(s_),
                 scalar=wcol[:, s_:s_ + 1], in1=out_ap, op0=ALU.mult, op1=ALU.add)

    def flush(nmax):
        for _ in range(min(nmax, len(pending))):
            pending.pop(0)()

    for ti, (kind, j, col0) in enumerate(tiles):
        ws, w = wst[ti % 2], wb[ti % 2]
        C.dma(ws[:], wsrc[:, :, col0:col0 + 128], writes=[ws])
        C.op("gpsimd", "tensor_copy", reads=[ws], writes=[w], out=w[:], in_=ws[:])
        pf = pfs[ti % 2]
        dest = {"s5": o16, "sc_c": qv}.get(kind, pf)
        if kind in ("s5", "sc_c"):
            flush(len(pending))
        per = (len(pending) + len(GROUPS) - 1) // len(GROUPS)
        for gi, (g0, gn) in enumerate(GROUPS):
            ps = C.psum[2 + nmm % 4]
            nmm += 1
            for k in range(8):
                C.mm(ps, ps[:, 0:gn], lhsT=w[:, k, :], rhs=hT[:, k, g0:g0 + gn], reads=[w, hbufs[gi]], start=(k == 0), stop=(k == 7))
            if kind == "s5":
                c0, ncq = g0 // 8, gn // 8
                o_ap = dest[:, 0:TOK].rearrange("p (t c) -> p t c", t=8)[:, :, c0:c0 + ncq]
                i_ap = ps[:, 0:gn].rearrange("p (c t) -> p t c", t=8)
            else:
                dcol = pf_off(g0)
                o_ap = dest[:, dcol:dcol + gn]
                i_ap = ps[:, 0:gn]
            C.op("scalar", "copy", reads=[ps], writes=[dest], out=o_ap, in_=i_ap)
            flush(per)
        flush(len(pending))
        W = PFW - 2
        if kind == "s5":
            R.dma(D["p5"][j * 128:(j + 1) * 128, :], o16[:, 0:TOK], reads=[o16])
        elif kind == "sc_x":
            R.op("vector", "tensor_tensor", reads=[pf, qv], writes=[qv], out=qv[:], in0=pf[:], in1=qv[:], op=ALU.mult)
            rconv3(q, q[:, 0:W], qv, lambda s_: qv[:, s_:s_ + W], cw[:, j, :])
        elif kind == "sc_b":
            R.op("vector", "tensor_tensor", reads=[pf, q], writes=[o16], out=o16[:, 0:W], in0=q[:, 0:W], in1=pf[:, 1:1 + W], op=ALU.mult)
            R.dma(D["scm"][j * 128:(j + 1) * 128, 0:256], o16[:, 0:256], reads=[o16])
            R.dma(D["scm"][j * 128:(j + 1) * 128, 256:TOK], o16[:, 258:258 + SEQ], reads=[o16])
        elif kind == "hy_v":
            rconv3(qv, qv[:, 0:W], pf, lambda s_, pf=pf: pf[:, s_:s_ + W], cw[:, 4 + j, :], bias=cb[:, j:j + 1])
        elif kind == "hy_x1":
            rconv3(q, q[:, 0:W], pf, lambda s_, pf=pf: pf[:, s_:s_ + W], cw[:, 8 + j, :], bias=cb[:, 4 + j:5 + j])
            R.op("vector", "tensor_tensor", reads=[q, qv], writes=[o16], out=o16[:, 0:W], in0=q[:, 0:W], in1=qv[:, 0:W], op=ALU.mult)
            for i0_ in range(0, NT, 8):
                n = min(8, NT - i0_)
                ps = C.psum[6 + (i0_ // 8) % 2]
                pv = ps.ap.bitcast(BF16).rearrange("p (k t) -> p k t", k=8)
                for ii in range(n):
                    oc = ctile_cols(i0_ + ii) - 1
                    R.op("tensor", "transpose", reads=[o16, C.ident_bf], writes=[ps], out=pv[:, ii, :],
                         in_=o16[:, oc:oc + 128], identity=C.ident_bf[:])
                R.op("scalar", "copy", reads=[ps], writes=[utm], out=utm[:, i0_:i0_ + n, j * 128:(j + 1) * 128], in_=pv[:, 0:n, :])
        elif kind == "hy_x0":
            rconv3(q, q[:, 0:W], pf, lambda s_, pf=pf: pf[:, s_:s_ + W], cw[:, 12 + j, :], bias=cb[:, 8 + j:9 + j])
            R.op("gpsimd", "tensor_copy", reads=[q], writes=[o16], out=o16[:, 0:W], in_=q[:, 0:W])
            R.dma(D["x0s"][j * 128:(j + 1) * 128, 0:256], o16[:, 0:256], reads=[o16])
            R.dma(D["x0s"][j * 128:(j + 1) * 128, 256:TOK], o16[:, 258:258 + SEQ], reads=[o16])
    flush(len(pending))
    if l == 0:
        for nm, src in (("d_p5", "p5"), ("d_x0", "x0s"), ("d_scm", "scm")):
            if nm in C.dbg:
                C.S.barrier()
                C.dma(D[nm][:, :], D[src][:, :], key="dbg")
        if "d_utm" in C.dbg:
            C.dma(D["d_utm"][:, :, :], utm[:], reads=[utm], key="dbg")


def reduce_sin(C, eng, out_t, out_ap, in_t, in_ap, tmp_t, tmp_ap, ki_t, ki_ap, mul=None, add=None, reads=()):
    src_t, src_ap = in_t, in_ap
    if mul is not None or add is not None:
        C.op(eng, "tensor_scalar", reads=[in_t] + list(reads), writes=[tmp_t], out=tmp_ap, in0=in_ap,
             scalar1=(1.0 if mul is None else mul), scalar2=(0.0 if add is None else add), op0=ALU.mult, op1=ALU.add)
        src_t, src_ap = tmp_t, tmp_ap
    C.op(eng, "tensor_scalar", reads=[src_t], writes=[ki_t], out=ki_ap, in0=src_ap, scalar1=1.0 / TWO_PI, scalar2=None, op0=ALU.mult)
    C.op(eng, "scalar_tensor_tensor", reads=[ki_t, src_t], writes=[tmp_t], out=tmp_ap, in0=ki_ap, scalar=-TWO_PI, in1=src_ap,
         op0=ALU.mult, op1=ALU.add)
    C.op("scalar", "activation", reads=[tmp_t], writes=[out_t], out=out_ap, in_=tmp_ap, func=AF.Sin)


def hyena_seq(C, l, L, tile0, sfx, base_top):
    D = C.D
    nt = L // 128
    N2 = 2 * L
    C.top = base_top
    regy = C.top
    fs = C.sb("fs", [128, nt, 512], BF16)
    fd = C.sb("fd", [128, nt, 512], BF16)
    end_y = C.top
    C.top = regy
    yab = C.sb("hyab", [128, nt, 2, 512], BF16)
    C.top = max(C.top, end_y)
    regd = C.top
    dC = [C.sb("dC%d" % i, [128, nt, 128], BF16) for i in range(2)]
    dS = [C.sb("dS%d" % i, [128, nt, 128], BF16) for i in range(2)]
    end_d = C.top
    C.top = regd
    h3 = C.sb("hh3", [64, L])
    C.top = max(C.top, end_d)
    sm = {n: C.sb("h" + n, [128, 512]) for n in ("hf", "hb", "kc", "ks", "t1", "t2", "t3", "t4", "rn", "kl", "yl", "hyd", "w0", "w1")}
    afs = [C.sb("haf%d" % i, [128, 512], BF16) for i in range(2)]; abs_ = [C.sb("hab%d" % i, [128, 512], BF16) for i in range(2)]
    hfs = [C.sb("hhf%d" % i, [128, 512]) for i in range(2)]; hbs = [C.sb("hhb%d" % i, [128, 512]) for i in range(2)]
    yst = [C.sb("yst%d" % i, [128, 2, 512], BF16) for i in range(2)]
    zb = C.sb("hz", [128, 512], BF16)
    x0t = [C.sb("hx0%d" % i, [128, 4, 128], BF16) for i in range(2)]
    zxt = [C.sb("hzx%d" % i, [128, 4, 128], BF16) for i in range(2)]
    w1 = C.sb("hw1", [33, 64]); w2 = C.sb("hw2", [64, 64]); w3 = C.sb("hw3", [64, 64]); w4 = C.sb("hw4", [64, 1024])
    fb = C.sb("hfb", [64, 4]); fbias = C.sb("hfbias", [64, 3]); wfc = C.sb("hwf", [128, nt])
    tmpa = C.sb("htmpa", [64, 512]); kia = C.sb("hkia", [64, 512], I32)
    zc = [C.sb("hzc%d" % i, [33, 512]) for i in range(2)]
    ha = C.sb("hha", [64, 512]); hb_ = C.sb("hhb", [64, 512])
    C.dma(w1[:], D["hyw1"][l], writes=[w1]); C.dma(w2[:], D["hyw2"][l], writes=[w2]); C.dma(w3[:], D["hyw3"][l], writes=[w3])
    C.dma(w4[:], D["hyw4"][l], writes=[w4]); C.dma(fb[:], D["hyfb"][:, l, :], writes=[fb]); C.dma(wfc[:], D["wf" + sfx][:, :], writes=[wfc])
    C.dma(sm["hyd"][:], D["hyd_bc"][:, l, :], writes=[sm["hyd"]])
    for i in range(3):
        C.op("vector", "tensor_tensor", reads=[fb], writes=[fbias], out=fbias[:, i:i + 1], in0=fb[:, 0:1], in1=fb[:, i + 1:i + 2], op=ALU.mult)
    C.S.cur_scope = C.S.cur_scope.split("/")[0] + "/H1a"
    CH = min(512, L)
    nps = 0
    for ci, c0 in enumerate(range(0, L, CH)):
        z = zc[ci % 2]
        C.dma(z[:, 0:CH], D["zT" + sfx][:, c0:c0 + CH], writes=[z])
        for (w, src_t, src_ap, dst_t, dst_ap, bi) in ((w1, z, z[:, 0:CH], ha, ha[:, 0:CH], 0), (w2, ha, ha[:, 0:CH], hb_, hb_[:, 0:CH], 1),
                                                     (w3, hb_, hb_[:, 0:CH], h3, h3[:, c0:c0 + CH], 2)):
            ps = C.psum[nps % 2]; nps += 1
            C.mm(ps, ps[0:64, 0:CH], lhsT=w[:], rhs=src_ap, reads=[w, src_t], start=True, stop=True)
            reduce_sin(C, "vector", dst_t, dst_ap, ps, ps[0:64, 0:CH], tmpa, tmpa[:, 0:CH], kia, kia[:, 0:CH],
                       mul=fb[:, 0:1], add=fbias[:, bi:bi + 1], reads=[fb, fbias])
    C.S.cur_scope = C.S.cur_scope.split("/")[0] + "/H1b"
    pN, pK = C.psum[6], C.psum[7]
    for tt in range(nt):
        pa, pb = C.psum[2 + (tt % 2) * 2], C.psum[3 + (tt % 2) * 2]
        wt = sm["w%d" % (tt % 2)]
        af, ab = afs[tt % 2], abs_[tt % 2]
        sm["hf"], sm["hb"] = hfs[tt % 2], hbs[tt % 2]
        C.dma(wt[:], D["win" + sfx][tt * 128:(tt + 1) * 128, :], writes=[wt])
        C.mm(pa, pa[:, :], lhsT=h3[:, tt * 128:(tt + 1) * 128], rhs=w4[:, 0:512], reads=[h3, w4], start=True, stop=True)
        C.mm(pb, pb[:, :], lhsT=h3[:, tt * 128:(tt + 1) * 128], rhs=w4[:, 512:1024], reads=[h3, w4], start=True, stop=True)
        C.op("vector", "tensor_tensor", reads=[pa, wt], writes=[sm["hf"]], out=sm["hf"][:], in0=pa[:, :], in1=wt[:], op=ALU.mult)
        C.op("vector", "tensor_tensor", reads=[pb, wt], writes=[sm["hb"]], out=sm["hb"][:], in0=pb[:, :], in1=wt[:], op=ALU.mult)
        if tt == 0:
            C.op("vector", "memset", writes=[sm["hb"]], ap=sm["hb"][0:1, :], constant=0.0)
        C.op("gpsimd", "tensor_tensor", reads=[sm["hf"], sm["hb"]], writes=[fs], out=fs[:, tt, :], in0=sm["hf"][:], in1=sm["hb"][:], op=ALU.add)
        C.op("gpsimd", "tensor_tensor", reads=[sm["hf"], sm["hb"]], writes=[fd], out=fd[:, tt, :], in0=sm["hf"][:], in1=sm["hb"][:], op=ALU.subtract)
        C.op("scalar", "activation", reads=[sm["hf"]], writes=[af], out=af[:], in_=sm["hf"][:], func=AF.Abs)
        C.op("scalar", "activation", reads=[sm["hb"]], writes=[ab], out=ab[:], in_=sm["hb"][:], func=AF.Abs)
        C.mm(pN, pN[:, :], lhsT=C.ones_bf[:], rhs=af[:], reads=[C.ones_bf, af], start=(tt == 0), stop=False)
        C.mm(pN, pN[:, :], lhsT=C.ones_bf[:], rhs=ab[:], reads=[C.ones_bf, ab], start=False, stop=(tt == nt - 1))
        C.mm(pK, pK[:, :], lhsT=C.alt_bf[:], rhs=fs[:, tt, :], reads=[C.alt_bf, fs], start=(tt == 0), stop=(tt == nt - 1))
    C.op("vector", "tensor_scalar", reads=[pN], writes=[sm["rn"]], out=sm["rn"][:], in0=pN[:, :], scalar1=1e-30, scalar2=None, op0=ALU.add)
    C.op("vector", "reciprocal", reads=[sm["rn"]], writes=[sm["rn"]], out=sm["rn"][:], in_=sm["rn"][:])
    C.op("scalar", "copy", reads=[pK], writes=[sm["kl"]], out=sm["kl"][:], in_=pK[:, :])
    pU = C.psum[6]
    for ch in range(nt):
        C.mm(pU, pU[:, :], lhsT=C.alt_bf[:], rhs=C.utm[:, tile0 + ch, :], reads=[C.alt_bf, C.utm], start=(ch == 0), stop=(ch == nt - 1))
    C.op("vector", "tensor_tensor", reads=[pU, sm["kl"]], writes=[sm["yl"]], out=sm["yl"][:], in0=pU[:, :], in1=sm["kl"][:], op=ALU.mult)
    C.op("vector", "scalar_tensor_tensor", reads=[sm["yl"], sm["rn"]], writes=[sm["yl"]], out=sm["yl"][:], in0=sm["yl"][:], scalar=1.0 / N2,
         in1=sm["rn"][:], op0=ALU.mult, op1=ALU.mult)
    C.S.cur_scope = C.S.cur_scope.split("/")[0] + "/Hf"
    byab = [Buf("yabd%d" % i) for i in range(nt)]
    for ft in range(nt):
        ct, st = dC[ft % 2], dS[ft % 2]
        C.dma(ct[:], D["dftc" + sfx][ft], writes=([ct, h3] if ft < 2 else [ct]))
        C.dma(st[:], D["dfts" + sfx][ft], writes=[st])
        pb4 = [C.psum[(ft % 2) * 4 + i] for i in range(4)]
        for bi, (mat, rhs_t, rhs_fn) in enumerate(((ct, C.utm, lambda ch: C.utm[:, tile0 + ch, :]), (st, C.utm, lambda ch: C.utm[:, tile0 + ch, :]),
                                                   (ct, fs, lambda ch: fs[:, ch, :]), (st, fd, lambda ch: fd[:, ch, :]))):
            for ch in range(nt):
                C.mm(pb4[bi], pb4[bi][:, :], lhsT=mat[:, ch, :], rhs=rhs_fn(ch), reads=[mat, rhs_t], start=(ch == 0), stop=(ch == nt - 1))
        pUr, pUs, pKc, pKs = pb4
        C.op("scalar", "copy", reads=[pKc], writes=[sm["kc"]], out=sm["kc"][:], in_=pKc[:, :])
        C.op("scalar", "copy", reads=[pKs], writes=[sm["ks"]], out=sm["ks"][:], in_=pKs[:, :])
        C.op("vector", "tensor_tensor", reads=[pUr, sm["kc"]], writes=[sm["t1"]], out=sm["t1"][:], in0=pUr[:, :], in1=sm["kc"][:], op=ALU.mult)
        C.op("vector", "tensor_tensor", reads=[pUs, sm["ks"]], writes=[sm["t2"]], out=sm["t2"][:], in0=pUs[:, :], in1=sm["ks"][:], op=ALU.mult)
        C.op("vector", "tensor_tensor", reads=[pUr, sm["ks"]], writes=[sm["t3"]], out=sm["t3"][:], in0=pUr[:, :], in1=sm["ks"][:], op=ALU.mult)
        C.op("vector", "tensor_tensor", reads=[pUs, sm["kc"]], writes=[sm["t4"]], out=sm["t4"][:], in0=pUs[:, :], in1=sm["kc"][:], op=ALU.mult)
        C.op("gpsimd", "tensor_tensor", reads=[sm["t1"], sm["t2"]], writes=[sm["t1"]], out=sm["t1"][:], in0=sm["t1"][:], in1=sm["t2"][:], op=ALU.subtract)
        C.op("gpsimd", "tensor_tensor", reads=[sm["t3"], sm["t4"]], writes=[sm["t3"]], out=sm["t3"][:], in0=sm["t3"][:], in1=sm["t4"][:], op=ALU.add)
        ys = yst[ft % 2]
        for ri, tn in ((0, "t1"), (1, "t3")):
            C.op("vector", "scalar_tensor_tensor", reads=[sm[tn], sm["rn"], wfc], writes=[ys], out=ys[:, ri, :], in0=sm[tn][:], scalar=wfc[:, ft:ft + 1],
                 in1=sm["rn"][:], op0=ALU.mult, op1=ALU.mult)
        C.dma(D["yab"][ft], ys[:], reads=[ys], writes=[byab[ft]], eng="scalar")
    C.S.cur_scope = C.S.cur_scope.split("/")[0] + "/H4"
    C.dma(yab[:], D["yab"][0:nt].rearrange("f p r c -> p f r c"), reads=byab, writes=[yab, fs, fd])
    x0v = D["x0s"].rearrange("(j c) t -> c j t", c=128)
    zxv = D["zxs"].rearrange("(j c) t -> c j t", c=128)
    for tt in range(nt):
        ct, st = dC[tt % 2], dS[tt % 2]
        C.dma(ct[:], D["dftc" + sfx][tt], writes=[ct])
        C.dma(st[:], D["dfts" + sfx][tt], writes=[st])
        tok0 = (tile0 + tt) * 128
        xt = x0t[tt % 2]
        C.dma(xt[:], x0v[:, :, tok0:tok0 + 128], writes=[xt])
        acc = C.psum[tt % 2]
        for ch in range(nt):
            C.mm(acc, acc[:, :], lhsT=ct[:, ch, :], rhs=yab[:, ch, 0, :], reads=[ct, yab], start=(ch == 0), stop=False)
        for ch in range(nt):
            C.mm(acc, acc[:, :], lhsT=st[:, ch, :], rhs=yab[:, ch, 1, :], reads=[st, yab], start=False, stop=(ch == nt - 1))
        C.op("vector", "scalar_tensor_tensor", reads=[sm["yl"], C.altcol, acc], writes=[sm["hf"]], out=sm["hf"][:], in0=sm["yl"][:],
             scalar=C.altcol[:, 0:1], in1=acc[:, :], op0=ALU.mult, op1=ALU.add)
        C.op("gpsimd", "tensor_tensor", reads=[C.utm, sm["hyd"]], writes=[sm["hb"]], out=sm["hb"][:], in0=C.utm[:, tile0 + tt, :], in1=sm["hyd"][:], op=ALU.mult)
        C.op("gpsimd", "tensor_tensor", reads=[sm["hf"], sm["hb"]], writes=[zb], out=zb[:], in0=sm["hf"][:], in1=sm["hb"][:], op=ALU.add)
        pT = C.psum[2 + tt % 2]
        pv = pT.ap.bitcast(BF16).rearrange("p (k t) -> p k t", k=8)
        for j in range(4):
            C.op("tensor", "transpose", reads=[zb, C.ident_bf], writes=[pT], out=pv[:, j, :], in_=zb[:, j * 128:(j + 1) * 128], identity=C.ident_bf[:])
        zx = zxt[tt % 2]
        C.op("vector", "tensor_tensor", reads=[pT, xt], writes=[zx], out=zx[:], in0=pv[:, 0:4, :], in1=xt[:], op=ALU.mult)
        C.dma(zxv[:, :, tok0:tok0 + 128], zx[:], reads=[zx], eng="scalar")
    if "d_filt" in C.dbg and l == 0 and sfx == "":
        C.dma(D["d_filt"][:, :, 0, :], fs[:], reads=[fs], key="dbg")
        C.dma(D["d_filt"][:, :, 1, :], fd[:], reads=[fd], key="dbg")
    if "d_misc" in C.dbg and l == 0 and sfx == "":
        for i, n in enumerate(("rn", "kl", "yl", "hyd")):
            C.dma(D["d_misc"][:, i, :], sm[n][:], reads=[sm[n]], key="dbg")


def phaseH(C, l, last):
    D = C.D
    assert C.top == C.utm_off
    C.top += NT * 512 * 2
    base_top = C.top
    hyena_seq(C, l, SEQ, 2, "", base_top)
    if not last:
        C.S.barrier()
        hyena_seq(C, l, CTX, 0, "256", base_top)
    if "d_zx" in C.dbg and l == 0:
        C.S.barrier()
        C.dma(D["d_zx"][:, :], D["zxs"][:, :], key="dbg")


def cplx_outer(C, eng, out_t, out_r, out_i, Pr, Pi, Mr, Mi, reads, tmp, neg_i=False):
    sh = [128, 8, 16]
    pr, pi_ = Pr.unsqueeze(2).to_broadcast(sh), Pi.unsqueeze(2).to_broadcast(sh)
    mr, mi = Mr.unsqueeze(1).to_broadcast(sh), Mi.unsqueeze(1).to_broadcast(sh)
    ta, tb = tmp
    va = ta[:].rearrange("q (t p) -> q t p", t=8); vb = tb[:].rearrange("q (t p) -> q t p", t=8)
    orr = out_r.rearrange("q (t p) -> q t p", t=8); oi = out_i.rearrange("q (t p) -> q t p", t=8)
    C.op(eng, "tensor_tensor", reads=reads, writes=[ta], out=va, in0=pr, in1=mr, op=ALU.mult)
    C.op(eng, "tensor_tensor", reads=reads, writes=[tb], out=vb, in0=pi_, in1=mi, op=ALU.mult)
    C.op(eng, "tensor_tensor", reads=[ta, tb], writes=[out_t], out=orr, in0=va, in1=vb, op=ALU.subtract)
    C.op(eng, "tensor_tensor", reads=reads, writes=[ta], out=va, in0=pr, in1=mi, op=ALU.mult)
    C.op(eng, "tensor_tensor", reads=reads, writes=[tb], out=vb, in0=pi_, in1=mr, op=ALU.mult)
    if neg_i:
        C.op(eng, "tensor_scalar", reads=[ta], writes=[ta], out=va, in0=va, scalar1=-1.0, scalar2=None, op0=ALU.mult)
        C.op(eng, "tensor_tensor", reads=[ta, tb], writes=[out_t], out=oi, in0=va, in1=vb, op=ALU.subtract)
    else:
        C.op(eng, "tensor_tensor", reads=[ta, tb], writes=[out_t], out=oi, in0=va, in1=vb, op=ALU.add)


def phaseS(C, l, last):
    D = C.D
    NCH = TOK // 8
    HC = NCH // 2
    p5 = C.sb("s5p", [128, 4, TOK], BF16)
    C.dma(p5[:], D["p5"].rearrange("(j c) t -> c j t", c=128), writes=[p5])
    X = C.sb("s5X", [128, 32, NCH], BF16)
    Xb = [Buf("s5X%d" % g) for g in range(32)]
    sel = C.sb("s5sel", [128, 8, 8, 128], BF16)
    selT = C.sb("s5selT", [128, 8, 8, 128], BF16)
    C.dma(sel[:], D["sel8"][:, :, :, :], writes=[sel])
    C.dma(selT[:], D["selT8"][:, :, :, :], writes=[selT])
    masks = C.sb("s5mask", [128, 2, 128])
    C.dma(masks[:], D["cmask"][:, :, :], writes=[masks])
    kp = C.sb("s5kp", [128, NCH])
    C.dma(kp[:], D["kpos"][:, 0:NCH], writes=[kp])
    ysb = C.sb("s5ysb", [128, TOK], BF16)
    pa = C.sb("s5a_", [128, 3, 32]); pb = C.sb("s5b_", [128, 2, 32, 16]); pc = C.sb("s5c_", [128, 2, 32, 16]); dcol = C.sb("s5d_", [128, 4])
    C.dma(pa[:], D["s5a"][:, l], writes=[pa]); C.dma(pb[:], D["s5b"][:, l], writes=[pb]); C.dma(pc[:], D["s5c"][:, l], writes=[pc])
    C.dma(dcol[:], D["s5d"][:, l, :], writes=[dcol])
    sc = {n: C.sb("s5s" + n, [128, 32]) for n in ("dt", "ard", "ang", "mag", "sn", "cs", "abr", "abi", "den", "t0", "t1", "fre", "fim", "nfim", "phi", "rho8")}
    ski = C.sb("s5ski", [128, 32], I32)
    Bb = C.sb("s5bb", [128, 2, 32, 16])
    apw = C.sb("s5apw", [128, 2, 32, 9]); apr = C.sb("s5apr", [128, 2, 32, 9]); ang_ = C.sb("s5ang", [128, 2, 32, 8])
    V = lambda name, **kw: C.op("vector", name, **kw)
    ps = C.psum
    PS = C.psum_all
    C.op("scalar", "activation", reads=[pa], writes=[sc["dt"]], out=sc["dt"][:], in_=pa[:, 2, :], func=AF.Exp)
    V("tensor_tensor", reads=[pa, sc["dt"]], writes=[sc["ard"]], out=sc["ard"][:], in0=pa[:, 0, :], in1=sc["dt"][:], op=ALU.mult)
    V("tensor_tensor", reads=[pa, sc["dt"]], writes=[sc["ang"]], out=sc["ang"][:], in0=pa[:, 1, :], in1=sc["dt"][:], op=ALU.mult)
    for j in range(9):
        for (tab, jj, sgn) in ((apw, j, 1.0), (apr, 8 - j, 1.0), (ang_, j, -1.0)):
            if tab is ang_ and j == 8:
                continue
            C.op("scalar", "activation", reads=[sc["ard"]], writes=[sc["mag"]], out=sc["mag"][:], in_=sc["ard"][:], func=AF.Exp, scale=sgn * jj)
            reduce_sin(C, "vector", sc["sn"], sc["sn"][:], sc["ang"], sc["ang"][:], sc["t0"], sc["t0"][:], ski, ski[:], mul=float(jj), add=0.0)
            reduce_sin(C, "vector", sc["cs"], sc["cs"][:], sc["ang"], sc["ang"][:], sc["t0"], sc["t0"][:], ski, ski[:], mul=float(jj), add=math.pi / 2)
            V("tensor_tensor", reads=[sc["mag"], sc["cs"]], writes=[tab], out=tab[:, 0, :, j], in0=sc["mag"][:], in1=sc["cs"][:], op=ALU.mult)
            V("scalar_tensor_tensor", reads=[sc["mag"], sc["sn"]], writes=[tab], out=tab[:, 1, :, j], in0=sc["mag"][:], scalar=sgn, in1=sc["sn"][:],
              op0=ALU.mult, op1=ALU.mult)
    V("tensor_copy", reads=[apw], writes=[sc["abr"]], out=sc["abr"][:], in_=apw[:, 0, :, 1])
    V("tensor_copy", reads=[apw], writes=[sc["abi"]], out=sc["abi"][:], in_=apw[:, 1, :, 1])
    V("tensor_scalar", reads=[sc["ang"]], writes=[sc["phi"]], out=sc["phi"][:], in0=sc["ang"][:], scalar1=8.0, scalar2=None, op0=ALU.mult)
    C.op("scalar", "activation", reads=[sc["ard"]], writes=[sc["rho8"]], out=sc["rho8"][:], in_=sc["ard"][:], func=AF.Exp, scale=8.0)
    V("tensor_tensor", reads=[pa], writes=[sc["den"]], out=sc["den"][:], in0=pa[:, 0, :], in1=pa[:, 0, :], op=ALU.mult)
    V("tensor_tensor", reads=[pa], writes=[sc["t0"]], out=sc["t0"][:], in0=pa[:, 1, :], in1=pa[:, 1, :], op=ALU.mult)
    V("scalar_tensor_tensor", reads=[sc["den"], sc["t0"]], writes=[sc["den"]], out=sc["den"][:], in0=sc["den"][:], scalar=1e-30, in1=sc["t0"][:],
      op0=ALU.add, op1=ALU.add)
    V("reciprocal", reads=[sc["den"]], writes=[sc["den"]], out=sc["den"][:], in_=sc["den"][:])
    V("tensor_scalar", reads=[sc["abr"]], writes=[sc["abr"]], out=sc["abr"][:], in0=sc["abr"][:], scalar1=-1.0, scalar2=None, op0=ALU.add)
    V("tensor_tensor", reads=[sc["abr"], pa], writes=[sc["t0"]], out=sc["t0"][:], in0=sc["abr"][:], in1=pa[:, 0, :], op=ALU.mult)
    V("tensor_tensor", reads=[sc["abi"], pa], writes=[sc["t1"]], out=sc["t1"][:], in0=sc["abi"][:], in1=pa[:, 1, :], op=ALU.mult)
    V("tensor_tensor", reads=[sc["t0"], sc["t1"]], writes=[sc["t0"]], out=sc["t0"][:], in0=sc["t0"][:], in1=sc["t1"][:], op=ALU.add)
    V("tensor_tensor", reads=[sc["t0"], sc["den"]], writes=[sc["fre"]], out=sc["fre"][:], in0=sc["t0"][:], in1=sc["den"][:], op=ALU.mult)
    V("tensor_tensor", reads=[sc["abi"], pa], writes=[sc["t0"]], out=sc["t0"][:], in0=sc["abi"][:], in1=pa[:, 0, :], op=ALU.mult)
    V("tensor_tensor", reads=[sc["abr"], pa], writes=[sc["t1"]], out=sc["t1"][:], in0=sc["abr"][:], in1=pa[:, 1, :], op=ALU.mult)
    V("tensor_tensor", reads=[sc["t0"], sc["t1"]], writes=[sc["t0"]], out=sc["t0"][:], in0=sc["t0"][:], in1=sc["t1"][:], op=ALU.subtract)
    V("tensor_tensor", reads=[sc["t0"], sc["den"]], writes=[sc["fim"]], out=sc["fim"][:], in0=sc["t0"][:], in1=sc["den"][:], op=ALU.mult)
    V("tensor_scalar", reads=[sc["fim"]], writes=[sc["nfim"]], out=sc["nfim"][:], in0=sc["fim"][:], scalar1=-1.0, scalar2=None, op0=ALU.mult)
    for kc in range(32):
        V("tensor_scalar", reads=[pb, sc["fre"]], writes=[Bb], out=Bb[:, 0, kc, :], in0=pb[:, 0, kc, :], scalar1=sc["fre"][:, kc:kc + 1], scalar2=None, op0=ALU.mult)
        V("scalar_tensor_tensor", reads=[pb, sc["nfim"], Bb], writes=[Bb], out=Bb[:, 0, kc, :], in0=pb[:, 1, kc, :], scalar=sc["nfim"][:, kc:kc + 1],
          in1=Bb[:, 0, kc, :], op0=ALU.mult, op1=ALU.add)
        V("tensor_scalar", reads=[pb, sc["fre"]], writes=[Bb], out=Bb[:, 1, kc, :], in0=pb[:, 1, kc, :], scalar1=sc["fre"][:, kc:kc + 1], scalar2=None, op0=ALU.mult)
        V("scalar_tensor_tensor", reads=[pb, sc["fim"], Bb], writes=[Bb], out=Bb[:, 1, kc, :], in0=pb[:, 0, kc, :], scalar=sc["fim"][:, kc:kc + 1],
          in1=Bb[:, 1, kc, :], op0=ALU.mult, op1=ALU.add)
    C.S.cur_scope = C.S.cur_scope.split("/")[0] + "/Srelay"
    for g in range(32):
        ct, gl8 = divmod(g, 8)
        b0 = (g % 2) * 2
        for h in range(2):
            bank = ps[b0 + h]
            for tau in range(8):
                C.mm(bank, bank[:, 0:HC], lhsT=sel[:, gl8, tau, :], rhs=p5[:, ct, tau * NCH + h * HC:tau * NCH + (h + 1) * HC], reads=[sel, p5],
                     start=(tau == 0), stop=(tau == 7))
        src = PS[:, b0 * 512:(b0 + 2) * 512].rearrange("q (b c) -> q b c", b=2)[:, :, 0:HC]
        dst = X[:, g, :].rearrange("q (b c) -> q b c", b=2)
        if g % 2 == 0:
            C.op("scalar", "copy", reads=[ps[b0], ps[b0 + 1]], writes=[Xb[g]], out=dst, in_=src)
        else:
            C.op("vector", "tensor_copy", reads=[ps[b0], ps[b0 + 1]], writes=[Xb[g]], out=dst, in_=src)
    C.S.cur_scope = C.S.cur_scope.split("/")[0] + "/Smain"
    wst = [C.sb("s5wst%d" % i, [128, 128]) for i in range(2)]
    xm = [C.sb("s5xm%d" % i, [128, 128]) for i in range(2)]
    ym = [C.sb("s5ym%d" % i, [128, 128]) for i in range(2)]
    tmpo = [C.sb("s5tmpo%d" % i, [128, 128]) for i in range(2)]
    tmpg = [C.sb("s5tmpg%d" % i, [128, 128]) for i in range(2)]
    Rm = [[C.sb("s5R%d%d" % (k, i), [128, 128], BF16) for i in range(2)] for k in range(2)]
    mstT = [C.sb("s5mstT%d" % i, [128, 128], BF16) for i in range(2)]
    mint = [[C.sb("s5mint%d%d" % (k, g2), [128, 128], BF16) for g2 in range(2)] for k in range(2)]
    Vs = [C.sb("s5Vs%d" % i, [128, NCH]) for i in range(2)]
    Vp = [C.sb("s5Vp%d" % i, [128, NCH]) for i in range(2)]
    Et = [C.sb("s5Et%d" % i, [128, NCH]) for i in range(2)]
    rh = C.sb("s5rh", [128, NCH])
    tq = [C.sb("s5tq%d" % i, [128, NCH]) for i in range(4)]
    kiq = C.sb("s5kiq", [128, NCH], I32)
    Sp = [[C.sb("s5Sp%d%d" % (k, i), [128, NCH], BF16) for i in range(2)] for k in range(2)]
    for k in range(2):
        for i in range(2):
            C.op("gpsimd", "memset", writes=[Sp[k][i]], ap=Sp[k][i][:], constant=0.0)
    ga = C.sb("s5ga", [128, 512]); gb = C.sb("s5gb", [128, 512]); gc = C.sb("s5gc", [128, 512])
    G = lambda name, **kw: C.op("gpsimd", name, **kw)
    for gp in range(16):
        ct = gp // 4
        for k in range(2):
            kc = k * 16 + gp
            Br, Bi = Bb[:, 0, kc, :], Bb[:, 1, kc, :]
            Cr, Ci = pc[:, 0, kc, :], pc[:, 1, kc, :]
            P = lambda tab, a, b: (tab[:, 0, kc, a:b], tab[:, 1, kc, a:b])
            if k == 0:
                pw_st, pw_x, pw_y, pw_r = P(apr, 1, 9), P(ang_, 0, 8), P(apw, 0, 8), P(apw, 1, 9)
            else:
                pw_st, pw_x, pw_y, pw_r = P(apw, 0, 8), P(apw, 0, 8), P(ang_, 0, 8), P(apr, 0, 8)
            cplx_outer(C, "vector", wst[0], wst[0][:], wst[1][:], pw_st[0], pw_st[1], Br, Bi, [apw, apr, ang_, Bb], tmpo)
            wst[1].b = wst[0].b
            if k == 0:
                cplx_outer(C, "vector", xm[0], xm[0][:], xm[1][:], pw_x[0], pw_x[1], Br, Bi, [apw, apr, ang_, Bb], tmpo)
                xm[1].b = xm[0].b
                xsrc = xm
            else:
                xsrc = wst
            cplx_outer(C, "gpsimd", ym[0], ym[0][:], ym[1][:], pw_y[0], pw_y[1], Cr, Ci, [apw, apr, ang_, pc], tmpg, neg_i=True)
            ym[1].b = ym[0].b
            cplx_outer(C, "gpsimd", Rm[k][0], Rm[k][0][:], Rm[k][1][:], pw_r[0], pw_r[1], Cr, Ci, [apw, apr, ang_, pc], tmpg, neg_i=True)
            Rm[k][1].b = Rm[k][0].b
            for ri in range(2):
                C.mm(ps[6], ps[6][:, ri * 128:(ri + 1) * 128], lhsT=wst[ri][:], rhs=C.ident_f[:], reads=[wst[0], C.ident_f], start=True, stop=True)
            C.op("scalar", "copy", reads=[ps[6]], writes=[mstT[0]], out=mstT[0][:], in_=ps[6][:, 0:128])
            C.op("scalar", "copy", reads=[ps[6]], writes=[mstT[1]], out=mstT[1][:], in_=ps[6][:, 128:256])
            for g2 in range(2):
                hs = slice(g2 * 64, (g2 + 1) * 64)
                o = ps[7][:, g2 * 128:(g2 + 1) * 128]
                C.mm(ps[7], o, lhsT=xsrc[0][hs, :], rhs=ym[0][hs, :], reads=[xsrc[0], ym[0]], start=True, stop=False)
                C.mm(ps[7], o, lhsT=xsrc[1][hs, :], rhs=ym[1][hs, :], reads=[xsrc[0], ym[0]], start=False, stop=True)
                V("tensor_tensor", reads=[ps[7], masks], writes=[mint[k][g2]], out=mint[k][g2][:], in0=o, in1=masks[:, k, :], op=ALU.mult)
            for ri in range(2):
                for g2 in range(2):
                    g = 2 * gp + g2
                    for h in range(2):
                        bank = ps[ri * 2 + h]
                        C.mm(bank, bank[g2 * 64:(g2 + 1) * 64, 0:HC], lhsT=mstT[ri][:, g2 * 64:(g2 + 1) * 64], rhs=X[:, g, h * HC:(h + 1) * HC],
                             reads=[mstT[ri], Xb[g]], start=True, stop=True)
                src = PS[:, ri * 1024:(ri + 1) * 1024].rearrange("q (b c) -> q b c", b=2)[:, :, 0:HC]
                C.op("scalar", "copy", reads=[ps[ri * 2], ps[ri * 2 + 1]], writes=[Vs[ri]], out=Vs[ri][:].rearrange("q (b c) -> q b c", b=2), in_=src)
            reduce_sin(C, "vector", Et[0], Et[0][:], kp, kp[:], tq[0], tq[0][:], kiq, kiq[:], mul=sc["phi"][:, kc:kc + 1], add=0.0, reads=[sc["phi"]])
            reduce_sin(C, "vector", Et[1], Et[1][:], kp, kp[:], tq[0], tq[0][:], kiq, kiq[:], mul=sc["phi"][:, kc:kc + 1], add=math.pi / 2, reads=[sc["phi"]])
            V("tensor_scalar", reads=[kp, sc["rho8"]], writes=[rh], out=rh[:], in0=kp[:], scalar1=0.0, scalar2=sc["rho8"][:, kc:kc + 1], op0=ALU.mult, op1=ALU.add)
            if k == 0:
                segs_in = [(slice(0, NCH), slice(0, NCH))]
                segs_out = [(slice(0, NCH - 1), slice(1, NCH))]
            else:
                segs_in = [(slice(0, 32), slice(31, None, -1)), (slice(32, NCH), slice(NCH - 1, 31, -1))]
                segs_out = [(slice(0, 31), slice(30, None, -1)), (slice(31, NCH - 1), slice(NCH - 1, 31, -1))]
            sn, cs = Et[0], Et[1]
            for (pp, cc) in segs_in:
                n = pp.stop - pp.start
                V("tensor_tensor", reads=[Vs[0], cs], writes=[tq[0]], out=tq[0][:, pp], in0=Vs[0][:, cc], in1=cs[:, pp], op=ALU.mult)
                V("tensor_tensor", reads=[Vs[1], sn], writes=[tq[1]], out=tq[1][:, pp], in0=Vs[1][:, cc], in1=sn[:, pp], op=ALU.mult)
                V("tensor_tensor", reads=[tq[0], tq[1]], writes=[Vp[0]], out=Vp[0][:, pp], in0=tq[0][:, pp], in1=tq[1][:, pp], op=ALU.add)
                G("tensor_tensor", reads=[Vs[1], cs], writes=[tq[2]], out=tq[2][:, pp], in0=Vs[1][:, cc], in1=cs[:, pp], op=ALU.mult)
                G("tensor_tensor", reads=[Vs[0], sn], writes=[tq[3]], out=tq[3][:, pp], in0=Vs[0][:, cc], in1=sn[:, pp], op=ALU.mult)
                G("tensor_tensor", reads=[tq[2], tq[3]], writes=[Vp[1]], out=Vp[1][:, pp], in0=tq[2][:, pp], in1=tq[3][:, pp], op=ALU.subtract)
            V("tensor_tensor_scan", reads=[rh, Vp[0]], writes=[Vp[0]], out=Vp[0][:], data0=rh[:], data1=Vp[0][:], initial=0.0, op0=ALU.mult, op1=ALU.add)
            V("tensor_tensor_scan", reads=[rh, Vp[1]], writes=[Vp[1]], out=Vp[1][:], data0=rh[:], data1=Vp[1][:], initial=0.0, op0=ALU.mult, op1=ALU.add)
            for (pp, cc) in segs_out:
                V("tensor_tensor", reads=[Vp[0], cs], writes=[tq[0]], out=tq[0][:, pp], in0=Vp[0][:, pp], in1=cs[:, pp], op=ALU.mult)
                V("tensor_tensor", reads=[Vp[1], sn], writes=[tq[1]], out=tq[1][:, pp], in0=Vp[1][:, pp], in1=sn[:, pp], op=ALU.mult)
                V("tensor_tensor", reads=[tq[0], tq[1]], writes=[Sp[k][0]], out=Sp[k][0][:, cc], in0=tq[0][:, pp], in1=tq[1][:, pp], op=ALU.subtract)
                G("tensor_tensor", reads=[Vp[1], cs], writes=[tq[2]], out=tq[2][:, pp], in0=Vp[1][:, pp], in1=cs[:, pp], op=ALU.mult)
                G("tensor_tensor", reads=[Vp[0], sn], writes=[tq[3]], out=tq[3][:, pp], in0=Vp[0][:, pp], in1=sn[:, pp], op=ALU.mult)
                G("tensor_tensor", reads=[tq[2], tq[3]], writes=[Sp[k][1]], out=Sp[k][1][:, cc], in0=tq[2][:, pp], in1=tq[3][:, pp], op=ALU.add)
        for g2 in range(2):
            g = 2 * gp + g2
            hs = slice(g2 * 64, (g2 + 1) * 64)
            for h in range(2):
                bank = ps[4 + h]
                cs_ = slice(h * HC, (h + 1) * HC)
                for k in range(2):
                    C.mm(bank, bank[:, 0:HC], lhsT=mint[k][g2][:], rhs=X[:, g, cs_], reads=[mint[k][g2], Xb[g]], start=(k == 0), stop=False)
                    C.mm(bank, bank[:, 0:HC], lhsT=Rm[k][0][hs, :], rhs=Sp[k][0][hs, cs_], reads=[Rm[k][0], Sp[k][0]], start=False, stop=False)
                    C.mm(bank, bank[:, 0:HC], lhsT=Rm[k][1][hs, :], rhs=Sp[k][1][hs, cs_], reads=[Rm[k][0], Sp[k][1]], start=False, stop=(k == 1))
            src = PS[:, 4 * 512:6 * 512].rearrange("q (b c) -> q b c", b=2)[:, :, 0:HC]
            C.op("scalar", "copy", reads=[ps[4], ps[5]], writes=[Xb[g]], out=X[:, g, :].rearrange("q (b c) -> q b c", b=2), in_=src)
    C.S.cur_scope = C.S.cur_scope.split("/")[0] + "/Sout"
    nb = 0
    for ct in range(4):
        for t0 in range(0, TOK, 512):
            bank = ps[nb % 4]; nb += 1
            c0 = t0 // 8
            nt_ = min(512, TOK - t0)
            ncq = nt_ // 8
            for tau in range(8):
                for gl8 in range(8):
                    g = ct * 8 + gl8
                    C.mm(bank, bank[:, tau:nt_:8], lhsT=selT[:, gl8, tau, :], rhs=X[:, g, c0:c0 + ncq], reads=[selT, Xb[g]], start=(gl8 == 0), stop=(gl8 == 7))
            cs_ = slice(t0, t0 + nt_)
            w_ = slice(0, nt_)
            u_ap = p5[:, ct, :].rearrange("p (t c) -> p c t", t=8)[:, c0:c0 + ncq, :]
            V("scalar_tensor_tensor", reads=[p5, dcol, bank], writes=[ga], out=ga[:, w_].rearrange("p (c t) -> p c t", t=8), in0=u_ap,
              scalar=dcol[:, ct:ct + 1], in1=bank[:, w_].rearrange("p (c t) -> p c t", t=8), op0=ALU.mult, op1=ALU.add)
            G("tensor_tensor", reads=[ga], writes=[gb], out=gb[:, w_], in0=ga[:, w_], in1=ga[:, w_], op=ALU.mult)
            G("tensor_scalar", reads=[gb], writes=[gb], out=gb[:, w_], in0=gb[:, w_], scalar1=0.044715, scalar2=1.0, op0=ALU.mult, op1=ALU.add)
            G("tensor_tensor", reads=[gb, ga], writes=[gb], out=gb[:, w_], in0=gb[:, w_], in1=ga[:, w_], op=ALU.mult)
            C.op("scalar", "activation", reads=[gb], writes=[gc], out=gc[:, w_], in_=gb[:, w_], func=AF.Tanh, scale=0.7978845608028654)
            C.op("scalar", "mul", reads=[ga], writes=[ga], out=ga[:, w_], in_=ga[:, w_], mul=0.5)
            V("scalar_tensor_tensor", reads=[gc, ga], writes=[ysb], out=ysb[:, cs_], in0=gc[:, w_], scalar=1.0, in1=ga[:, w_], op0=ALU.add, op1=ALU.mult)
        C.dma(D["yss"][ct * 128:(ct + 1) * 128, :], ysb[:], reads=[ysb], eng="scalar")
    if "d_ys" in C.dbg and l == 0:
        C.S.barrier()
        C.dma(D["d_ys"][:, :], D["yss"][:, :], key="dbg")


def load_weight_bf16(C, dst, src, nk, ncols, stg, col0=0, blk=None, cnt=[0]):
    blk = blk or stg[0].ap.shape[1]
    for k in range(nk):
        for c0 in range(0, ncols, blk):
            n = min(blk, ncols - c0)
            st = stg[cnt[0] % len(stg)]
            C.dma(st[:, 0:n], src[k * 128:(k + 1) * 128, col0 + c0:col0 + c0 + n], writes=[st])
            e = cnt[0] % 3
            if e == 0:
                C.op("gpsimd", "tensor_copy", reads=[st], writes=[dst], out=dst[:, k, c0:c0 + n], in_=st[:, 0:n])
            elif e == 1:
                C.op("scalar", "copy", reads=[st], writes=[dst], out=dst[:, k, c0:c0 + n], in_=st[:, 0:n])
            else:
                C.op("vector", "tensor_copy", reads=[st], writes=[dst], out=dst[:, k, c0:c0 + n], in_=st[:, 0:n])
            cnt[0] += 1


def bcast_mod(C, l, j, dst):
    row = C.sb("bcrow", [2, 1024])
    C.dma(row[:], C.D["mods"][l, :, j * 1024:(j + 1) * 1024], reads=[C.bmods], writes=[row])
    for r in range(2):
        for half in range(2):
            ps = C.psum[r * 2 + half]
            C.mm(ps, ps[:, :], lhsT=C.sel2[:, r, :], rhs=row[:, half * 512:(half + 1) * 512], reads=[C.sel2, row], start=True, stop=True)
            C.op("vector", "tensor_copy", reads=[ps], writes=[dst], out=dst[:, r, half * 512:(half + 1) * 512], in_=ps[:, :])


def phaseC(C, l, last):
    D = C.D
    WG = C.sb("cWG", [128, 8, 3072], BF16)
    glu = C.sb("cglu", [128, 4, 2048], BF16)
    scow = C.sb("cscow", [128, 4, 1024], BF16)
    hyow = C.sb("chyow", [128, 4, 1024], BF16)
    outw = C.sb("coutw", [128, 8, 1024], BF16)
    stg = [C.sb("cstg%d" % i, [128, 512]) for i in range(2)]
    g1bc = C.sb("cg1bc", [128, 2, 1024])
    xts = [C.sb("cxt%d" % i, [128, 4, 1024]) for i in range(2)]
    xtbs = [[Buf("cxt%d_%d" % (s_, i)) for i in range(4)] for s_ in range(2)]
    hTs = [C.sb("chT%d" % i, [128, 8, 512], BF16) for i in range(2)]
    ysg = C.sb("cysg", [128, 4, 512], BF16)
    scg = C.sb("cscg", [128, 4, 512], BF16)
    zxg = C.sb("czxg", [128, 4, 512], BF16)
    m = C.sb("cm", [128, 8, 512], BF16)
    nsc = make_norm_scratch(C)
    sg = [C.sb("csg%d" % i, [128, 512]) for i in range(2)]
    t1 = [C.sb("ct1%d" % i, [128, 512]) for i in range(2)]
    macc = [C.sb("cmacc%d" % i, [128, 512]) for i in range(2)]
    tmpx = [C.sb("ctmpx%d" % i, [128, 512]) for i in range(2)]
    bcast_mod(C, l, 2, g1bc)
    load_weight_bf16(C, WG, D["w_in"][l], 8, 3072, stg, col0=OFF_GATE)
    load_weight_bf16(C, glu, D["glu_w"][l], 4, 2048, stg)
    load_weight_bf16(C, scow, D["sc_out_w"][l], 4, 1024, stg)
    load_weight_bf16(C, hyow, D["hy_out_w"][l], 4, 1024, stg)
    load_weight_bf16(C, outw, D["out_w"][l], 8, 1024, stg)
    C.S.cur_scope = C.S.cur_scope.split("/")[0] + "/Cmain"
    ysv = D["yss"].rearrange("(j c) t -> c j t", c=128)
    scv = D["scm"].rearrange("(j c) t -> c j t", c=128)
    zxv = D["zxs"].rearrange("(j c) t -> c j t", c=128)
    groups = GROUPS[1:] if last else GROUPS
    cnt = 0
    ring = [0]
    def prep(n):
        g0, gn = groups[n]
        r = 1 if g0 < 256 else 0
        xt, xtb, hT = xts[n % 2], xtbs[n % 2], hTs[n % 2]
        items = []
        for ti in range(gn // 128):
            i = g0 // 128 + ti
            xti = T(xt[:, ti, :], "x"); xti.b = xtb[ti]
            C.dma(xt[:, ti, :], D["xs"][i * 128:(i + 1) * 128, :], reads=[C.bxs[i]], writes=[xtb[ti]])
            items.append((r, xti, hT, ti * 128))
        norm_tiles(C, 0, items, nsc, [C.psum[6], C.psum[7]])
        C.dma(ysg[:, :, 0:gn], ysv[:, :, g0:g0 + gn], writes=[ysg])
        C.dma(scg[:, :, 0:gn], scv[:, :, g0:g0 + gn], writes=[scg])
        C.dma(zxg[:, :, 0:gn], zxv[:, :, g0:g0 + gn], writes=[zxg])

    prep(0)
    for gidx, (g0, gn) in enumerate(groups):
        r = 1 if g0 < 256 else 0
        ntl = gn // 128
        xt, xtb, hT = xts[gidx % 2], xtbs[gidx % 2], hTs[gidx % 2]
        for j in range(8):
            js = slice(j * 128, (j + 1) * 128)
            mc = macc[j % 2]
            stages = [
                [(glu, ysg, lambda k: glu[:, k, js], 4), (glu, ysg, lambda k: glu[:, k, 1024 + j * 128:1024 + (j + 1) * 128], 4),
                 (WG, hT, lambda k: WG[:, k, j * 128:(j + 1) * 128], 8)],
                [(scow, scg, lambda k: scow[:, k, js], 4), (WG, hT, lambda k: WG[:, k, 1024 + j * 128:1024 + (j + 1) * 128], 8)],
                [(hyow, zxg, lambda k: hyow[:, k, js], 4), (WG, hT, lambda k: WG[:, k, 2048 + j * 128:2048 + (j + 1) * 128], 8)],
            ]
            for si, stage in enumerate(stages):
                bk = []
                for (wt_, rt_, lfn, nk) in stage:
                    b_ = C.psum[ring[0] % 8]; ring[0] += 1
                    bk.append(b_)
                    for k in range(nk):
                        C.mm(b_, b_[:, 0:gn], lhsT=lfn(k), rhs=rt_[:, k, 0:gn], reads=[wt_, rt_], start=(k == 0), stop=(k == nk - 1))
                sgt, tt = sg[cnt % 2], t1[cnt % 2]; cnt += 1
                if si == 0:
                    C.op("scalar", "activation", reads=[bk[1]], writes=[sgt], out=sgt[:, 0:gn], in_=bk[1][:, 0:gn], func=AF.Sigmoid)
                    C.op("vector", "tensor_tensor", reads=[bk[0], sgt], writes=[mc], out=mc[:, 0:gn], in0=bk[0][:, 0:gn], in1=sgt[:, 0:gn], op=ALU.mult)
                    sg2 = sg[cnt % 2]; cnt += 1
                    C.op("scalar", "activation", reads=[bk[2]], writes=[sg2], out=sg2[:, 0:gn], in_=bk[2][:, 0:gn], func=AF.Sigmoid)
                    C.op("gpsimd", "tensor_tensor", reads=[mc, sg2], writes=[mc], out=mc[:, 0:gn], in0=mc[:, 0:gn], in1=sg2[:, 0:gn], op=ALU.mult)
                else:
                    C.op("scalar", "activation", reads=[bk[1]], writes=[sgt], out=sgt[:, 0:gn], in_=bk[1][:, 0:gn], func=AF.Sigmoid)
                    C.op("vector", "tensor_tensor", reads=[bk[0], sgt], writes=[tt], out=tt[:, 0:gn], in0=bk[0][:, 0:gn], in1=sgt[:, 0:gn], op=ALU.mult)
                    if si == 1:
                        C.op("gpsimd", "tensor_tensor", reads=[mc, tt], writes=[mc], out=mc[:, 0:gn], in0=mc[:, 0:gn], in1=tt[:, 0:gn], op=ALU.add)
                    else:
                        C.op("gpsimd", "tensor_tensor", reads=[mc, tt], writes=[m], out=m[:, j, 0:gn], in0=mc[:, 0:gn], in1=tt[:, 0:gn], op=ALU.add)
        if gidx + 1 < len(groups):
            prep(gidx + 1)
        for ti in range(ntl):
            i = g0 // 128 + ti
            for half in range(2):
                ps = C.psum[ring[0] % 8]; ring[0] += 1
                hs = slice(half * 512, (half + 1) * 512)
                for k in range(8):
                    C.mm(ps, ps[:, :], lhsT=m[:, k, ti * 128:(ti + 1) * 128], rhs=outw[:, k, hs], reads=[m, outw], start=(k == 0), stop=(k == 7))
                tx = tmpx[half]
                C.op("vector", "tensor_tensor", reads=[ps, g1bc], writes=[tx], out=tx[:], in0=ps[:, :], in1=g1bc[:, r, hs], op=ALU.mult)
                C.op("gpsimd", "tensor_tensor", reads=[tx, xtb[ti]], writes=[xtb[ti]], out=xt[:, ti, hs], in0=xt[:, ti, hs], in1=tx[:], op=ALU.add)
            C.dma(D["xs"][i * 128:(i + 1) * 128, :], xt[:, ti, :], reads=[xtb[ti]], writes=[C.bxs[i]], eng="scalar")
    if "d_xs" in C.dbg and C.dbg_stop == ("phaseC", l):
        C.S.barrier()
        C.dma(D["d_xs"][:, :], D["xs"][:, :], key="dbg")


def phaseD(C, l, last):
    D = C.D
    w1b = C.sb("dw1b", [128, 8, 4096], BF16)
    w2b = C.sb("dw2b", [128, 32, 1024], BF16)
    stg = [C.sb("dstg%d" % i, [128, 512]) for i in range(2)]
    g2bc = C.sb("dg2bc", [128, 2, 1024])
    xts = [C.sb("dxt%d" % i, [128, 2, 1024]) for i in range(2)]
    xtbs = [[Buf("dxt%d_%d" % (s_, i)) for i in range(2)] for s_ in range(2)]
    h2s = [C.sb("dh2%d" % i, [128, 8, 256], BF16) for i in range(2)]
    rr_ = C.sb("dr", [128, 32, 256], BF16)
    nsc = make_norm_scratch(C)
    rt = [C.sb("drt%d" % i, [128, 256]) for i in range(2)]
    tmpx = [C.sb("dtmpx%d" % i, [128, 512]) for i in range(2)]
    if last:
        fg = C.sb("dfg", [128, 1024]); fo = C.sb("dfo", [128, 1024])
        fss = C.sb("dfss", [128, 1]); frs = C.sb("dfrs", [128, 1])
        C.dma(fg[:], D["finalg_bc"][:, :], writes=[fg])
    bcast_mod(C, l, 5, g2bc)
    load_weight_bf16(C, w1b, D["mlp_w1"][l], 8, 4096, stg)
    load_weight_bf16(C, w2b, D["mlp_w2"][l], 32, 1024, stg)
    C.S.cur_scope = C.S.cur_scope.split("/")[0] + "/Dmain"
    glist = list(range(1 if last else 0, NT // 2))

    def prep(n):
        gi = glist[n]
        r = 1 if gi == 0 else 0
        xt, xtb, h2 = xts[n % 2], xtbs[n % 2], h2s[n % 2]
        items = []
        for ti in range(2):
            i = gi * 2 + ti
            xti = T(xt[:, ti, :], "x"); xti.b = xtb[ti]
            C.dma(xt[:, ti, :], D["xs"][i * 128:(i + 1) * 128, :], reads=[C.bxs[i]], writes=[xtb[ti]])
            items.append((r, xti, h2, ti * 128))
        norm_tiles(C, 1, items, nsc, [C.psum[6], C.psum[7]])

    prep(0)
    for n, gi in enumerate(glist):
        r = 1 if gi == 0 else 0
        xt, xtb, h2 = xts[n % 2], xtbs[n % 2], h2s[n % 2]
        for i in range(32):
            ps = C.psum[i % 4]
            for k in range(8):
                C.mm(ps, ps[:, 0:256], lhsT=w1b[:, k, i * 128:(i + 1) * 128], rhs=h2[:, k, :], reads=[w1b, h2], start=(k == 0), stop=(k == 7))
            rtt = rt[i % 2]
            C.op("scalar", "activation", reads=[ps], writes=[rtt], out=rtt[:], in_=ps[:, 0:256], func=AF.Relu)
            C.op("gpsimd", "tensor_tensor", reads=[rtt], writes=[rr_], out=rr_[:, i, :], in0=rtt[:], in1=rtt[:], op=ALU.mult)
        if n + 1 < len(glist):
            prep(n + 1)
        for ti in range(2):
            i = gi * 2 + ti
            for half in range(2):
                ps = C.psum[4 + (ti * 2 + half) % 2]
                hs = slice(half * 512, (half + 1) * 512)
                for k in range(32):
                    C.mm(ps, ps[:, :], lhsT=rr_[:, k, ti * 128:(ti + 1) * 128], rhs=w2b[:, k, hs], reads=[rr_, w2b], start=(k == 0), stop=(k == 31))
                tx = tmpx[half]
                C.op("vector", "tensor_tensor", reads=[ps, g2bc], writes=[tx], out=tx[:], in0=ps[:, :], in1=g2bc[:, r, hs], op=ALU.mult)
                C.op("gpsimd", "tensor_tensor", reads=[tx, xtb[ti]], writes=[xtb[ti]], out=xt[:, ti, hs], in0=xt[:, ti, hs], in1=tx[:], op=ALU.add)
            if not last:
                C.dma(D["xs"][i * 128:(i + 1) * 128, :], xt[:, ti, :], reads=[xtb[ti]], writes=[C.bxs[i]], eng="scalar")
            else:
                if "d_xs" in C.dbg:
                    C.dma(D["xs"][i * 128:(i + 1) * 128, :], xt[:, ti, :], reads=[xtb[ti]], writes=[C.bxs[i]], eng="scalar")
                C.op("scalar", "activation", reads=[xtb[ti]], writes=[fo, fss], out=fo[:], in_=xt[:, ti, :], func=AF.Square, accum_out=fss[:])
                C.op("vector", "tensor_scalar", reads=[fss], writes=[frs], out=frs[:], in0=fss[:], scalar1=1.0 / D_MODEL, scalar2=EPS, op0=ALU.mult, op1=ALU.add)
                C.op("scalar", "activation", reads=[frs], writes=[frs], out=frs[:], in_=frs[:], func=AF.Sqrt)
                C.op("vector", "reciprocal", reads=[frs], writes=[frs], out=frs[:], in_=frs[:])
                C.op("vector", "scalar_tensor_tensor", reads=[xtb[ti], frs, fg], writes=[fo], out=fo[:], in0=xt[:, ti, :], scalar=frs[:, 0:1], in1=fg[:],
                     op0=ALU.mult, op1=ALU.mult)
                C.dma(D["out"][(i - 2) * 128:(i - 1) * 128, :], fo[:], reads=[fo], eng="scalar")
    if "d_xs" in C.dbg and C.dbg_stop == ("phaseD", l):
        C.S.barrier()
        C.dma(D["d_xs"][:, :], D["xs"][:, :], key="dbg")


_CONST = {}


def _consts():
    if _CONST:
        return _CONST
    bf = ml_dtypes.bfloat16
    c = _CONST
    c["ident_bf"] = np.eye(128, dtype=np.float32).astype(bf)
    c["ident_f"] = np.eye(128, dtype=np.float32)
    c["ones_bf"] = np.ones((128, 128), np.float32).astype(bf)
    alt = (1.0 - 2.0 * (np.arange(128) % 2)).astype(np.float32)
    c["alt_bf"] = np.repeat(alt[:, None], 128, axis=1).astype(bf)
    c["altcol"] = alt[:, None].copy()
    sel = np.zeros((2, 2, 128), np.float32); sel[0, 0] = 1; sel[1, 1] = 1
    c["sel2"] = sel
    for L, sfx in ((SEQ, ""), (CTX, "256")):
        N = 2 * L
        nt = L // 128
        t = (np.arange(nt)[None, :, None, None] * 128 + np.arange(128)[None, None, :, None]).astype(np.int64)
        f = (np.arange(nt)[:, None, None, None] * 128 + np.arange(128)[None, None, None, :]).astype(np.int64)
        ph = ((t * f) % N).astype(np.float64) * (2.0 * np.pi / N)
        c["dftc" + sfx] = np.ascontiguousarray(np.cos(ph).transpose(0, 2, 1, 3)).astype(np.float32).astype(bf)
        c["dfts" + sfx] = np.ascontiguousarray(np.sin(ph).transpose(0, 2, 1, 3)).astype(np.float32).astype(bf)
        tl = np.linspace(0.0, 1.0, L, dtype=np.float32)[:, None]
        ang = (2.0 * np.float32(math.pi) * np.arange(L, dtype=np.float32)[:, None] / np.float32(L)).astype(np.float32)
        bands = np.linspace(1e-4, 15, 16, dtype=np.float32)[None, :]
        z = np.concatenate([tl, np.cos(bands * ang), -np.sin(bands * ang)], axis=-1).astype(np.float32)
        c["zT" + sfx] = np.ascontiguousarray(z.T)
        mx = math.log(1e-2) / 0.3; mn = math.log(1e-2) / 1.5
        deltas = np.abs(np.linspace(mn, mx, 512, dtype=np.float32))
        c["win" + sfx] = (np.exp(-tl * deltas[None, :]) + np.float32(0.05)).astype(np.float32)
        fidx = np.arange(nt)[None, :] * 128 + np.arange(128)[:, None]
        c["wf" + sfx] = np.where(fidx == 0, 1.0 / N, 2.0 / N).astype(np.float32)
    c["kpos"] = np.repeat(np.arange(TOK, dtype=np.float32)[None, :], 128, axis=0)
    sel = np.zeros((128, 8, 8, 128), np.float32)
    for g in range(8):
        for tau in range(8):
            for p in range(16):
                sel[g * 16 + p, g, tau, tau * 16 + p] = 1.0
    c["sel8"] = sel.astype(bf)
    c["selT8"] = np.ascontiguousarray(sel.transpose(3, 1, 2, 0)).astype(bf)
    ti = np.arange(128)[:, None] // 16; to = np.arange(128)[None, :] // 16
    c["cmask"] = np.ascontiguousarray(np.stack([(to >= ti), (to <= ti)], axis=1)).astype(np.float32)
    quarter = D_MODEL // 4
    omega = (1.0 / (10000.0 ** (np.arange(quarter, dtype=np.float32) / np.float32(quarter)))).astype(np.float32)
    rows = SEQ // 64
    er = np.arange(rows, dtype=np.float32)[:, None] * omega[None]
    ec = np.arange(64, dtype=np.float32)[:, None] * omega[None]
    er = np.concatenate([np.sin(er), np.cos(er)], -1); ec = np.concatenate([np.sin(ec), np.cos(ec)], -1)
    emb = np.concatenate([np.broadcast_to(er[:, None, :], (rows, 64, 512)), np.broadcast_to(ec[None, :, :], (rows, 64, 512))], -1)
    c["pos"] = np.ascontiguousarray(emb.reshape(SEQ, D_MODEL)).astype(np.float32)
    return c


def _colmajor(v, nk):
    v = np.asarray(v, np.float32)
    r = v.reshape(v.shape[:-1] + (nk, 128))
    return np.ascontiguousarray(np.moveaxis(r, -1, 0))


def prep_shared(inp):
    L = DEPTH
    sh = dict(_consts())
    f32 = lambda a: np.ascontiguousarray(np.asarray(a, np.float32))
    for k_src, k_dst in (("ada_w", "ada_w"), ("w_in", "w_in"), ("s5_glu_w", "glu_w"), ("sc_out_w", "sc_out_w"),
                         ("hy_out_w", "hy_out_w"), ("out_w", "out_w"), ("mlp_w1", "mlp_w1"), ("mlp_w2", "mlp_w2"),
                         ("hy_f_w1", "hyw1"), ("hy_f_w2", "hyw2"), ("hy_f_w3", "hyw3"), ("hy_f_w4", "hyw4")):
        sh[k_dst] = f32(inp[k_src])
    sh["adab2"] = np.ascontiguousarray(np.repeat(f32(inp["ada_b"])[:, None, :], 2, axis=1))
    n1 = _colmajor(inp["norm1_g"], 8); n2 = _colmajor(inp["norm2_g"], 8)
    sh["ncol"] = np.ascontiguousarray(np.stack([n1, n2], axis=2))
    sh["finalg_bc"] = np.ascontiguousarray(np.repeat(f32(inp["final_g"])[None, :], 128, axis=0))
    sh["scw"] = np.ascontiguousarray(_colmajor(inp["sc_conv_w"], 4).transpose(0, 1, 3, 2))
    sh["hcw"] = np.ascontiguousarray(_colmajor(inp["hy_conv_w"], 12).transpose(0, 1, 3, 2))
    sh["hcb"] = _colmajor(inp["hy_conv_b"], 12)
    def qlay(a):
        a = f32(a)
        s = a.shape
        a = a.reshape(s[0], 2, 16, 2, 64, *s[4:])
        a = np.moveaxis(a, (3, 4), (0, 1))
        return np.ascontiguousarray(a.reshape(128, s[0], 32, *s[4:]))
    ldt = np.repeat(f32(inp["s5_log_dt"])[:, :, :, None], 64, axis=3)
    sh["s5a"] = np.ascontiguousarray(np.stack([qlay(inp["s5_a_re"]), qlay(inp["s5_a_im"]), qlay(ldt)], axis=2))
    sh["s5b"] = np.ascontiguousarray(np.stack([qlay(inp["s5_b_re"]), qlay(inp["s5_b_im"])], axis=2))
    cre = np.swapaxes(f32(inp["s5_c_re"]), 3, 4); cim = np.swapaxes(f32(inp["s5_c_im"]), 3, 4)
    sh["s5c"] = np.ascontiguousarray(np.stack([qlay(cre), qlay(cim)], axis=2))
    sh["s5d"] = _colmajor(inp["s5_d"], 4)
    hyfb = np.stack([f32(inp["hy_f_freq"]), f32(inp["hy_f_b1"]), f32(inp["hy_f_b2"]), f32(inp["hy_f_b3"])], axis=-1)
    sh["hyfb"] = np.ascontiguousarray(hyfb.transpose(1, 0, 2))
    sh["hyd_bc"] = np.ascontiguousarray(np.repeat(f32(inp["hy_d"])[None, :, :], 128, axis=0))
    return sh


def prep_core(inp, sh, b):
    m = dict(sh)
    m["x_in"] = np.ascontiguousarray(np.asarray(inp["x"][b], np.float32))
    m["ctx_in"] = np.ascontiguousarray(np.asarray(inp["ctx"][b], np.float32))
    cc = np.stack([_colmajor(np.asarray(inp["c"][b]), 8), _colmajor(np.asarray(inp["c_ctx"]), 8)], axis=-1)
    m["cc"] = np.ascontiguousarray(cc)
    return m


_NC_CACHE = {}


def kernel(**inputs):
    inp = {k: np.asarray(v) for k, v in inputs.items()}
    sh = prep_shared(inp)
    if "nc" not in _NC_CACHE:
        _NC_CACHE["nc"] = build_program()
    nc = _NC_CACHE["nc"]
    real = [prep_core(inp, sh, b) for b in range(4)]
    consts = set(_consts().keys())
    zero = {k: (v if k in consts else np.zeros_like(v)) for k, v in real[0].items()}
    in_maps = [real[c // 2] if c % 2 == 0 else zero for c in range(8)]
    res = run_bass_kernel_spmd(nc, in_maps, core_ids=list(range(8)))
    out = np.stack([np.asarray(res.results[2 * b]["out"], np.float32) for b in range(4)], axis=0)
    return out
```
